# Optimizing a Trainium2 kernel written in Bass

```python
import jax, jax.numpy as jnp
from jax import lax
import numpy as np

D_MODEL = 1024
BATCH = 32
SEQ = 256
DEPTH = 4
DEC_BATCH = 4
DEC_SEQ = 2048
PAST_LEN = 512

GRID_W = 64
N_MIXERS = 2
N_A_LAYERS = (DEPTH + 1) // 2
N_B_LAYERS = DEPTH // 2
A_HEADS = 8
A_DK = D_MODEL // A_HEADS
A_DV = D_MODEL // A_HEADS
A_FDIM = A_HEADS * A_DK
A_VDIM = A_HEADS * A_DV
A_IN = 3 * A_FDIM + 2 * A_VDIM
CHUNK = 32
ATT_HEADS = 8
KV_HEADS = 2
GROUP = ATT_HEADS // KV_HEADS
HEAD_DIM = D_MODEL // ATT_HEADS
QKV_DIM = (ATT_HEADS + 2 * KV_HEADS) * HEAD_DIM
WINDOW = 128
BLOCK = 128
D_FF = ((8 * D_MODEL // 3 + 255) // 256) * 256
ROPE_BASE = 10000.0
EPS = 1e-6
F32 = jnp.float32

kernel_name = 'hybrid_hgrn2_swa_diffusion_step'


def rmsnorm(x, g):
    xf = x.astype(F32)
    y = xf * lax.rsqrt(jnp.mean(xf * xf, axis=-1, keepdims=True) + EPS)
    return (y * g.astype(F32)).astype(x.dtype)


def modulate(h, shift, scale):
    return h * (1 + scale) + shift


def adaln(cond, w, b):
    mod = jax.nn.silu(cond) @ w + b
    return jnp.split(mod[:, None, :], 6, axis=-1)


def swiglu(h, w_gu, w_d):
    g, u = jnp.split(h @ w_gu, 2, axis=-1)
    return (jax.nn.silu(g) * u) @ w_d


def rope_1d(x, pos):
    half = x.shape[-1] // 2
    inv = ROPE_BASE ** (-jnp.arange(half, dtype=F32) / half)
    ang = pos[:, None] * inv[None, :]
    cos = jnp.cos(ang)[None, :, None, :]
    sin = jnp.sin(ang)[None, :, None, :]
    xf = x.astype(F32)
    x1, x2 = xf[..., :half], xf[..., half:]
    return jnp.concatenate([x1 * cos - x2 * sin, x1 * sin + x2 * cos], axis=-1).astype(x.dtype)


def axial_rope(x, row_pos, col_pos):
    h = x.shape[-1] // 2
    return jnp.concatenate([rope_1d(x[..., :h], row_pos), rope_1d(x[..., h:], col_pos)], axis=-1)


def grid_positions(n):
    rows = n // GRID_W
    row_pos = jnp.repeat(jnp.arange(rows, dtype=F32), GRID_W)
    col_pos = jnp.tile(jnp.arange(GRID_W, dtype=F32), rows)
    return row_pos, col_pos


def hgrn_lower_bounds(lb_param):
    p = jax.nn.softmax(lb_param.astype(F32), axis=0)
    cs = jnp.cumsum(p, axis=0)
    return cs - cs[:1]


def gla_scan(q, k, v, logf, s0):
    B, N, H, _ = q.shape
    nc = N // CHUNK

    def to_chunks(a):
        return a.reshape(B, nc, CHUNK, H, a.shape[-1]).transpose(1, 0, 3, 2, 4)

    causal = jnp.tril(jnp.ones((CHUNK, CHUNK), dtype=bool))[:, :, None]

    def step(S, inp):
        qc, kc, vc, gc = inp
        b = jnp.cumsum(gc, axis=2)
        o_inter = jnp.einsum('bhtd,bhde->bhte', qc * jnp.exp(b), S)
        rel = b[:, :, :, None, :] - b[:, :, None, :, :]
        decay = jnp.exp(jnp.where(causal, rel, -jnp.inf))
        scores = jnp.einsum('bhtd,bhsd,bhtsd->bhts', qc, kc, decay)
        o_intra = jnp.einsum('bhts,bhse->bhte', scores, vc)
        b_last = b[:, :, -1]
        k_dec = kc * jnp.exp(b_last[:, :, None, :] - b)
        S_new = jnp.exp(b_last)[..., None] * S + jnp.einsum('bhsd,bhse->bhde', k_dec, vc)
        return S_new, o_inter + o_intra

    s_fin, o = lax.scan(step, s0, (to_chunks(q), to_chunks(k), to_chunks(v), to_chunks(logf)))
    o = o.transpose(1, 0, 3, 2, 4).reshape(B, N, H, v.shape[-1])
    return o, s_fin


def hgrn_mixer(h, w_in, gnorm, w_out, lb, s0):
    B, N, _ = h.shape
    proj = h @ w_in
    q, zf, zb, iv, g = jnp.split(proj, [A_FDIM, 2 * A_FDIM, 3 * A_FDIM, 3 * A_FDIM + A_VDIM], axis=-1)

    def heads(a, d):
        return a.reshape(B, N, A_HEADS, d)

    def gate(z, lbd):
        f = lbd + (1 - lbd) * jax.nn.sigmoid(z.astype(F32))
        return 1 - f, jnp.log(f)

    qf = jax.nn.silu(heads(q, A_DK).astype(F32))
    v = heads(iv, A_DV).astype(F32)
    k_fw, g_fw = gate(heads(zf, A_DK), lb[0].reshape(A_HEADS, A_DK))
    k_bw, g_bw = gate(heads(zb, A_DK), lb[1].reshape(A_HEADS, A_DK))
    s0f = s0.astype(F32)

    def rev(a):
        return jnp.flip(a, axis=1)

    o_fw, s_fw = gla_scan(qf, k_fw, v, g_fw, s0f[:, 0])
    o_bw, s_bw = gla_scan(rev(qf), rev(k_bw), rev(v), rev(g_bw), s0f[:, 1])
    o = o_fw + rev(o_bw)
    o = o * lax.rsqrt(jnp.mean(o * o, axis=-1, keepdims=True) + EPS) * gnorm.reshape(A_HEADS, A_DV).astype(F32)
    o = o.astype(h.dtype) * jax.nn.silu(heads(g, A_DV))
    out = o.reshape(B, N, A_VDIM) @ w_out
    return out, jnp.stack([s_fw, s_bw], axis=1).astype(s0.dtype)


def qkv_heads(h, w_qkv):
    B, N, _ = h.shape
    proj = h @ w_qkv
    q = proj[..., :ATT_HEADS * HEAD_DIM].reshape(B, N, ATT_HEADS, HEAD_DIM)
    k = proj[..., ATT_HEADS * HEAD_DIM:(ATT_HEADS + KV_HEADS) * HEAD_DIM].reshape(B, N, KV_HEADS, HEAD_DIM)
    v = proj[..., (ATT_HEADS + KV_HEADS) * HEAD_DIM:].reshape(B, N, KV_HEADS, HEAD_DIM)
    return q, k, v


def sink_logits(sink, shape):
    s = sink.reshape(KV_HEADS, GROUP).astype(F32)[:, :, None, None]
    return jnp.broadcast_to(s, shape[:-1] + (1,))


def attn_context(h, w_qkv, w_o, sink):
    B, S, _ = h.shape
    q, k, v = qkv_heads(h, w_qkv)
    nq = S // BLOCK
    qb = q.reshape(B, nq, BLOCK, KV_HEADS, GROUP, HEAD_DIM).transpose(1, 0, 2, 3, 4, 5)
    scale = HEAD_DIM ** -0.5

    def one_block(qi):
        lg = jnp.einsum('bqkgd,bskd->bkgqs', qi, k).astype(F32) * scale
        p = jax.nn.softmax(jnp.concatenate([sink_logits(sink, lg.shape), lg], axis=-1), axis=-1)
        return jnp.einsum('bkgqs,bskd->bqkgd', p[..., 1:].astype(h.dtype), v)

    o = lax.map(one_block, qb)
    o = o.transpose(1, 0, 2, 3, 4, 5).reshape(B, S, ATT_HEADS * HEAD_DIM)
    return o @ w_o, k, v


def attn_latent(h, w_qkv, w_o, sink, k_ctx, v_ctx, row_pos, col_pos):
    B, N, _ = h.shape
    q, k, v = qkv_heads(h, w_qkv)
    q = axial_rope(q, row_pos, col_pos)
    k = axial_rope(k, row_pos, col_pos)
    nb = N // BLOCK
    qb = q.reshape(B, nb, BLOCK, KV_HEADS, GROUP, HEAD_DIM)

    def band(a):
        ap = jnp.pad(a, ((0, 0), (BLOCK, BLOCK), (0, 0), (0, 0))).reshape(B, nb + 2, BLOCK, KV_HEADS, HEAD_DIM)
        return jnp.concatenate([ap[:, :-2], ap[:, 1:-1], ap[:, 2:]], axis=2)

    kb, vb = band(k), band(v)
    qi = jnp.arange(BLOCK)[:, None]
    kj = jnp.arange(3 * BLOCK)[None, :]
    rel = kj - BLOCK - qi
    key_pos = (jnp.arange(nb)[:, None, None] - 1) * BLOCK + kj[None]
    mask = (jnp.abs(rel)[None] <= WINDOW) & (key_pos >= 0) & (key_pos < N)
    scale = HEAD_DIM ** -0.5
    lw = jnp.einsum('bnqkgd,bnskd->bnkgqs', qb, kb).astype(F32) * scale
    lw = jnp.where(mask[None, :, None, None], lw, -jnp.inf)
    lc = jnp.einsum('bnqkgd,bpkd->bnkgqp', qb, k_ctx).astype(F32) * scale
    p = jax.nn.softmax(jnp.concatenate([sink_logits(sink, lw.shape), lc, lw], axis=-1), axis=-1)
    P = k_ctx.shape[1]
    pc = p[..., 1:1 + P].astype(h.dtype)
    pw = p[..., 1 + P:].astype(h.dtype)
    o = jnp.einsum('bnkgqp,bpkd->bnqkgd', pc, v_ctx) + jnp.einsum('bnkgqs,bnskd->bnqkgd', pw, vb)
    return o.reshape(B, N, ATT_HEADS * HEAD_DIM) @ w_o


def setup_inputs(seed: int = 0) -> dict:
    key = jax.random.key(seed)
    ks = jax.random.split(key, 21)

    def nrm(k, shape, scale):
        return jax.random.normal(k, shape, F32) * scale

    return {
        'x_prompt': nrm(ks[0], (BATCH, SEQ, D_MODEL), 1.0),
        'x_sample': nrm(ks[1], (DEC_BATCH, DEC_SEQ, D_MODEL), 1.0),
        'cache_k': nrm(ks[2], (DEC_BATCH, N_B_LAYERS, PAST_LEN, KV_HEADS, HEAD_DIM), 1.0),
        'cache_v': nrm(ks[3], (DEC_BATCH, N_B_LAYERS, PAST_LEN, KV_HEADS, HEAD_DIM), 1.0),
        'state_hgrn': nrm(ks[4], (DEC_BATCH, N_A_LAYERS, 2, A_HEADS, A_DK, A_DV), 0.5),
        'c': nrm(ks[5], (DEC_BATCH, D_MODEL), 1.0),
        'c_ctx': nrm(ks[6], (D_MODEL,), 1.0),
        'w_ada': nrm(ks[7], (DEPTH, D_MODEL, 6 * D_MODEL), 0.5 * D_MODEL ** -0.5),
        'b_ada': nrm(ks[8], (DEPTH, 6 * D_MODEL), 0.01),
        'norm1': 1.0 + nrm(ks[9], (DEPTH, D_MODEL), 0.01),
        'norm2': 1.0 + nrm(ks[10], (DEPTH, D_MODEL), 0.01),
        'norm_final': 1.0 + nrm(ks[11], (D_MODEL,), 0.01),
        'w_gate_up': nrm(ks[12], (DEPTH, D_MODEL, 2 * D_FF), D_MODEL ** -0.5),
        'w_down': nrm(ks[13], (DEPTH, D_FF, D_MODEL), D_FF ** -0.5),
        'w_in_a': nrm(ks[14], (N_A_LAYERS, D_MODEL, A_IN), D_MODEL ** -0.5),
        'lower_bounds': nrm(ks[15], (N_A_LAYERS, 2, A_FDIM), 1.0),
        'gnorm_a': 1.0 + nrm(ks[16], (N_A_LAYERS, A_VDIM), 0.01),
        'w_out_a': nrm(ks[17], (N_A_LAYERS, A_VDIM, D_MODEL), A_VDIM ** -0.5),
        'w_qkv_b': nrm(ks[18], (N_B_LAYERS, D_MODEL, QKV_DIM), D_MODEL ** -0.5),
        'w_out_b': nrm(ks[19], (N_B_LAYERS, ATT_HEADS * HEAD_DIM, D_MODEL), (ATT_HEADS * HEAD_DIM) ** -0.5),
        'sink_b': nrm(ks[20], (N_B_LAYERS, ATT_HEADS), 1.0),
    }


def reference(x_prompt, x_sample, cache_k, cache_v, state_hgrn, c, c_ctx, w_ada, b_ada, norm1, norm2, norm_final,
              w_gate_up, w_down, w_in_a, lower_bounds, gnorm_a, w_out_a, w_qkv_b, w_out_b, sink_b):
    lb_all = hgrn_lower_bounds(lower_bounds)

    x = x_prompt
    cond_ctx = c_ctx[None, :]
    new_k, new_v, new_s = [], [], []
    for l in range(DEPTH):
        sh1, sc1, g1, sh2, sc2, g2 = adaln(cond_ctx, w_ada[l], b_ada[l])
        h = modulate(rmsnorm(x, norm1[l]), sh1, sc1)
        j = l // N_MIXERS
        if l % N_MIXERS == 0:
            s0 = jnp.zeros((x.shape[0], 2, A_HEADS, A_DK, A_DV), x.dtype)
            out, s_fin = hgrn_mixer(h, w_in_a[j], gnorm_a[j], w_out_a[j], lb_all[j], s0)
            new_s.append(s_fin)
        else:
            out, k_c, v_c = attn_context(h, w_qkv_b[j], w_out_b[j], sink_b[j])
            new_k.append(k_c)
            new_v.append(v_c)
        x = x + g1 * out
        x = x + g2 * swiglu(modulate(rmsnorm(x, norm2[l]), sh2, sc2), w_gate_up[l], w_down[l])
    y_prompt = rmsnorm(x, norm_final)
    new_cache_k = jnp.stack(new_k, axis=1)
    new_cache_v = jnp.stack(new_v, axis=1)
    new_state_hgrn = jnp.stack(new_s, axis=1)

    x = x_sample
    row_pos, col_pos = grid_positions(x_sample.shape[1])
    for l in range(DEPTH):
        sh1, sc1, g1, sh2, sc2, g2 = adaln(c, w_ada[l], b_ada[l])
        h = modulate(rmsnorm(x, norm1[l]), sh1, sc1)
        j = l // N_MIXERS
        if l % N_MIXERS == 0:
            out, _ = hgrn_mixer(h, w_in_a[j], gnorm_a[j], w_out_a[j], lb_all[j], state_hgrn[:, j])
        else:
            out = attn_latent(h, w_qkv_b[j], w_out_b[j], sink_b[j], cache_k[:, j], cache_v[:, j], row_pos, col_pos)
        x = x + g1 * out
        x = x + g2 * swiglu(modulate(rmsnorm(x, norm2[l]), sh2, sc2), w_gate_up[l], w_down[l])
    y_sample = rmsnorm(x, norm_final)

    return (y_prompt, y_sample, new_cache_k, new_cache_v, new_state_hgrn)
```

```python
import numpy as np
import contextlib
import concourse.bass as bass
import concourse.mybir as mybir
from concourse.bass_utils import run_bass_kernel_spmd
F32 = mybir.dt.float32
BF16 = mybir.dt.bfloat16
AF = mybir.ActivationFunctionType
ALU = mybir.AluOpType
AX = mybir.AxisListType


COMPUTE = ("pe", "act", "dve", "pool")


class Op:
    __slots__ = ("eng", "fn", "reads", "writes", "deps", "is_dma", "marked",
                 "token", "idx", "eidx", "pre_waits")

    def __init__(self, eng, fn, reads, writes, is_dma):
        self.eng = eng
        self.fn = fn
        self.reads = reads
        self.writes = writes
        self.is_dma = is_dma
        self.deps = []
        self.marked = False
        self.token = None
        self.pre_waits = []


class Sched:
    def __init__(self, nc, es, n_dma_sems=8):
        self.nc = nc
        self.engs = {"pe": nc.tensor, "act": nc.scalar, "dve": nc.vector,
                     "pool": nc.gpsimd, "sp": nc.sync}
        self.sems = {e: es.enter_context(nc.semaphore("s_" + e)) for e in COMPUTE}
        self.dma_sems = {}
        for q in ("sp", "pool", "act"):
            self.dma_sems[q] = [es.enter_context(nc.semaphore("d_%s%d" % (q, i)))
                                for i in range(n_dma_sems)]
        self.ops = []
        self.last_w = {}
        self.readers = {}
        self.out_dma_ops = []
        self.last_on_eng = {}
        self.dma_since = []
        self.bar = []

    def barrier(self):
        self.bar = sorted(set(list(self.last_on_eng.values()) + self.dma_since))
        self.dma_since = []

    def _keys(self, ks):
        out = []
        for k in ks:
            if isinstance(k, list):
                out.extend(k)
            else:
                out.append(k)
        return out

    def op(self, eng, fn, reads=(), writes=()):
        rk = self._keys(reads)
        wk = self._keys(writes)
        wk = wk + [k for k in rk if isinstance(k, tuple) and k[0] == "ps"]
        rk = [k for k in rk if not (isinstance(k, tuple) and k[0] == "ps")]
        o = Op(eng, fn, rk, wk, False)
        self._add(o, True)
        self.last_on_eng[eng] = o.idx
        return o

    def dma(self, queue, fn, reads=(), writes=(), is_output=False, scoped=True):
        o = Op(queue, fn, self._keys(reads), self._keys(writes), True)
        self._add(o, scoped)
        if scoped:
            self.dma_since.append(o.idx)
        if is_output:
            self.out_dma_ops.append(o)
        return o

    def _add(self, o, scoped=True):
        o.idx = len(self.ops)
        deps = set(self.bar) if scoped else set()
        for k in o.reads:
            w = self.last_w.get(k)
            if w is not None:
                deps.add(w)
        for k in o.writes:
            w = self.last_w.get(k)
            if w is not None:
                deps.add(w)
            for r in self.readers.get(k, ()):
                deps.add(r)
        deps.discard(o.idx)
        o.deps = sorted(deps)
        for k in o.reads:
            self.readers.setdefault(k, []).append(o.idx)
        for k in o.writes:
            self.last_w[k] = o.idx
            self.readers[k] = []
        self.ops.append(o)

    def emit(self):
        ops = self.ops
        eng_pos = {}
        per_eng_count = {}
        for o in ops:
            o.eidx = per_eng_count.get(o.eng, 0)
            per_eng_count[o.eng] = o.eidx + 1
        for o in ops:
            for d in o.deps:
                p = ops[d]
                if p.is_dma:
                    continue
                if p.eng == o.eng and not o.is_dma:
                    if p.eng == "pe":
                        continue
                p.marked = True
        cnt = {e: 0 for e in COMPUTE}
        dma_rr = {q: 0 for q in self.dma_sems}
        dma_val = {}
        for q, lst in self.dma_sems.items():
            for i in range(len(lst)):
                dma_val[(q, i)] = 0
        for o in ops:
            if o.is_dma:
                q = o.eng
                i = dma_rr[q]
                dma_rr[q] = (i + 1) % len(self.dma_sems[q])
                prev = dma_val[(q, i)]
                if prev > 0:
                    o.pre_waits.append((("dma", q, i), prev))
                dma_val[(q, i)] = prev + 16
                o.token = (("dma", q, i), prev + 16)
            elif o.marked:
                cnt[o.eng] += 1
                o.token = (("eng", o.eng), cnt[o.eng])
        clock = {e: {} for e in self.engs}
        vc = [None] * len(ops)

        def semh(s):
            if s[0] == "eng":
                return self.sems[s[1]]
            return self.dma_sems[s[1]][s[2]]

        def merge(a, b):
            for k, v in b.items():
                if a.get(k, 0) < v:
                    a[k] = v

        nwaits = 0
        for o in ops:
            e = o.eng
            ck = clock[e]
            need = {}
            for s, v in o.pre_waits:
                if ck.get(s, 0) < v:
                    need[s] = max(need.get(s, 0), v)
            for d in o.deps:
                p = ops[d]
                if p.token is None:
                    continue
                if (not p.is_dma) and p.eng == e and not o.is_dma:
                    if e == "pe":
                        continue
                s, v = p.token
                if ck.get(s, 0) >= v:
                    continue
                need[s] = max(need.get(s, 0), v)
                merge(ck, vc[d])
            engh = self.engs[e]
            for s, v in need.items():
                engh.wait_ge(semh(s), v)
                nwaits += 1
                if ck.get(s, 0) < v:
                    ck[s] = v
            inst = o.fn()
            if o.token is not None:
                s, v = o.token
                if o.is_dma:
                    inst.then_inc(semh(s), 16)
                else:
                    inst.then_inc(semh(s), 1)
                snap = dict(ck)
                snap[s] = v
                vc[o.idx] = snap
        sp = self.engs["sp"]
        ck = clock["sp"]
        need = {}
        for o in self.out_dma_ops:
            s, v = o.token
            need[s] = max(need.get(s, 0), v)
        for (q, i), v in dma_val.items():
            if v > 0:
                s = ("dma", q, i)
                need[s] = max(need.get(s, 0), v)
        for s, v in need.items():
            sp.wait_ge(semh(s), v)
        return dict(n_ops=len(ops), n_waits=nwaits, counts=per_eng_count, sem_counts=cnt)

T = 2048
D = 1024
KC = 8
NT = 16
NB = 4
TB = 512
DFF = 2816
EPS = 1e-6
CFG = dict(depth=4, mixers=True, ffn=True, dbg=False, l0=0)

R_BADA = 0
R_N1 = 192
R_N2 = 224
R_NF = 256
R_GN = 264
R_LB = 280
N_VROWS = 384
PC_CARRY = 0
PC_CTXB = 1
PC_MASK = 2
N_PCS = 2 + 64


def build_nc(cfg):
    nc = bass.Bass("TRN2", target_bir_lowering=False)
    depth = cfg["depth"]

    def din(name, shape, dt=F32):
        return nc.dram_tensor(name, list(shape), dt, kind="ExternalInput").ap()

    def dout(name, shape, dt=F32):
        return nc.dram_tensor(name, list(shape), dt, kind="ExternalOutput").ap()

    x_in = din("x", [T, D])
    cond_in = din("cond", [128, KC])
    vecs_in = din("vecs", [N_VROWS, 128])
    cmat_in = din("cmat", [5, 128, 128])
    pcs_in = din("pcs", [128, N_PCS])
    rope_in = din("rope", [NT, 128, 256])
    ck_in = din("ck", [2, 512, 256])
    cv_in = din("cv", [2, 512, 256])
    st0_in = din("st0", [2, 2, 8, 128, 128])
    sink_in = din("sinkb", [128, 16])
    bm_in = din("bm", [128, 4])
    w_ada = din("w_ada", [4, D, 6 * D])
    w_gu = din("w_gate_up", [4, D, 2 * DFF])
    w_dn = din("w_down", [4, DFF, D])
    w_ina = din("w_in_a", [2, D, 5 * D])
    w_outa = din("w_out_a", [2, D, D])
    w_qkv = din("w_qkv_b", [2, D, 1536])
    w_outb = din("w_out_b", [2, D, D])
    y_out = dout("y", [T, D])
    nk_out = dout("nk", [2, T, 256])
    nv_out = dout("nv", [2, T, 256])
    ns_out = dout("ns", [2, 2, 8, 8, 128, 128])

    with contextlib.ExitStack() as es:
        S = Sched(nc, es)

        def sb(name, shape, dt):
            return es.enter_context(nc.sbuf_tensor("sb_" + name, list(shape), dt))

        uniq = [0]

        def pl(ph, name, shape, dt):
            uniq[0] += 1
            return ph.enter_context(nc.sbuf_tensor("pl%d_%s" % (uniq[0], name), list(shape), dt))

        xT = sb("xT", [128, KC, T], F32)
        hT = sb("hT", [128, KC, T], BF16)
        ring = [sb("ring%d" % i, [128, 4096], BF16) for i in range(4)]
        vecT = sb("vecT", [128, N_VROWS], F32)
        idf = sb("idf", [128, 128], F32)
        idb = sb("idb", [128, 128], BF16)
        tri = sb("tri", [128, 2, 128], BF16)
        hmk = sb("hmk", [128, 2, 128], BF16)
        ones1 = sb("ones1", [128, 128], BF16)
        onesd = sb("onesd", [128, 128], BF16)
        onese = sb("onese", [128, 128], BF16)
        pcs = sb("pcs", [128, N_PCS], F32)
        bmb = sb("bmb", [128, 4], BF16)
        esink = sb("esink", [128, 16], F32)
        scb = sb("scb", [128, KC], BF16)
        condf = sb("condf", [128, KC], F32)
        modall = sb("modall", [128, 4, 64], F32)
        lbt = sb("lbt", [128, 2, 16], F32)
        omlt = sb("omlt", [128, 2, 16], F32)
        nomlt = sb("nomlt", [128, 2, 16], F32)
        ps = [es.enter_context(nc.psum_tensor("ps%d" % i, [128, 512], F32)) for i in range(8)]
        psb = [p[:].bitcast(BF16) for p in ps]

        ring_i = [0]

        def ring_next():
            i = ring_i[0]
            ring_i[0] = (i + 1) % 4
            return ring[i], ("ring", i)

        def MM(out, lhsT, rhs, start, stop, r, w, **kw):
            S.op("pe", lambda: nc.tensor.matmul(out, lhsT, rhs, start=start, stop=stop, **kw), r, w)

        def TR(out, in_, ident, r, w):
            S.op("pe", lambda: nc.tensor.transpose(out, in_, ident), r, w)

        def ACT(out, in_, func, r, w, bias=None, scale=None):
            kw = {}
            if bias is not None:
                kw["bias"] = bias
            if scale is not None:
                kw["scale"] = scale
            S.op("act", lambda: nc.scalar.activation(out=out, in_=in_, func=func, **kw), r, w)

        def ENG(e):
            return {"dve": nc.vector, "pool": nc.gpsimd}[e]

        def TT(e, out, in0, in1, op, r, w):
            S.op(e, lambda: ENG(e).tensor_tensor(out=out, in0=in0, in1=in1, op=op), r, w)

        def TS(e, out, in0, s1, s2, op0, op1, r, w):
            if s2 is None:
                S.op(e, lambda: ENG(e).tensor_scalar(out=out, in0=in0, scalar1=s1, scalar2=None, op0=op0), r, w)
            else:
                S.op(e, lambda: ENG(e).tensor_scalar(out=out, in0=in0, scalar1=s1, scalar2=s2, op0=op0, op1=op1), r, w)

        def STT(e, out, in0, scalar, in1, op0, op1, r, w):
            S.op(e, lambda: ENG(e).scalar_tensor_tensor(out=out, in0=in0, scalar=scalar, in1=in1, op0=op0, op1=op1), r, w)

        def CP(e, out, in_, r, w):
            if e == "act":
                S.op("act", lambda: nc.scalar.copy(out=out, in_=in_), r, w)
            else:
                S.op(e, lambda: ENG(e).tensor_copy(out=out, in_=in_), r, w)

        def MEMSET(e, out, val, w):
            S.op(e, lambda: ENG(e).memset(out, val), (), w)

        def DMA(q, out, in_, r, w, is_output=False, scoped=True):
            eng = {"sp": nc.sync, "pool": nc.gpsimd, "act": nc.scalar}[q]
            S.dma(q, lambda: eng.dma_start(out=out, in_=in_), r, w, is_output=is_output, scoped=scoped)

        def tbs(tb):
            return slice(tb * TB, (tb + 1) * TB)

        with contextlib.ExitStack() as ph:
            vtmp = pl(ph, "vtmp", [128, 3, 128], F32)
            DMA("sp", idf[:], cmat_in[0], (), ["idf"])
            DMA("pool", idb[:], cmat_in[0], (), ["idb"])
            DMA("pool", tri[:, 0, :], cmat_in[1], (), ["tri"])
            DMA("pool", tri[:, 1, :], cmat_in[2], (), ["tri"])
            DMA("pool", hmk[:, 0, :], cmat_in[3], (), ["hmk"])
            DMA("pool", hmk[:, 1, :], cmat_in[4], (), ["hmk"])
            DMA("sp", pcs[:], pcs_in, (), ["pcs"])
            DMA("pool", bmb[:], bm_in, (), ["bmb"])
            DMA("sp", esink[:], sink_in, (), ["esink"])
            DMA("sp", condf[:], cond_in, (), ["condf"])
            for i in range(3):
                DMA("sp", vtmp[:, i, :], vecs_in[i * 128:(i + 1) * 128, :], (), [("vtmp", i)])
            MEMSET("dve", ones1[:], 1.0, ["ones1"])
            MEMSET("dve", onesd[:], 1.0 / D, ["onesd"])
            MEMSET("dve", onese[:], 1.0 / 128, ["onese"])
            for i in range(3):
                TR(ps[0][:, i * 128:(i + 1) * 128], vtmp[:, i, :], idf[:], [("vtmp", i), "idf"], [("ps", 0)])
            CP("dve", vecT[:], ps[0][:, 0:N_VROWS], [("ps", 0)], ["vecT"])
            ACT(esink[:], esink[:], AF.Exp, ["esink"], ["esink"])
            ACT(scb[:], condf[:], AF.Silu, ["condf"], ["scb"])
            MEMSET("dve", lbt[:, 0, :], 0.0, ["lbt"])
            TT("dve", lbt[:, 1, :], vecT[:, R_LB + 16:R_LB + 32], vecT[:, R_LB:R_LB + 16], ALU.subtract, ["vecT", "lbt"], ["lbt"])
            ACT(lbt[:, 1, :], lbt[:, 1, :], AF.Sigmoid, ["lbt"], ["lbt"])
            TS("dve", omlt[:], lbt[:], -1.0, 1.0, ALU.mult, ALU.add, ["lbt"], ["omlt"])
            TS("dve", nomlt[:], omlt[:], -1.0, None, ALU.mult, None, ["omlt"], ["nomlt"])

            xtok = [pl(ph, "xtok%d" % i, [128, D], F32) for i in range(2)]
            for t in range(NT):
                xt = xtok[t % 2]
                DMA("sp", xt[:], x_in[t * 128:(t + 1) * 128, :], (), [("xtok", t % 2)])
                for hf in range(2):
                    pb = (2 * t + hf) % 4
                    for c4 in range(4):
                        c = hf * 4 + c4
                        TR(ps[pb][:, c4 * 128:(c4 + 1) * 128], xt[:, c * 128:(c + 1) * 128], idf[:],
                           [("xtok", t % 2), "idf"], [("ps", pb)])
                    dst = xT[:, hf * 4:(hf + 1) * 4, t * 128:(t + 1) * 128]
                    src = ps[pb][:].rearrange("p (c t) -> p c t", c=4)
                    wk = [("x", hf * 4 + c4, t // 4) for c4 in range(4)]
                    if hf == 0:
                        CP("dve", dst, src, [("ps", pb)], wk)
                    else:
                        CP("act", dst, src, [("ps", pb)], wk)
        S.barrier()

        def adaln_gen(l):
            for s in range(12):
                slot, sk = ring_next()
                sv = slot[:, 0:4096].rearrange("p (k f) -> p k f", k=8)
                DMA("pool", sv, w_ada[l].rearrange("(k p) f -> p k f", p=128)[:, :, s * 512:(s + 1) * 512],
                    (), [sk], scoped=False)
                for f in range(4):
                    col = s * 4 + f
                    for kc in range(KC):
                        MM(ps[7][:, col:col + 1], sv[:, kc, f * 128:(f + 1) * 128], scb[:, kc:kc + 1],
                           kc == 0, kc == KC - 1, [sk, "scb"], [("ps", 7)])
                yield s
            mk = ("mod", l)
            TT("dve", modall[:, l, 0:48], ps[7][:, 0:48], vecT[:, R_BADA + l * 48:R_BADA + (l + 1) * 48], ALU.add,
               [("ps", 7), "vecT"], [mk])
            STT("dve", modall[:, l, 48:56], modall[:, l, 8:16], 1.0, vecT[:, R_N1 + l * 8:R_N1 + (l + 1) * 8],
                ALU.add, ALU.mult, [mk, "vecT"], [mk])
            STT("dve", modall[:, l, 56:64], modall[:, l, 32:40], 1.0, vecT[:, R_N2 + l * 8:R_N2 + (l + 1) * 8],
                ALU.add, ALU.mult, [mk, "vecT"], [mk])
            yield 12

        def adaln(l):
            for _ in adaln_gen(l):
                pass

        ada_next = [None]

        def ada_step():
            g = ada_next[0]
            if g is not None:
                try:
                    next(g)
                except StopIteration:
                    ada_next[0] = None

        def norm_mod(l, which):
            mk = ("mod", l)
            Acol = 48 if which == 0 else 56
            Bcol = 0 if which == 0 else 24
            with contextlib.ExitStack() as ph:
                sq = [pl(ph, "sq%d" % i, [128, KC, TB], BF16) for i in range(2)]
                rstd = [pl(ph, "rstd%d" % i, [128, TB], F32) for i in range(2)]
                tmpn = [pl(ph, "tmpn%d" % i, [128, TB], F32) for i in range(2)]
                for tb in range(NB):
                    q = sq[tb % 2]
                    rs = rstd[tb % 2]
                    for c in range(KC):
                        ACT(q[:, c, :], xT[:, c, tbs(tb)], AF.Square, [("x", c, tb)], [("sq", tb % 2, c)])
                    for c in range(KC):
                        MM(ps[6][:], onesd[:], q[:, c, :], c == 0, c == KC - 1, [("sq", tb % 2, c), "onesd"], [("ps", 6)])
                    ACT(rs[:], ps[6][:], AF.Ln, [("ps", 6)], [("rstd", tb % 2)], bias=EPS)
                    ACT(rs[:], rs[:], AF.Exp, [("rstd", tb % 2)], [("rstd", tb % 2)], scale=-0.5)
                    for c in range(KC):
                        tm = tmpn[c % 2]
                        STT("dve", tm[:], xT[:, c, tbs(tb)], modall[:, l, Acol + c:Acol + c + 1], rs[:], ALU.mult, ALU.mult,
                            [("x", c, tb), mk, ("rstd", tb % 2)], [("tmpn", c % 2)])
                        ACT(hT[:, c, tbs(tb)], tm[:], AF.Identity, [("tmpn", c % 2), mk], [("h", c, tb)],
                            bias=modall[:, l, Bcol + c:Bcol + c + 1])
            S.barrier()

        def ffn(l):
            mk = ("mod", l)
            with contextlib.ExitStack() as ph:
                actT = pl(ph, "actT", [128, 11, T], BF16)
                sg = [pl(ph, "sg%d" % i, [128, TB], F32) for i in range(2)]
                k = 0
                kd = 0
                for half in range(2):
                    for (g0, gn) in ((0, 4), (4, 4), (8, 3)):
                        ada_step()
                        fc0 = half * 11 + g0
                        sl_g, kg = ring_next()
                        vg = sl_g[:, 0:4096].rearrange("p (k f) -> p k f", k=8)
                        src = w_gu[l].rearrange("(k p) f -> p k f", p=128)
                        DMA("pool", vg[:, :, 0:gn * 128], src[:, :, fc0 * 128:(fc0 + gn) * 128], (), [kg], scoped=False)
                        sl_u, ku = ring_next()
                        vu = sl_u[:, 0:4096].rearrange("p (k f) -> p k f", k=8)
                        DMA("pool", vu[:, :, 0:gn * 128], src[:, :, DFF + fc0 * 128:DFF + (fc0 + gn) * 128], (), [ku], scoped=False)
                        for tb in range(NB):
                            for j in range(gn):
                                pg = k % 2
                                pu = 2 + k % 2
                                k += 1
                                for kc in range(KC):
                                    MM(ps[pg][:], vg[:, kc, j * 128:(j + 1) * 128], hT[:, kc, tbs(tb)], kc == 0, kc == KC - 1,
                                       [kg, ("h", kc, tb)], [("ps", pg)])
                                for kc in range(KC):
                                    MM(ps[pu][:], vu[:, kc, j * 128:(j + 1) * 128], hT[:, kc, tbs(tb)], kc == 0, kc == KC - 1,
                                       [ku, ("h", kc, tb)], [("ps", pu)])
                                s_ = sg[k % 2]
                                ACT(s_[:], ps[pg][:], AF.Silu, [("ps", pg)], [("sg", k % 2)])
                                TT("dve", actT[:, g0 + j, tbs(tb)], s_[:], ps[pu][:], ALU.mult,
                                   [("sg", k % 2), ("ps", pu)], [("act", g0 + j, tb)])
                    for s in range(4):
                        ada_step()
                        sl_d, kdk = ring_next()
                        vd = sl_d[:, 0:11 * 256].rearrange("p (k f) -> p k f", k=11)
                        srcd = w_dn[l][half * 11 * 128:(half + 1) * 11 * 128, :].rearrange("(k p) f -> p k f", p=128)
                        DMA("pool", vd, srcd[:, :, s * 256:(s + 1) * 256], (), [kdk], scoped=False)
                        for tb in range(NB):
                            for dd in range(2):
                                dm = 2 * s + dd
                                pd = 4 + kd % 2
                                kd += 1
                                for fl in range(11):
                                    MM(ps[pd][:], vd[:, fl, dd * 128:(dd + 1) * 128], actT[:, fl, tbs(tb)], fl == 0, fl == 10,
                                       [kdk, ("act", fl, tb)], [("ps", pd)])
                                STT("dve", xT[:, dm, tbs(tb)], ps[pd][:], modall[:, l, 40 + dm:41 + dm], xT[:, dm, tbs(tb)],
                                    ALU.mult, ALU.add, [("ps", pd), mk, ("x", dm, tb)], [("x", dm, tb)])
            S.barrier()

        def final_out():
            with contextlib.ExitStack() as ph:
                sq = [pl(ph, "fsq%d" % i, [128, KC, TB], BF16) for i in range(2)]
                rstd = [pl(ph, "frstd%d" % i, [128, TB], F32) for i in range(2)]
                ynT = [pl(ph, "ynT%d" % i, [128, KC, TB], F32) for i in range(2)]
                ytok = [pl(ph, "ytok%d" % i, [128, D], F32) for i in range(2)]
                kk = 0
                for tb in range(NB):
                    q = sq[tb % 2]
                    rs = rstd[tb % 2]
                    yn = ynT[tb % 2]
                    for c in range(KC):
                        ACT(q[:, c, :], xT[:, c, tbs(tb)], AF.Square, [("x", c, tb)], [("sq", tb % 2, c)])
                    for c in range(KC):
                        MM(ps[6][:], onesd[:], q[:, c, :], c == 0, c == KC - 1, [("sq", tb % 2, c), "onesd"], [("ps", 6)])
                    ACT(rs[:], ps[6][:], AF.Ln, [("ps", 6)], [("rstd", tb % 2)], bias=EPS)
                    ACT(rs[:], rs[:], AF.Exp, [("rstd", tb % 2)], [("rstd", tb % 2)], scale=-0.5)
                    for c in range(KC):
                        STT("dve", yn[:, c, :], xT[:, c, tbs(tb)], vecT[:, R_NF + c:R_NF + c + 1], rs[:], ALU.mult, ALU.mult,
                            [("x", c, tb), "vecT", ("rstd", tb % 2)], [("yn", tb % 2, c)])
                    for t4 in range(4):
                        t = tb * 4 + t4
                        yt = ytok[kk % 2]
                        for hf in range(2):
                            pb = (2 * kk + hf) % 4
                            for c4 in range(4):
                                c = hf * 4 + c4
                                TR(ps[pb][:, c4 * 128:(c4 + 1) * 128], yn[:, c, t4 * 128:(t4 + 1) * 128], idf[:],
                                   [("yn", tb % 2, c), "idf"], [("ps", pb)])
                            if hf == 0:
                                CP("dve", yt[:, 0:512], ps[pb][:], [("ps", pb)], [("ytok", kk % 2, 0)])
                            else:
                                CP("act", yt[:, 512:1024], ps[pb][:], [("ps", pb)], [("ytok", kk % 2, 1)])
                        DMA("sp", y_out[t * 128:(t + 1) * 128, :], yt[:], [("ytok", kk % 2, 0), ("ytok", kk % 2, 1)], [],
                            is_output=True)
                        kk += 1
            S.barrier()


        def attn(l):
            j = l // 2
            mk = ("mod", l)
            SC = 1.0 / (128.0 ** 0.5)
            with contextlib.ExitStack() as ph:
                kctxT = pl(ph, "kctxT", [128, 2, 512], BF16)
                vctx = pl(ph, "vctx", [128, 4, 256], BF16)
                kctok = pl(ph, "kctok", [128, 4, 256], BF16)
                qoT = pl(ph, "qoT", [128, 4, T], BF16)
                kT = pl(ph, "kT", [128, T], BF16)
                vtok = pl(ph, "vtok", [128, NT, 128], BF16)
                ropet = [pl(ph, "ropet%d" % i, [128, 256], F32) for i in range(2)]
                qkf = [pl(ph, "qkf%d" % i, [128, 768], F32) for i in range(2)]
                r1 = [pl(ph, "r1%d" % i, [128, 640], F32) for i in range(2)]
                r2 = [pl(ph, "r2%d" % i, [128, 640], F32) for i in range(2)]
                rb = [pl(ph, "rb%d" % i, [128, 640], BF16) for i in range(2)]
                pT = [pl(ph, "pT%d" % i, [128, 7, 512], BF16) for i in range(2)]
                mskt = [pl(ph, "mskt%d" % i, [128, 128], BF16) for i in range(2)]
                den2 = [pl(ph, "den%d" % i, [128, 512], F32) for i in range(2)]

                DMA("pool", kctok[:], ck_in[j].rearrange("(t p) f -> p t f", p=128), (), ["kctok"])
                DMA("pool", vctx[:], cv_in[j].rearrange("(t p) f -> p t f", p=128), (), ["vctx"])
                for kvh in range(2):
                    for t in range(4):
                        TR(psb[kvh][:, t * 128:(t + 1) * 128], kctok[:, t, kvh * 128:(kvh + 1) * 128], idb[:],
                           ["kctok", "idb"], [("ps", kvh)])
                    CP("dve", kctxT[:, kvh, :], psb[kvh][:, 0:512], [("ps", kvh)], [("kctx", kvh)])

                kk = 0
                mi = 0
                for g in range(2 if cfg.get("att_stop", 9) > 0 else 0):
                    slq, kq = ring_next()
                    vq = slq[:, 0:4096].rearrange("p (k f) -> p k f", k=8)
                    wsrc = w_qkv[j].rearrange("(k p) f -> p k f", p=128)
                    DMA("pool", vq, wsrc[:, :, g * 512:(g + 1) * 512], (), [kq], scoped=False)
                    slkv, kkv = ring_next()
                    vkv = slkv[:, 0:4096].rearrange("p (k f) -> p k f", k=8)
                    DMA("pool", vkv[:, :, 0:128], wsrc[:, :, 1024 + g * 128:1024 + (g + 1) * 128], (), [kkv], scoped=False)
                    DMA("pool", vkv[:, :, 128:256], wsrc[:, :, 1280 + g * 128:1280 + (g + 1) * 128], (), [kkv], scoped=False)
                    def stage1_b(t):
                        tsl = slice(t * 128, (t + 1) * 128)
                        b2 = t % 2
                        pb = 4 + b2
                        for hh in range(5):
                            TR(psb[pb][:, hh * 128:(hh + 1) * 128], rb[b2][:, hh * 128:(hh + 1) * 128], idb[:],
                               [("rb", b2), "idb"], [("ps", pb)])
                        CP("act", qoT[:, :, tsl], psb[pb][:, 0:512].rearrange("p (h t) -> p h t", h=4), [("ps", pb)], [("qo", t)])
                        CP("act", kT[:, tsl], psb[pb][:, 512:640], [("ps", pb)], [("kT", t)])

                    for t in range(NT):
                        tsl = slice(t * 128, (t + 1) * 128)
                        b2 = t % 2
                        pq = ps[b2]
                        pkv = ps[2 + b2]
                        for kc in range(KC):
                            MM(pq[:], hT[:, kc, tsl], vq[:, kc, :], kc == 0, kc == KC - 1, [("h", kc, t // 4), kq], [("ps", b2)])
                        for kc in range(KC):
                            MM(pkv[:, 0:256], hT[:, kc, tsl], vkv[:, kc, 0:256], kc == 0, kc == KC - 1,
                               [("h", kc, t // 4), kkv], [("ps", 2 + b2)])
                        DMA("sp", ropet[b2][:], rope_in[t], (), [("ropet", b2)])
                        CP("act", qkf[b2][:, 0:512], pq[:], [("ps", b2)], [("qkf", b2, 0)])
                        CP("act", qkf[b2][:, 512:768], pkv[:, 0:256], [("ps", 2 + b2)], [("qkf", b2, 1)])
                        CP("dve", vtok[:, t, :], qkf[b2][:, 640:768], [("qkf", b2, 1)], [("vt", t)])
                        DMA("sp", nk_out[j, tsl, g * 128:(g + 1) * 128], qkf[b2][:, 512:640], [("qkf", b2, 1)], [], is_output=True)
                        DMA("sp", nv_out[j, tsl, g * 128:(g + 1) * 128], qkf[b2][:, 640:768], [("qkf", b2, 1)], [], is_output=True)
                        x3 = qkf[b2][:, 0:640].rearrange("p (h d) -> p h d", h=5)
                        cosb = ropet[b2][:, 0:128].unsqueeze(1).to_broadcast([128, 5, 128])
                        TT("dve", r1[b2][:].rearrange("p (h d) -> p h d", h=5), x3, cosb, ALU.mult,
                           [("qkf", b2, 0), ("qkf", b2, 1), ("ropet", b2)], [("r1", b2)])
                        x5 = qkf[b2][:, 0:640].rearrange("p (h a s i) -> p h a s i", h=5, a=2, s=2)
                        o5 = r2[b2][:].rearrange("p (h a s i) -> p h a s i", h=5, a=2, s=2)
                        s4 = ropet[b2][:, 128:256].rearrange("p (a s i) -> p a s i", a=2, s=2)
                        for sidx in range(2):
                            sinb = s4[:, :, sidx, :].unsqueeze(1).to_broadcast([128, 5, 2, 32])
                            TT("dve", o5[:, :, :, sidx, :], x5[:, :, :, 1 - sidx, :], sinb, ALU.mult,
                               [("qkf", b2, 0), ("qkf", b2, 1), ("ropet", b2)], [("r2", b2, sidx)])
                        TT("dve", rb[b2][:], r1[b2][:], r2[b2][:], ALU.add, [("r1", b2), ("r2", b2, 0), ("r2", b2, 1)], [("rb", b2)])
                        if t > 0:
                            stage1_b(t - 1)
                    stage1_b(NT - 1)
                    def scores(n):
                        nonlocal kk, mi
                        nsl = slice(n * 128, (n + 1) * 128)
                        chunks = [("ctx", t) for t in range(4)]
                        if n > 0:
                            chunks.append(("prev", n - 1))
                        chunks.append(("cen", n))
                        if n < NT - 1:
                            chunks.append(("next", n + 1))
                        pt = pT[n % 2]
                        for ci, (kind, idx) in enumerate(chunks):
                            pS = 4 + kk % 4
                            kk += 1
                            if kind == "ctx":
                                lhs = kctxT[:, g, idx * 128:(idx + 1) * 128]
                                rk = [("kctx", g)]
                            else:
                                lhs = kT[:, idx * 128:(idx + 1) * 128]
                                rk = [("kT", idx)]
                            MM(ps[pS][:], lhs, qoT[:, :, nsl], True, True, rk + [("qo", n)], [("ps", pS)])
                            pk = ("pT", n % 2, ci)
                            if kind == "ctx":
                                ACT(pt[:, ci, :], ps[pS][:], AF.Exp, [("ps", pS), "pcs"], [pk], scale=SC,
                                    bias=pcs[:, PC_CTXB:PC_CTXB + 1])
                            else:
                                ACT(pt[:, ci, :], ps[pS][:], AF.Exp, [("ps", pS)], [pk], scale=SC)
                            if kind in ("prev", "next"):
                                side = 0 if kind == "prev" else 1
                                base = PC_MASK + (n * 2 + side) * 2
                                mt = mskt[mi % 2]
                                mkk = ("mskt", mi % 2)
                                mi += 1
                                TS("dve", mt[:], tri[:, side, :], pcs[:, base:base + 1], pcs[:, base + 1:base + 2],
                                   ALU.mult, ALU.add, ["tri", "pcs"], [mkk])
                                pv = pt[:, ci, :].rearrange("p (h t) -> p h t", h=4)
                                TT("dve", pv, pv, mt[:].unsqueeze(1).to_broadcast([128, 4, 128]), ALU.mult, [pk, mkk], [pk])
                            yield chunks

                    def pv_norm(n, chunks):
                        nsl = slice(n * 128, (n + 1) * 128)
                        pt = pT[n % 2]
                        nch = len(chunks)
                        bO = n % 2
                        bD = 2 + n % 2
                        dn = den2[n % 2]
                        dk = ("den", n % 2)
                        for ci, (kind, idx) in enumerate(chunks):
                            pk = ("pT", n % 2, ci)
                            if kind == "ctx":
                                vl = vctx[:, idx, g * 128:(g + 1) * 128]
                                rv = ["vctx"]
                            else:
                                vl = vtok[:, idx, :]
                                rv = [("vt", idx)]
                            MM(ps[bO][:], vl, pt[:, ci, :], ci == 0, ci == nch - 1, rv + [pk], [("ps", bO)])
                            MM(ps[bD][:], ones1[:], pt[:, ci, :], ci == 0, ci == nch - 1, ["ones1", pk], [("ps", bD)])
                            yield ci
                        d3 = dn[:].rearrange("p (h t) -> p h t", h=4)
                        es4 = esink[:, j * 8 + g * 4:j * 8 + g * 4 + 4].unsqueeze(2).to_broadcast([128, 4, 128])
                        TT("dve", d3, ps[bD][:].rearrange("p (h t) -> p h t", h=4), es4, ALU.add, [("ps", bD), "esink"], [dk])
                        ACT(dn[:], dn[:], AF.Ln, [dk], [dk])
                        ACT(dn[:], dn[:], AF.Exp, [dk], [dk], scale=-1.0)
                        TT("dve", qoT[:, :, nsl], ps[bO][:].rearrange("p (h t) -> p h t", h=4), d3, ALU.mult,
                           [("ps", bO), dk], [("qo", n)])

                    def chunk_list(n):
                        chunks = [("ctx", t) for t in range(4)]
                        if n > 0:
                            chunks.append(("prev", n - 1))
                        chunks.append(("cen", n))
                        if n < NT - 1:
                            chunks.append(("next", n + 1))
                        return chunks

                    def drain(gen):
                        for _ in gen:
                            pass

                    if cfg.get("att_stop", 9) > 1:
                        drain(scores(0))
                        for n in range(NT):
                            gs_ = scores(n + 1) if n + 1 < NT else None
                            gp_ = pv_norm(n, chunk_list(n))
                            while gs_ is not None or gp_ is not None:
                                if gs_ is not None:
                                    try:
                                        next(gs_)
                                    except StopIteration:
                                        gs_ = None
                                if gp_ is not None:
                                    try:
                                        next(gp_)
                                    except StopIteration:
                                        gp_ = None
                    slo, ko = ring_next()
                    vo = slo[:, 0:4096].rearrange("p (k f) -> p k f", k=4)
                    DMA("pool", vo, w_outb[j][g * 512:(g + 1) * 512, :].rearrange("(k p) f -> p k f", p=128), (), [ko], scoped=False)
                    ko_i = 0
                    for tb in range(NB if cfg.get("att_stop", 9) > 2 else 0):
                        for dm in range(KC):
                            pd = ko_i % 2
                            ko_i += 1
                            for hh in range(4):
                                MM(ps[pd][:], vo[:, hh, dm * 128:(dm + 1) * 128], qoT[:, hh, tbs(tb)], hh == 0, hh == 3,
                                   [ko] + [("qo", 4 * tb + q) for q in range(4)], [("ps", pd)])
                            STT("dve", xT[:, dm, tbs(tb)], ps[pd][:], modall[:, l, 16 + dm:17 + dm], xT[:, dm, tbs(tb)],
                                ALU.mult, ALU.add, [("ps", pd), mk, ("x", dm, tb)], [("x", dm, tb)])
            S.barrier()

        def rev_ap(t, n):
            return bass.AP(t, n - 1, [[n, 128], [-1, n]])

        def hgrn(l):
            j = l // 2
            mk = ("mod", l)
            with contextlib.ExitStack() as ph:
                ogT = pl(ph, "ogT", [128, 2, T], BF16)
                qd = [pl(ph, "qd%d" % i, [128, T], BF16) for i in range(2)]
                kdT = [pl(ph, "kdT%d" % i, [128, T], BF16) for i in range(2)]
                kdtok = [pl(ph, "kdtok%d" % i, [128, NT, 128], BF16) for i in range(2)]
                vtok = pl(ph, "hvtok", [128, NT, 128], BF16)
                oacc = pl(ph, "oacc", [128, T], F32)
                sig = [pl(ph, "sig%d" % i, [128, TB], F32) for i in range(2)]
                lf = [pl(ph, "lf%d" % i, [128, TB], F32) for i in range(2)]
                ebt = pl(ph, "ebt", [128, TB], F32)
                enbt = pl(ph, "enbt", [128, TB], F32)
                qs = pl(ph, "qs", [128, TB], F32)
                osq = pl(ph, "osq", [128, TB], BF16)
                segm = [pl(ph, "segm%d" % i, [128, TB], F32) for i in range(2)]
                Dor = [pl(ph, "Dor%d" % i, [128, 64], F32) for i in range(2)]
                Dca = [pl(ph, "Dca%d" % i, [128, 64], F32) for i in range(2)]
                Tst = [[pl(ph, "Tst%d_%d" % (d, i), [128, 128], F32) for i in range(2)] for d in range(2)]
                Sbf = [[pl(ph, "Sbf%d_%d" % (d, i), [128, 128], BF16) for i in range(2)] for d in range(2)]
                sfin = [pl(ph, "sfin%d" % i, [128, 128], F32) for i in range(2)]
                amask = [[pl(ph, "am%d_%d" % (d, i), [128, 128], BF16) for i in range(2)] for d in range(2)]
                vblk = [pl(ph, "vblk%d" % i, [128, 2, 4, 128], BF16) for i in range(2)]

                MEMSET("dve", segm[0][:], 1.0, ["segm0"])
                MEMSET("dve", segm[1][:], 1.0, ["segm1"])
                MEMSET("dve", segm[0][:].rearrange("p (c i) -> p c i", i=32)[:, :, 0:1], 0.0, ["segm0"])
                MEMSET("dve", segm[1][:].rearrange("p (c i) -> p c i", i=32)[:, :, 31:32], 0.0, ["segm1"])
                sf_i = 0
                for h in range(8):
                    slA, kA = ring_next()
                    vA = slA[:, 0:4096].rearrange("p (k f) -> p k f", k=8)
                    wsrc = w_ina[j].rearrange("(k p) f -> p k f", p=128)
                    for gi, c0 in enumerate((0, 1024, 2048, 4096)):
                        DMA("pool", vA[:, :, gi * 128:(gi + 1) * 128], wsrc[:, :, c0 + h * 128:c0 + (h + 1) * 128], (), [kA], scoped=False)
                    slB, kB = ring_next()
                    vB = slB[:, 0:4096].rearrange("p (k f) -> p k f", k=8)
                    DMA("pool", vB[:, :, 0:128], wsrc[:, :, 3072 + h * 128:3072 + (h + 1) * 128], (), [kB], scoped=False)
                    for d in range(2):
                        DMA("sp", Tst[d][1][:], st0_in[j, d, h], (), [("T", d, 1)])
                    for tb in range(NB):
                        for gi in range(3):
                            for kc in range(KC):
                                MM(ps[gi][:], vA[:, kc, gi * 128:(gi + 1) * 128], hT[:, kc, tbs(tb)], kc == 0, kc == KC - 1,
                                   [kA, ("h", kc, tb)], [("ps", gi)])
                        ACT(qs[:], ps[0][:], AF.Sigmoid, [("ps", 0)], ["qs"])
                        for d in range(2):
                            ACT(sig[d][:], ps[1 + d][:], AF.Sigmoid, [("ps", 1 + d)], [("sig", d)])
                        TT("dve", qs[:], qs[:], ps[0][:], ALU.mult, ["qs", ("ps", 0)], ["qs"])
                        for ti in range(4):
                            t = tb * 4 + ti
                            for kc in range(KC):
                                MM(ps[3][:, ti * 128:(ti + 1) * 128], hT[:, kc, t * 128:(t + 1) * 128], vB[:, kc, 0:128],
                                   kc == 0, kc == KC - 1, [kB, ("h", kc, tb)], [("ps", 3)])
                        CP("dve", vtok[:, tb * 4:(tb + 1) * 4, :], ps[3][:].rearrange("p (t e) -> p t e", t=4), [("ps", 3)],
                           [("hvt", tb * 4 + q) for q in range(4)])
                        for d in range(2):
                            col = d * 8 + h
                            ACT(lf[d][:], sig[d][:], AF.Ln, [("sig", d), "lbt", "omlt"], [("lf", d)],
                                scale=omlt[:, j, col:col + 1], bias=lbt[:, j, col:col + 1])
                        for d in range(2):
                            if d == 0:
                                S.op("dve", lambda: nc.vector.tensor_tensor_scan(
                                    out=lf[0][:], data0=segm[0][:], data1=lf[0][:], initial=0.0, op0=ALU.mult, op1=ALU.add),
                                    [("lf", 0), "segm0"], [("lf", 0)])
                            else:
                                S.op("dve", lambda: nc.vector.tensor_tensor_scan(
                                    out=rev_ap(lf[1], TB), data0=rev_ap(segm[1], TB), data1=rev_ap(lf[1], TB), initial=0.0,
                                    op0=ALU.mult, op1=ALU.add), [("lf", 1), "segm1"], [("lf", 1)])
                        for d in range(2):
                            col = d * 8 + h
                            ACT(ebt[:], lf[d][:], AF.Exp, [("lf", d)], ["ebt"])
                            ACT(enbt[:], lf[d][:], AF.Exp, [("lf", d)], ["enbt"], scale=-1.0)
                            TS("dve", sig[d][:], sig[d][:], nomlt[:, j, col:col + 1], omlt[:, j, col:col + 1], ALU.mult, ALU.add,
                               [("sig", d), "omlt", "nomlt"], [("sig", d)])
                            TT("dve", qd[d][:, tbs(tb)], qs[:], ebt[:], ALU.mult, ["qs", "ebt"], [("qd", d, tb)])
                            TT("dve", kdT[d][:, tbs(tb)], sig[d][:], enbt[:], ALU.mult, [("sig", d), "enbt"], [("kdT", d, tb)])
                            e3 = ebt[:].rearrange("p (c i) -> p c i", i=32)
                            pos = 31 if d == 0 else 0
                            CP("dve", Dor[d][:, tb * 16:(tb + 1) * 16].unsqueeze(2), e3[:, :, pos:pos + 1], ["ebt"], [("Dor", d)])
                        for d in range(2):
                            pb = 6 + d
                            for ti in range(4):
                                t = tb * 4 + ti
                                TR(psb[pb][:, ti * 128:(ti + 1) * 128], kdT[d][:, t * 128:(t + 1) * 128], idb[:],
                                   [("kdT", d, tb), "idb"], [("ps", pb)])
                            CP("dve", kdtok[d][:, tb * 4:(tb + 1) * 4, :], psb[pb][:, 0:512].rearrange("p (t e) -> p t e", t=4),
                               [("ps", pb)], [("kdt", d, tb * 4 + q) for q in range(4)])
                    for d in range(2):
                        CP("dve", Dca[d][:], Dor[d][:], [("Dor", d)], [("Dca", d)])
                        pos = 7 if d == 0 else 0
                        dv = Dca[d][:].rearrange("p (s c) -> p s c", c=8)[:, :, pos:pos + 1]
                        TS("dve", dv, dv, pcs[:, PC_CARRY:PC_CARRY + 1], None, ALU.mult, None, [("Dca", d), "pcs"], [("Dca", d)])
                    for i in range(NT):
                        dirs = ((0, i), (1, NT - 1 - i))
                        for ti_, tl_ in enumerate((i, NT - 1 - i)):
                            TT("pool", vblk[i % 2][:, ti_, :, :], vtok[:, tl_, :].unsqueeze(1).to_broadcast([128, 4, 128]),
                               bmb[:].unsqueeze(2).to_broadcast([128, 4, 128]), ALU.mult,
                               [("hvt", tl_), "bmb"], [("vblk", i % 2, ti_)])
                        for d, tile in dirs:
                            tsl = slice(tile * 128, (tile + 1) * 128)
                            pU = d * 2 + i % 2
                            MM(ps[pU][:], kdtok[d][:, tile, :], vblk[i % 2][:, d, :, :], True, True,
                               [("kdt", d, tile), ("vblk", i % 2, d)], [("ps", pU)])
                            xb = 4 + d * 2 + i % 2
                            MM(ps[xb][:, 0:128], kdT[d][:, tsl], qd[d][:, tsl], True, True,
                               [("kdT", d, tile // 4), ("qd", d, tile // 4)], [("ps", xb)])
                        for d, tile in dirs:
                            xb = 4 + d * 2 + i % 2
                            TT("dve", amask[d][i % 2][:], ps[xb][:, 0:128], hmk[:, d, :], ALU.mult, [("ps", xb), "hmk"],
                               [("am", d, i % 2)])
                        for d, tile in dirs:
                            xb = 4 + d * 2 + i % 2
                            MM(ps[xb][:, 128:256], vtok[:, tile, :], amask[d][i % 2][:], True, False,
                               [("hvt", tile), ("am", d, i % 2)], [("ps", xb)])
                        for cc in range(4):
                            for d, tile in dirs:
                                pU = d * 2 + i % 2
                                xb = 4 + d * 2 + i % 2
                                ok = ("ps", xb)
                                pO = ps[xb][:, 128:256]
                                c = cc if d == 0 else 3 - cc
                                cidx = tile * 4 + c
                                n = cidx if d == 0 else 63 - cidx
                                pc = cidx - 1 if d == 0 else cidx + 1
                                told = Tst[d][(n - 1) % 2]
                                tnew = Tst[d][n % 2]
                                sbf = Sbf[d][n % 2]
                                kold = ("T", d, (n - 1) % 2)
                                knew = ("T", d, n % 2)
                                ksb = ("Sbf", d, n % 2)
                                if n == 0:
                                    CP("act", sbf[:], told[:], [kold], [ksb])
                                else:
                                    ACT(sbf[:], told[:], AF.Copy, [kold, ("Dca", d)], [ksb], scale=Dca[d][:, pc:pc + 1])
                                MM(pO[:, c * 32:(c + 1) * 32], sbf[:], qd[d][:, tile * 128 + c * 32:tile * 128 + (c + 1) * 32],
                                   False, cc == 3, [ksb, ("qd", d, tile // 4)], [ok])
                                uap = ps[pU][:, c * 128:(c + 1) * 128]
                                if n == 0:
                                    TT("dve", tnew[:], told[:], uap, ALU.add, [kold, ("ps", pU)], [knew])
                                else:
                                    STT("dve", tnew[:], told[:], Dca[d][:, pc:pc + 1], uap, ALU.mult, ALU.add,
                                        [kold, ("Dca", d), ("ps", pU)], [knew])
                                seg_end = (cidx % 8 == 7) if d == 0 else (cidx % 8 == 0)
                                if seg_end:
                                    m = cidx // 8
                                    sfb = sfin[sf_i % 2]
                                    ksf = ("sfin", sf_i % 2)
                                    sf_i += 1
                                    ACT(sfb[:], tnew[:], AF.Copy, [knew, ("Dor", d)], [ksf], scale=Dor[d][:, cidx:cidx + 1])
                                    DMA("sp", ns_out[j, d, m, h], sfb[:], [ksf], [], is_output=True)
                        for d, tile in dirs:
                            tsl = slice(tile * 128, (tile + 1) * 128)
                            xb = 4 + d * 2 + i % 2
                            pO = ps[xb][:, 128:256]
                            if i <= 7:
                                CP("dve", oacc[:, tsl], pO, [("ps", xb)], [("oacc", tile)])
                            else:
                                TT("dve", oacc[:, tsl], oacc[:, tsl], pO, ALU.add, [("ps", xb), ("oacc", tile)], [("oacc", tile)])
                    gcol = R_GN + j * 8 + h
                    for tb in range(NB):
                        ok4 = [("oacc", 4 * tb + q) for q in range(4)]
                        if tb % 2 == 0:
                            b_rs, k_rs = ebt, "ebt"
                            b_t1, k_t1 = enbt, "enbt"
                            b_gs, k_gs = qs, "qs"
                            b_sq, k_sq = osq[:], "osq"
                        else:
                            b_rs, k_rs = sig[0], ("sig", 0)
                            b_t1, k_t1 = sig[1], ("sig", 1)
                            b_gs, k_gs = lf[0], ("lf", 0)
                            b_sq, k_sq = lf[1][:].bitcast(BF16)[:, 0:TB], ("lf", 1)
                        pA_ = (tb % 2) * 2
                        pG_ = (tb % 2) * 2 + 1
                        ACT(b_sq, oacc[:, tbs(tb)], AF.Square, ok4, [k_sq])
                        MM(ps[pA_][:], onese[:], b_sq, True, True, [k_sq, "onese"], [("ps", pA_)])
                        for kc in range(KC):
                            MM(ps[pG_][:], vA[:, kc, 384:512], hT[:, kc, tbs(tb)], kc == 0, kc == KC - 1, [kA, ("h", kc, tb)], [("ps", pG_)])
                        ACT(b_rs[:], ps[pA_][:], AF.Ln, [("ps", pA_)], [k_rs], bias=EPS)
                        ACT(b_rs[:], b_rs[:], AF.Exp, [k_rs], [k_rs], scale=-0.5)
                        ACT(b_gs[:], ps[pG_][:], AF.Sigmoid, [("ps", pG_)], [k_gs])
                        STT("dve", b_t1[:], oacc[:, tbs(tb)], vecT[:, gcol:gcol + 1], b_rs[:], ALU.mult, ALU.mult,
                            ok4 + ["vecT", k_rs], [k_t1])
                        TT("dve", b_t1[:], b_t1[:], b_gs[:], ALU.mult, [k_t1, k_gs], [k_t1])
                        TT("dve", ogT[:, h % 2, tbs(tb)], b_t1[:], ps[pG_][:], ALU.mult, [k_t1, ("ps", pG_)], [("og", h % 2, tb)])
                    if h % 2 == 1:
                        slo, ko = ring_next()
                        vo = slo[:, 0:2048].rearrange("p (k f) -> p k f", k=2)
                        DMA("pool", vo, w_outa[j][(h - 1) * 128:(h + 1) * 128, :].rearrange("(k p) f -> p k f", p=128), (), [ko],
                            scoped=False)
                        ko_i = 0
                        for tb in range(NB):
                            for dm in range(KC):
                                pd = 4 + ko_i % 4
                                ko_i += 1
                                for hh in range(2):
                                    MM(ps[pd][:], vo[:, hh, dm * 128:(dm + 1) * 128], ogT[:, hh, tbs(tb)], hh == 0, hh == 1,
                                       [ko, ("og", hh, tb)], [("ps", pd)])
                                STT("dve", xT[:, dm, tbs(tb)], ps[pd][:], modall[:, l, 16 + dm:17 + dm], xT[:, dm, tbs(tb)],
                                    ALU.mult, ALU.add, [("ps", pd), mk, ("x", dm, tb)], [("x", dm, tb)])
            S.barrier()


        l0 = cfg.get("l0", 0)
        for l in range(l0, depth):
            if l == l0 or not cfg["ffn"]:
                adaln(l)
            if cfg["mixers"]:
                norm_mod(l, 0)
                if l % 2 == 0:
                    hgrn(l)
                else:
                    attn(l)
            if cfg["ffn"]:
                norm_mod(l, 1)
                if l + 1 < depth:
                    ada_next[0] = adaln_gen(l + 1)
                ffn(l)
                while ada_next[0] is not None:
                    ada_step()
        final_out()
        stats = S.emit()
        if cfg.get("dbg"):
            print("sched stats", stats)
    return nc

def _consts():
    cm = np.zeros((5, 128, 128), np.float32)
    cm[0] = np.eye(128, dtype=np.float32)
    j = np.arange(128)[:, None]
    i = np.arange(128)[None, :]
    cm[1] = (i <= j).astype(np.float32)
    cm[2] = (j <= i).astype(np.float32)
    same = (j // 32) == (i // 32)
    cm[3] = (same & (j <= i)).astype(np.float32)
    cm[4] = (same & (j >= i)).astype(np.float32)
    return cm


def _rope_tables(sample):
    tab = np.zeros((NT, 128, 256), np.float32)
    if not sample:
        tab[:, :, 0:128] = 1.0
        return tab
    t = np.arange(T, dtype=np.float32)
    row = np.floor(t / 64.0).astype(np.float32)
    col = (t - row * 64.0).astype(np.float32)
    inv = (10000.0 ** (-np.arange(32, dtype=np.float32) / 32.0)).astype(np.float32)
    ar = row[:, None] * inv[None, :]
    ac = col[:, None] * inv[None, :]
    cr, sr, cc, sc_ = np.cos(ar), np.sin(ar), np.cos(ac), np.sin(ac)
    cos = np.concatenate([cr, cr, cc, cc], axis=1)
    sin = np.concatenate([-sr, sr, -sc_, sc_], axis=1)
    tab[:, :, 0:128] = cos.reshape(NT, 128, 128)
    tab[:, :, 128:256] = sin.reshape(NT, 128, 128)
    return tab.astype(np.float32)


def _pcs(sample):
    p = np.zeros((N_PCS,), np.float32)
    p[PC_CARRY] = 1.0 if sample else 0.0
    p[PC_CTXB] = 0.0 if sample else -30000.0
    for n in range(NT):
        for side in range(2):
            base = PC_MASK + (n * 2 + side) * 2
            if sample:
                p[base], p[base + 1] = 1.0, 0.0
            else:
                valid = (side == 1 and n % 2 == 0) or (side == 0 and n % 2 == 1)
                p[base], p[base + 1] = 0.0, (1.0 if valid else 0.0)
    return np.ascontiguousarray(np.broadcast_to(p[None, :], (128, N_PCS))).astype(np.float32)


def kernel(x_prompt, x_sample, cache_k, cache_v, state_hgrn, c, c_ctx, w_ada, b_ada, norm1, norm2, norm_final,
           w_gate_up, w_down, w_in_a, lower_bounds, gnorm_a, w_out_a, w_qkv_b, w_out_b, sink_b):
    f = lambda a: np.ascontiguousarray(np.asarray(a), dtype=np.float32)
    x_prompt, x_sample, cache_k, cache_v, state_hgrn = map(f, (x_prompt, x_sample, cache_k, cache_v, state_hgrn))
    c, c_ctx, b_ada, norm1, norm2, norm_final = map(f, (c, c_ctx, b_ada, norm1, norm2, norm_final))
    lower_bounds, gnorm_a, sink_b = map(f, (lower_bounds, gnorm_a, sink_b))
    weights = dict(w_ada=f(w_ada), w_gate_up=f(w_gate_up), w_down=f(w_down), w_in_a=f(w_in_a),
                   w_out_a=f(w_out_a), w_qkv_b=f(w_qkv_b), w_out_b=f(w_out_b))
    vecs = np.zeros((N_VROWS, 128), np.float32)
    vecs[R_BADA:R_BADA + 192] = b_ada.reshape(192, 128)
    vecs[R_N1:R_N1 + 32] = norm1.reshape(32, 128)
    vecs[R_N2:R_N2 + 32] = norm2.reshape(32, 128)
    vecs[R_NF:R_NF + 8] = norm_final.reshape(8, 128)
    vecs[R_GN:R_GN + 16] = gnorm_a.reshape(16, 128)
    vecs[R_LB:R_LB + 32] = lower_bounds.reshape(32, 128)
    cm = _consts()
    bm = (np.arange(128)[:, None] // 32 == np.arange(4)[None, :]).astype(np.float32)
    sinkb = np.ascontiguousarray(np.broadcast_to(sink_b.reshape(1, 16), (128, 16))).astype(np.float32)
    rope_s, rope_p = _rope_tables(True), _rope_tables(False)
    pcs_s, pcs_p = _pcs(True), _pcs(False)
    zck = np.zeros((2, 512, 256), np.float32)
    zst = np.zeros((2, 2, 8, 128, 128), np.float32)
    in_maps = []
    for core in range(8):
        sample = core < 4
        if sample:
            b = core
            m = dict(x=x_sample[b], cond=np.ascontiguousarray(c[b].reshape(8, 128).T),
                     ck=np.ascontiguousarray(cache_k[b].reshape(2, 512, 256)),
                     cv=np.ascontiguousarray(cache_v[b].reshape(2, 512, 256)),
                     st0=np.ascontiguousarray(state_hgrn[b]), rope=rope_s, pcs=pcs_s)
        else:
            s0 = (core - 4) * 8
            m = dict(x=np.ascontiguousarray(x_prompt[s0:s0 + 8].reshape(T, D)),
                     cond=np.ascontiguousarray(c_ctx.reshape(8, 128).T), ck=zck, cv=zck, st0=zst,
                     rope=rope_p, pcs=pcs_p)
        m.update(vecs=vecs, cmat=cm, sinkb=sinkb, bm=bm)
        m.update(weights)
        in_maps.append(m)
    nc = build_nc(CFG)
    res = run_bass_kernel_spmd(nc, in_maps, core_ids=list(range(8)))
    R = res.results
    y_prompt = np.zeros((32, 256, D), np.float32)
    y_sample = np.zeros((4, T, D), np.float32)
    nk = np.zeros((32, 2, 256, 2, 128), np.float32)
    nv = np.zeros((32, 2, 256, 2, 128), np.float32)
    nst = np.zeros((32, 2, 2, 8, 128, 128), np.float32)
    for core in range(8):
        r = R[core]
        if core < 4:
            y_sample[core] = r["y"]
        else:
            s0 = (core - 4) * 8
            y_prompt[s0:s0 + 8] = r["y"].reshape(8, 256, D)
            nk[s0:s0 + 8] = r["nk"].reshape(2, 8, 256, 2, 128).transpose(1, 0, 2, 3, 4)
            nv[s0:s0 + 8] = r["nv"].reshape(2, 8, 256, 2, 128).transpose(1, 0, 2, 3, 4)
            nst[s0:s0 + 8] = r["ns"].transpose(2, 0, 1, 3, 4, 5)
    return (y_prompt, y_sample, nk, nv, nst)
```

```python
import numpy as np
import contextlib
import concourse.bass as bass
import concourse.mybir as mybir
from concourse.bass_utils import run_bass_kernel_spmd
F32 = mybir.dt.float32
BF16 = mybir.dt.bfloat16
AF = mybir.ActivationFunctionType
ALU = mybir.AluOpType
AX = mybir.AxisListType


COMPUTE = ("pe", "act", "dve", "pool")


class Op:
    __slots__ = ("eng", "fn", "reads", "writes", "deps", "is_dma", "marked",
                 "token", "idx", "eidx", "pre_waits")

    def __init__(self, eng, fn, reads, writes, is_dma):
        self.eng = eng
        self.fn = fn
        self.reads = reads
        self.writes = writes
        self.is_dma = is_dma
        self.deps = []
        self.marked = False
        self.token = None
        self.pre_waits = []


class Sched:
    def __init__(self, nc, es, n_dma_sems=8):
        self.nc = nc
        self.engs = {"pe": nc.tensor, "act": nc.scalar, "dve": nc.vector,
                     "pool": nc.gpsimd, "sp": nc.sync}
        self.sems = {e: es.enter_context(nc.semaphore("s_" + e)) for e in COMPUTE}
        self.dma_sems = {}
        for q in ("sp", "pool", "act"):
            self.dma_sems[q] = [es.enter_context(nc.semaphore("d_%s%d" % (q, i)))
                                for i in range(n_dma_sems)]
        self.ops = []
        self.last_w = {}
        self.readers = {}
        self.out_dma_ops = []
        self.last_on_eng = {}
        self.dma_since = []
        self.bar = []

    def barrier(self):
        self.bar = sorted(set(list(self.last_on_eng.values()) + self.dma_since))
        self.dma_since = []

    def _keys(self, ks):
        out = []
        for k in ks:
            if isinstance(k, list):
                out.extend(k)
            else:
                out.append(k)
        return out

    def op(self, eng, fn, reads=(), writes=()):
        rk = self._keys(reads)
        wk = self._keys(writes)
        wk = wk + [k for k in rk if isinstance(k, tuple) and k[0] == "ps"]
        rk = [k for k in rk if not (isinstance(k, tuple) and k[0] == "ps")]
        o = Op(eng, fn, rk, wk, False)
        self._add(o, True)
        self.last_on_eng[eng] = o.idx
        return o

    def dma(self, queue, fn, reads=(), writes=(), is_output=False, scoped=True):
        o = Op(queue, fn, self._keys(reads), self._keys(writes), True)
        self._add(o, scoped)
        if scoped:
            self.dma_since.append(o.idx)
        if is_output:
            self.out_dma_ops.append(o)
        return o

    def _add(self, o, scoped=True):
        o.idx = len(self.ops)
        deps = set(self.bar) if scoped else set()
        for k in o.reads:
            w = self.last_w.get(k)
            if w is not None:
                deps.add(w)
        for k in o.writes:
            w = self.last_w.get(k)
            if w is not None:
                deps.add(w)
            for r in self.readers.get(k, ()):
                deps.add(r)
        deps.discard(o.idx)
        o.deps = sorted(deps)
        for k in o.reads:
            self.readers.setdefault(k, []).append(o.idx)
        for k in o.writes:
            self.last_w[k] = o.idx
            self.readers[k] = []
        self.ops.append(o)

    def emit(self):
        ops = self.ops
        eng_pos = {}
        per_eng_count = {}
        for o in ops:
            o.eidx = per_eng_count.get(o.eng, 0)
            per_eng_count[o.eng] = o.eidx + 1
        for o in ops:
            for d in o.deps:
                p = ops[d]
                if p.is_dma:
                    continue
                if p.eng == o.eng and not o.is_dma:
                    if p.eng == "pe":
                        continue
                p.marked = True
        cnt = {e: 0 for e in COMPUTE}
        dma_rr = {q: 0 for q in self.dma_sems}
        dma_val = {}
        for q, lst in self.dma_sems.items():
            for i in range(len(lst)):
                dma_val[(q, i)] = 0
        for o in ops:
            if o.is_dma:
                q = o.eng
                i = dma_rr[q]
                dma_rr[q] = (i + 1) % len(self.dma_sems[q])
                prev = dma_val[(q, i)]
                if prev > 0:
                    o.pre_waits.append((("dma", q, i), prev))
                dma_val[(q, i)] = prev + 16
                o.token = (("dma", q, i), prev + 16)
            elif o.marked:
                cnt[o.eng] += 1
                o.token = (("eng", o.eng), cnt[o.eng])
        clock = {e: {} for e in self.engs}
        vc = [None] * len(ops)

        def semh(s):
            if s[0] == "eng":
                return self.sems[s[1]]
            return self.dma_sems[s[1]][s[2]]

        def merge(a, b):
            for k, v in b.items():
                if a.get(k, 0) < v:
                    a[k] = v

        nwaits = 0
        for o in ops:
            e = o.eng
            ck = clock[e]
            need = {}
            for s, v in o.pre_waits:
                if ck.get(s, 0) < v:
                    need[s] = max(need.get(s, 0), v)
            for d in o.deps:
                p = ops[d]
                if p.token is None:
                    continue
                if (not p.is_dma) and p.eng == e and not o.is_dma:
                    if e == "pe":
                        continue
                s, v = p.token
                if ck.get(s, 0) >= v:
                    continue
                need[s] = max(need.get(s, 0), v)
                merge(ck, vc[d])
            engh = self.engs[e]
            for s, v in need.items():
                engh.wait_ge(semh(s), v)
                nwaits += 1
                if ck.get(s, 0) < v:
                    ck[s] = v
            inst = o.fn()
            if o.token is not None:
                s, v = o.token
                if o.is_dma:
                    inst.then_inc(semh(s), 16)
                else:
                    inst.then_inc(semh(s), 1)
                snap = dict(ck)
                snap[s] = v
                vc[o.idx] = snap
        sp = self.engs["sp"]
        ck = clock["sp"]
        need = {}
        for o in self.out_dma_ops:
            s, v = o.token
            need[s] = max(need.get(s, 0), v)
        for (q, i), v in dma_val.items():
            if v > 0:
                s = ("dma", q, i)
                need[s] = max(need.get(s, 0), v)
        for s, v in need.items():
            sp.wait_ge(semh(s), v)
        return dict(n_ops=len(ops), n_waits=nwaits, counts=per_eng_count, sem_counts=cnt)

T = 2048
D = 1024
KC = 8
NT = 16
NB = 4
TB = 512
DFF = 2816
EPS = 1e-6
CFG = dict(depth=4, mixers=True, ffn=True, dbg=False, l0=0)

R_BADA = 0
R_N1 = 192
R_N2 = 224
R_NF = 256
R_GN = 264
R_LB = 280
N_VROWS = 384
PC_CARRY = 0
PC_CTXB = 1
PC_MASK = 2
N_PCS = 2 + 64


def build_nc(cfg):
    nc = bass.Bass("TRN2", target_bir_lowering=False)
    depth = cfg["depth"]

    def din(name, shape, dt=F32):
        return nc.dram_tensor(name, list(shape), dt, kind="ExternalInput").ap()

    def dout(name, shape, dt=F32):
        return nc.dram_tensor(name, list(shape), dt, kind="ExternalOutput").ap()

    x_in = din("x", [T, D])
    cond_in = din("cond", [128, KC])
    vecs_in = din("vecs", [N_VROWS, 128])
    cmat_in = din("cmat", [5, 128, 128])
    pcs_in = din("pcs", [128, N_PCS])
    rope_in = din("rope", [NT, 128, 256])
    ck_in = din("ck", [2, 512, 256])
    cv_in = din("cv", [2, 512, 256])
    st0_in = din("st0", [2, 2, 8, 128, 128])
    sink_in = din("sinkb", [128, 16])
    bm_in = din("bm", [128, 4])
    w_ada = din("w_ada", [4, D, 6 * D])
    w_gu = din("w_gate_up", [4, D, 2 * DFF])
    w_dn = din("w_down", [4, DFF, D])
    w_ina = din("w_in_a", [2, D, 5 * D])
    w_outa = din("w_out_a", [2, D, D])
    w_qkv = din("w_qkv_b", [2, D, 1536])
    w_outb = din("w_out_b", [2, D, D])
    y_out = dout("y", [T, D])
    nk_out = dout("nk", [2, T, 256])
    nv_out = dout("nv", [2, T, 256])
    ns_out = dout("ns", [2, 2, 8, 8, 128, 128])

    with contextlib.ExitStack() as es:
        S = Sched(nc, es)

        def sb(name, shape, dt):
            return es.enter_context(nc.sbuf_tensor("sb_" + name, list(shape), dt))

        uniq = [0]

        def pl(ph, name, shape, dt):
            uniq[0] += 1
            return ph.enter_context(nc.sbuf_tensor("pl%d_%s" % (uniq[0], name), list(shape), dt))

        xT = sb("xT", [128, KC, T], F32)
        hT = sb("hT", [128, KC, T], BF16)
        ring = [sb("ring%d" % i, [128, 4096], BF16) for i in range(4)]
        vecT = sb("vecT", [128, N_VROWS], F32)
        idf = sb("idf", [128, 128], F32)
        idb = sb("idb", [128, 128], BF16)
        tri = sb("tri", [128, 2, 128], BF16)
        hmk = sb("hmk", [128, 2, 128], BF16)
        ones1 = sb("ones1", [128, 128], BF16)
        onesd = sb("onesd", [128, 128], BF16)
        onese = sb("onese", [128, 128], BF16)
        pcs = sb("pcs", [128, N_PCS], F32)
        bmb = sb("bmb", [128, 4], BF16)
        esink = sb("esink", [128, 16], F32)
        scb = sb("scb", [128, KC], BF16)
        condf = sb("condf", [128, KC], F32)
        modall = sb("modall", [128, 4, 64], F32)
        lbt = sb("lbt", [128, 2, 16], F32)
        omlt = sb("omlt", [128, 2, 16], F32)
        nomlt = sb("nomlt", [128, 2, 16], F32)
        ps = [es.enter_context(nc.psum_tensor("ps%d" % i, [128, 512], F32)) for i in range(8)]
        psb = [p[:].bitcast(BF16) for p in ps]

        ring_i = [0]

        def ring_next():
            i = ring_i[0]
            ring_i[0] = (i + 1) % 4
            return ring[i], ("ring", i)

        def MM(out, lhsT, rhs, start, stop, r, w, **kw):
            S.op("pe", lambda: nc.tensor.matmul(out, lhsT, rhs, start=start, stop=stop, **kw), r, w)

        def TR(out, in_, ident, r, w):
            S.op("pe", lambda: nc.tensor.transpose(out, in_, ident), r, w)

        def ACT(out, in_, func, r, w, bias=None, scale=None):
            kw = {}
            if bias is not None:
                kw["bias"] = bias
            if scale is not None:
                kw["scale"] = scale
            S.op("act", lambda: nc.scalar.activation(out=out, in_=in_, func=func, **kw), r, w)

        def ENG(e):
            return {"dve": nc.vector, "pool": nc.gpsimd}[e]

        def TT(e, out, in0, in1, op, r, w):
            S.op(e, lambda: ENG(e).tensor_tensor(out=out, in0=in0, in1=in1, op=op), r, w)

        def TS(e, out, in0, s1, s2, op0, op1, r, w):
            if s2 is None:
                S.op(e, lambda: ENG(e).tensor_scalar(out=out, in0=in0, scalar1=s1, scalar2=None, op0=op0), r, w)
            else:
                S.op(e, lambda: ENG(e).tensor_scalar(out=out, in0=in0, scalar1=s1, scalar2=s2, op0=op0, op1=op1), r, w)

        def STT(e, out, in0, scalar, in1, op0, op1, r, w):
            S.op(e, lambda: ENG(e).scalar_tensor_tensor(out=out, in0=in0, scalar=scalar, in1=in1, op0=op0, op1=op1), r, w)

        def CP(e, out, in_, r, w):
            if e == "act":
                S.op("act", lambda: nc.scalar.copy(out=out, in_=in_), r, w)
            else:
                S.op(e, lambda: ENG(e).tensor_copy(out=out, in_=in_), r, w)

        def MEMSET(e, out, val, w):
            S.op(e, lambda: ENG(e).memset(out, val), (), w)

        def DMA(q, out, in_, r, w, is_output=False, scoped=True):
            eng = {"sp": nc.sync, "pool": nc.gpsimd, "act": nc.scalar}[q]
            S.dma(q, lambda: eng.dma_start(out=out, in_=in_), r, w, is_output=is_output, scoped=scoped)

        def tbs(tb):
            return slice(tb * TB, (tb + 1) * TB)

        with contextlib.ExitStack() as ph:
            vtmp = pl(ph, "vtmp", [128, 3, 128], F32)
            DMA("sp", idf[:], cmat_in[0], (), ["idf"])
            DMA("pool", idb[:], cmat_in[0], (), ["idb"])
            DMA("pool", tri[:, 0, :], cmat_in[1], (), ["tri"])
            DMA("pool", tri[:, 1, :], cmat_in[2], (), ["tri"])
            DMA("pool", hmk[:, 0, :], cmat_in[3], (), ["hmk"])
            DMA("pool", hmk[:, 1, :], cmat_in[4], (), ["hmk"])
            DMA("sp", pcs[:], pcs_in, (), ["pcs"])
            DMA("pool", bmb[:], bm_in, (), ["bmb"])
            DMA("sp", esink[:], sink_in, (), ["esink"])
            DMA("sp", condf[:], cond_in, (), ["condf"])
            for i in range(3):
                DMA("sp", vtmp[:, i, :], vecs_in[i * 128:(i + 1) * 128, :], (), [("vtmp", i)])
            MEMSET("dve", ones1[:], 1.0, ["ones1"])
            MEMSET("dve", onesd[:], 1.0 / D, ["onesd"])
            MEMSET("dve", onese[:], 1.0 / 128, ["onese"])
            for i in range(3):
                TR(ps[0][:, i * 128:(i + 1) * 128], vtmp[:, i, :], idf[:], [("vtmp", i), "idf"], [("ps", 0)])
            CP("dve", vecT[:], ps[0][:, 0:N_VROWS], [("ps", 0)], ["vecT"])
            ACT(esink[:], esink[:], AF.Exp, ["esink"], ["esink"])
            ACT(scb[:], condf[:], AF.Silu, ["condf"], ["scb"])
            MEMSET("dve", lbt[:, 0, :], 0.0, ["lbt"])
            TT("dve", lbt[:, 1, :], vecT[:, R_LB + 16:R_LB + 32], vecT[:, R_LB:R_LB + 16], ALU.subtract, ["vecT", "lbt"], ["lbt"])
            ACT(lbt[:, 1, :], lbt[:, 1, :], AF.Sigmoid, ["lbt"], ["lbt"])
            TS("dve", omlt[:], lbt[:], -1.0, 1.0, ALU.mult, ALU.add, ["lbt"], ["omlt"])
            TS("dve", nomlt[:], omlt[:], -1.0, None, ALU.mult, None, ["omlt"], ["nomlt"])

            xtok = [pl(ph, "xtok%d" % i, [128, D], F32) for i in range(2)]
            for t in range(NT):
                xt = xtok[t % 2]
                DMA("sp", xt[:], x_in[t * 128:(t + 1) * 128, :], (), [("xtok", t % 2)])
                for hf in range(2):
                    pb = (2 * t + hf) % 4
                    for c4 in range(4):
                        c = hf * 4 + c4
                        TR(ps[pb][:, c4 * 128:(c4 + 1) * 128], xt[:, c * 128:(c + 1) * 128], idf[:],
                           [("xtok", t % 2), "idf"], [("ps", pb)])
                    dst = xT[:, hf * 4:(hf + 1) * 4, t * 128:(t + 1) * 128]
                    src = ps[pb][:].rearrange("p (c t) -> p c t", c=4)
                    wk = [("x", hf * 4 + c4, t // 4) for c4 in range(4)]
                    if hf == 0:
                        CP("dve", dst, src, [("ps", pb)], wk)
                    else:
                        CP("act", dst, src, [("ps", pb)], wk)
        S.barrier()

        def adaln_gen(l):
            for s in range(12):
                slot, sk = ring_next()
                sv = slot[:, 0:4096].rearrange("p (k f) -> p k f", k=8)
                DMA("pool", sv, w_ada[l].rearrange("(k p) f -> p k f", p=128)[:, :, s * 512:(s + 1) * 512],
                    (), [sk], scoped=False)
                for f in range(4):
                    col = s * 4 + f
                    for kc in range(KC):
                        MM(ps[7][:, col:col + 1], sv[:, kc, f * 128:(f + 1) * 128], scb[:, kc:kc + 1],
                           kc == 0, kc == KC - 1, [sk, "scb"], [("ps", 7)])
                yield s
            mk = ("mod", l)
            TT("dve", modall[:, l, 0:48], ps[7][:, 0:48], vecT[:, R_BADA + l * 48:R_BADA + (l + 1) * 48], ALU.add,
               [("ps", 7), "vecT"], [mk])
            STT("dve", modall[:, l, 48:56], modall[:, l, 8:16], 1.0, vecT[:, R_N1 + l * 8:R_N1 + (l + 1) * 8],
                ALU.add, ALU.mult, [mk, "vecT"], [mk])
            STT("dve", modall[:, l, 56:64], modall[:, l, 32:40], 1.0, vecT[:, R_N2 + l * 8:R_N2 + (l + 1) * 8],
                ALU.add, ALU.mult, [mk, "vecT"], [mk])
            yield 12

        def adaln(l):
            for _ in adaln_gen(l):
                pass

        ada_next = [None]

        def ada_step():
            g = ada_next[0]
            if g is not None:
                try:
                    next(g)
                except StopIteration:
                    ada_next[0] = None

        def norm_mod(l, which):
            mk = ("mod", l)
            Acol = 48 if which == 0 else 56
            Bcol = 0 if which == 0 else 24
            with contextlib.ExitStack() as ph:
                sq = [pl(ph, "sq%d" % i, [128, KC, TB], BF16) for i in range(2)]
                rstd = [pl(ph, "rstd%d" % i, [128, TB], F32) for i in range(2)]
                tmpn = [pl(ph, "tmpn%d" % i, [128, TB], F32) for i in range(2)]
                for tb in range(NB):
                    q = sq[tb % 2]
                    rs = rstd[tb % 2]
                    for c in range(KC):
                        ACT(q[:, c, :], xT[:, c, tbs(tb)], AF.Square, [("x", c, tb)], [("sq", tb % 2, c)])
                    for c in range(KC):
                        MM(ps[6][:], onesd[:], q[:, c, :], c == 0, c == KC - 1, [("sq", tb % 2, c), "onesd"], [("ps", 6)])
                    ACT(rs[:], ps[6][:], AF.Ln, [("ps", 6)], [("rstd", tb % 2)], bias=EPS)
                    ACT(rs[:], rs[:], AF.Exp, [("rstd", tb % 2)], [("rstd", tb % 2)], scale=-0.5)
                    for c in range(KC):
                        tm = tmpn[c % 2]
                        STT("dve", tm[:], xT[:, c, tbs(tb)], modall[:, l, Acol + c:Acol + c + 1], rs[:], ALU.mult, ALU.mult,
                            [("x", c, tb), mk, ("rstd", tb % 2)], [("tmpn", c % 2)])
                        ACT(hT[:, c, tbs(tb)], tm[:], AF.Identity, [("tmpn", c % 2), mk], [("h", c, tb)],
                            bias=modall[:, l, Bcol + c:Bcol + c + 1])
            S.barrier()

        def ffn(l):
            mk = ("mod", l)
            with contextlib.ExitStack() as ph:
                actT = pl(ph, "actT", [128, 11, T], BF16)
                sg = [pl(ph, "sg%d" % i, [128, TB], F32) for i in range(2)]
                k = 0
                kd = 0
                for half in range(2):
                    for (g0, gn) in ((0, 4), (4, 4), (8, 3)):
                        ada_step()
                        fc0 = half * 11 + g0
                        sl_g, kg = ring_next()
                        vg = sl_g[:, 0:4096].rearrange("p (k f) -> p k f", k=8)
                        src = w_gu[l].rearrange("(k p) f -> p k f", p=128)
                        DMA("pool", vg[:, :, 0:gn * 128], src[:, :, fc0 * 128:(fc0 + gn) * 128], (), [kg], scoped=False)
                        sl_u, ku = ring_next()
                        vu = sl_u[:, 0:4096].rearrange("p (k f) -> p k f", k=8)
                        DMA("pool", vu[:, :, 0:gn * 128], src[:, :, DFF + fc0 * 128:DFF + (fc0 + gn) * 128], (), [ku], scoped=False)
                        for tb in range(NB):
                            for j in range(gn):
                                pg = k % 2
                                pu = 2 + k % 2
                                k += 1
                                for kc in range(KC):
                                    MM(ps[pg][:], vg[:, kc, j * 128:(j + 1) * 128], hT[:, kc, tbs(tb)], kc == 0, kc == KC - 1,
                                       [kg, ("h", kc, tb)], [("ps", pg)])
                                for kc in range(KC):
                                    MM(ps[pu][:], vu[:, kc, j * 128:(j + 1) * 128], hT[:, kc, tbs(tb)], kc == 0, kc == KC - 1,
                                       [ku, ("h", kc, tb)], [("ps", pu)])
                                s_ = sg[k % 2]
                                ACT(s_[:], ps[pg][:], AF.Silu, [("ps", pg)], [("sg", k % 2)])
                                TT("dve", actT[:, g0 + j, tbs(tb)], s_[:], ps[pu][:], ALU.mult,
                                   [("sg", k % 2), ("ps", pu)], [("act", g0 + j, tb)])
                    for s in range(4):
                        ada_step()
                        sl_d, kdk = ring_next()
                        vd = sl_d[:, 0:11 * 256].rearrange("p (k f) -> p k f", k=11)
                        srcd = w_dn[l][half * 11 * 128:(half + 1) * 11 * 128, :].rearrange("(k p) f -> p k f", p=128)
                        DMA("pool", vd, srcd[:, :, s * 256:(s + 1) * 256], (), [kdk], scoped=False)
                        for tb in range(NB):
                            for dd in range(2):
                                dm = 2 * s + dd
                                pd = 4 + kd % 2
                                kd += 1
                                for fl in range(11):
                                    MM(ps[pd][:], vd[:, fl, dd * 128:(dd + 1) * 128], actT[:, fl, tbs(tb)], fl == 0, fl == 10,
                                       [kdk, ("act", fl, tb)], [("ps", pd)])
                                STT("dve", xT[:, dm, tbs(tb)], ps[pd][:], modall[:, l, 40 + dm:41 + dm], xT[:, dm, tbs(tb)],
                                    ALU.mult, ALU.add, [("ps", pd), mk, ("x", dm, tb)], [("x", dm, tb)])
            S.barrier()

        def final_out():
            with contextlib.ExitStack() as ph:
                sq = [pl(ph, "fsq%d" % i, [128, KC, TB], BF16) for i in range(2)]
                rstd = [pl(ph, "frstd%d" % i, [128, TB], F32) for i in range(2)]
                ynT = [pl(ph, "ynT%d" % i, [128, KC, TB], F32) for i in range(2)]
                ytok = [pl(ph, "ytok%d" % i, [128, D], F32) for i in range(2)]
                kk = 0
                for tb in range(NB):
                    q = sq[tb % 2]
                    rs = rstd[tb % 2]
                    yn = ynT[tb % 2]
                    for c in range(KC):
                        ACT(q[:, c, :], xT[:, c, tbs(tb)], AF.Square, [("x", c, tb)], [("sq", tb % 2, c)])
                    for c in range(KC):
                        MM(ps[6][:], onesd[:], q[:, c, :], c == 0, c == KC - 1, [("sq", tb % 2, c), "onesd"], [("ps", 6)])
                    ACT(rs[:], ps[6][:], AF.Ln, [("ps", 6)], [("rstd", tb % 2)], bias=EPS)
                    ACT(rs[:], rs[:], AF.Exp, [("rstd", tb % 2)], [("rstd", tb % 2)], scale=-0.5)
                    for c in range(KC):
                        STT("dve", yn[:, c, :], xT[:, c, tbs(tb)], vecT[:, R_NF + c:R_NF + c + 1], rs[:], ALU.mult, ALU.mult,
                            [("x", c, tb), "vecT", ("rstd", tb % 2)], [("yn", tb % 2, c)])
                    for t4 in range(4):
                        t = tb * 4 + t4
                        yt = ytok[kk % 2]
                        for hf in range(2):
                            pb = (2 * kk + hf) % 4
                            for c4 in range(4):
                                c = hf * 4 + c4
                                TR(ps[pb][:, c4 * 128:(c4 + 1) * 128], yn[:, c, t4 * 128:(t4 + 1) * 128], idf[:],
                                   [("yn", tb % 2, c), "idf"], [("ps", pb)])
                            if hf == 0:
                                CP("dve", yt[:, 0:512], ps[pb][:], [("ps", pb)], [("ytok", kk % 2, 0)])
                            else:
                                CP("act", yt[:, 512:1024], ps[pb][:], [("ps", pb)], [("ytok", kk % 2, 1)])
                        DMA("sp", y_out[t * 128:(t + 1) * 128, :], yt[:], [("ytok", kk % 2, 0), ("ytok", kk % 2, 1)], [],
                            is_output=True)
                        kk += 1
            S.barrier()


        def attn(l):
            j = l // 2
            mk = ("mod", l)
            SC = 1.0 / (128.0 ** 0.5)
            with contextlib.ExitStack() as ph:
                kctxT = pl(ph, "kctxT", [128, 2, 512], BF16)
                vctx = pl(ph, "vctx", [128, 4, 256], BF16)
                kctok = pl(ph, "kctok", [128, 4, 256], BF16)
                qoT = pl(ph, "qoT", [128, 4, T], BF16)
                kT = pl(ph, "kT", [128, T], BF16)
                vtok = pl(ph, "vtok", [128, NT, 128], BF16)
                ropet = [pl(ph, "ropet%d" % i, [128, 256], F32) for i in range(2)]
                qkf = [pl(ph, "qkf%d" % i, [128, 768], F32) for i in range(2)]
                r1 = [pl(ph, "r1%d" % i, [128, 640], F32) for i in range(2)]
                r2 = [pl(ph, "r2%d" % i, [128, 640], F32) for i in range(2)]
                rb = [pl(ph, "rb%d" % i, [128, 640], BF16) for i in range(2)]
                pT = [pl(ph, "pT%d" % i, [128, 7, 512], BF16) for i in range(2)]
                mskt = [pl(ph, "mskt%d" % i, [128, 128], BF16) for i in range(2)]
                den2 = [pl(ph, "den%d" % i, [128, 512], F32) for i in range(2)]

                DMA("pool", kctok[:], ck_in[j].rearrange("(t p) f -> p t f", p=128), (), ["kctok"])
                DMA("pool", vctx[:], cv_in[j].rearrange("(t p) f -> p t f", p=128), (), ["vctx"])
                for kvh in range(2):
                    for t in range(4):
                        TR(psb[kvh][:, t * 128:(t + 1) * 128], kctok[:, t, kvh * 128:(kvh + 1) * 128], idb[:],
                           ["kctok", "idb"], [("ps", kvh)])
                    CP("dve", kctxT[:, kvh, :], psb[kvh][:, 0:512], [("ps", kvh)], [("kctx", kvh)])

                kk = 0
                mi = 0
                for g in range(2 if cfg.get("att_stop", 9) > 0 else 0):
                    slq, kq = ring_next()
                    vq = slq[:, 0:4096].rearrange("p (k f) -> p k f", k=8)
                    wsrc = w_qkv[j].rearrange("(k p) f -> p k f", p=128)
                    DMA("pool", vq, wsrc[:, :, g * 512:(g + 1) * 512], (), [kq], scoped=False)
                    slkv, kkv = ring_next()
                    vkv = slkv[:, 0:4096].rearrange("p (k f) -> p k f", k=8)
                    DMA("pool", vkv[:, :, 0:128], wsrc[:, :, 1024 + g * 128:1024 + (g + 1) * 128], (), [kkv], scoped=False)
                    DMA("pool", vkv[:, :, 128:256], wsrc[:, :, 1280 + g * 128:1280 + (g + 1) * 128], (), [kkv], scoped=False)
                    def stage1_b(t):
                        tsl = slice(t * 128, (t + 1) * 128)
                        b2 = t % 2
                        pb = 4 + b2
                        for hh in range(5):
                            TR(psb[pb][:, hh * 128:(hh + 1) * 128], rb[b2][:, hh * 128:(hh + 1) * 128], idb[:],
                               [("rb", b2), "idb"], [("ps", pb)])
                        CP("act", qoT[:, :, tsl], psb[pb][:, 0:512].rearrange("p (h t) -> p h t", h=4), [("ps", pb)], [("qo", t)])
                        CP("act", kT[:, tsl], psb[pb][:, 512:640], [("ps", pb)], [("kT", t)])

                    for t in range(NT):
                        tsl = slice(t * 128, (t + 1) * 128)
                        b2 = t % 2
                        pq = ps[b2]
                        pkv = ps[2 + b2]
                        for kc in range(KC):
                            MM(pq[:], hT[:, kc, tsl], vq[:, kc, :], kc == 0, kc == KC - 1, [("h", kc, t // 4), kq], [("ps", b2)])
                        for kc in range(KC):
                            MM(pkv[:, 0:256], hT[:, kc, tsl], vkv[:, kc, 0:256], kc == 0, kc == KC - 1,
                               [("h", kc, t // 4), kkv], [("ps", 2 + b2)])
                        DMA("sp", ropet[b2][:], rope_in[t], (), [("ropet", b2)])
                        CP("act", qkf[b2][:, 0:512], pq[:], [("ps", b2)], [("qkf", b2, 0)])
                        CP("act", qkf[b2][:, 512:768], pkv[:, 0:256], [("ps", 2 + b2)], [("qkf", b2, 1)])
                        CP("dve", vtok[:, t, :], qkf[b2][:, 640:768], [("qkf", b2, 1)], [("vt", t)])
                        DMA("sp", nk_out[j, tsl, g * 128:(g + 1) * 128], qkf[b2][:, 512:640], [("qkf", b2, 1)], [], is_output=True)
                        DMA("sp", nv_out[j, tsl, g * 128:(g + 1) * 128], qkf[b2][:, 640:768], [("qkf", b2, 1)], [], is_output=True)
                        x3 = qkf[b2][:, 0:640].rearrange("p (h d) -> p h d", h=5)
                        cosb = ropet[b2][:, 0:128].unsqueeze(1).to_broadcast([128, 5, 128])
                        TT("dve", r1[b2][:].rearrange("p (h d) -> p h d", h=5), x3, cosb, ALU.mult,
                           [("qkf", b2, 0), ("qkf", b2, 1), ("ropet", b2)], [("r1", b2)])
                        x5 = qkf[b2][:, 0:640].rearrange("p (h a s i) -> p h a s i", h=5, a=2, s=2)
                        o5 = r2[b2][:].rearrange("p (h a s i) -> p h a s i", h=5, a=2, s=2)
                        s4 = ropet[b2][:, 128:256].rearrange("p (a s i) -> p a s i", a=2, s=2)
                        for sidx in range(2):
                            sinb = s4[:, :, sidx, :].unsqueeze(1).to_broadcast([128, 5, 2, 32])
                            TT("dve", o5[:, :, :, sidx, :], x5[:, :, :, 1 - sidx, :], sinb, ALU.mult,
                               [("qkf", b2, 0), ("qkf", b2, 1), ("ropet", b2)], [("r2", b2, sidx)])
                        TT("dve", rb[b2][:], r1[b2][:], r2[b2][:], ALU.add, [("r1", b2), ("r2", b2, 0), ("r2", b2, 1)], [("rb", b2)])
                        if t > 0:
                            stage1_b(t - 1)
                    stage1_b(NT - 1)
                    def scores(n):
                        nonlocal kk, mi
                        nsl = slice(n * 128, (n + 1) * 128)
                        chunks = [("ctx", t) for t in range(4)]
                        if n > 0:
                            chunks.append(("prev", n - 1))
                        chunks.append(("cen", n))
                        if n < NT - 1:
                            chunks.append(("next", n + 1))
                        pt = pT[n % 2]
                        for ci, (kind, idx) in enumerate(chunks):
                            pS = 4 + kk % 4
                            kk += 1
                            if kind == "ctx":
                                lhs = kctxT[:, g, idx * 128:(idx + 1) * 128]
                                rk = [("kctx", g)]
                            else:
                                lhs = kT[:, idx * 128:(idx + 1) * 128]
                                rk = [("kT", idx)]
                            MM(ps[pS][:], lhs, qoT[:, :, nsl], True, True, rk + [("qo", n)], [("ps", pS)])
                            pk = ("pT", n % 2, ci)
                            if kind == "ctx":
                                ACT(pt[:, ci, :], ps[pS][:], AF.Exp, [("ps", pS), "pcs"], [pk], scale=SC,
                                    bias=pcs[:, PC_CTXB:PC_CTXB + 1])
                            else:
                                ACT(pt[:, ci, :], ps[pS][:], AF.Exp, [("ps", pS)], [pk], scale=SC)
                            if kind in ("prev", "next"):
                                side = 0 if kind == "prev" else 1
                                base = PC_MASK + (n * 2 + side) * 2
                                mt = mskt[mi % 2]
                                mkk = ("mskt", mi % 2)
                                mi += 1
                                TS("dve", mt[:], tri[:, side, :], pcs[:, base:base + 1], pcs[:, base + 1:base + 2],
                                   ALU.mult, ALU.add, ["tri", "pcs"], [mkk])
                                pv = pt[:, ci, :].rearrange("p (h t) -> p h t", h=4)
                                TT("dve", pv, pv, mt[:].unsqueeze(1).to_broadcast([128, 4, 128]), ALU.mult, [pk, mkk], [pk])
                            yield chunks

                    def pv_norm(n, chunks):
                        nsl = slice(n * 128, (n + 1) * 128)
                        pt = pT[n % 2]
                        nch = len(chunks)
                        bO = n % 2
                        bD = 2 + n % 2
                        dn = den2[n % 2]
                        dk = ("den", n % 2)
                        for ci, (kind, idx) in enumerate(chunks):
                            pk = ("pT", n % 2, ci)
                            if kind == "ctx":
                                vl = vctx[:, idx, g * 128:(g + 1) * 128]
                                rv = ["vctx"]
                            else:
                                vl = vtok[:, idx, :]
                                rv = [("vt", idx)]
                            MM(ps[bO][:], vl, pt[:, ci, :], ci == 0, ci == nch - 1, rv + [pk], [("ps", bO)])
                            MM(ps[bD][:], ones1[:], pt[:, ci, :], ci == 0, ci == nch - 1, ["ones1", pk], [("ps", bD)])
                            yield ci
                        d3 = dn[:].rearrange("p (h t) -> p h t", h=4)
                        es4 = esink[:, j * 8 + g * 4:j * 8 + g * 4 + 4].unsqueeze(2).to_broadcast([128, 4, 128])
                        TT("dve", d3, ps[bD][:].rearrange("p (h t) -> p h t", h=4), es4, ALU.add, [("ps", bD), "esink"], [dk])
                        ACT(dn[:], dn[:], AF.Ln, [dk], [dk])
                        ACT(dn[:], dn[:], AF.Exp, [dk], [dk], scale=-1.0)
                        TT("dve", qoT[:, :, nsl], ps[bO][:].rearrange("p (h t) -> p h t", h=4), d3, ALU.mult,
                           [("ps", bO), dk], [("qo", n)])

                    def chunk_list(n):
                        chunks = [("ctx", t) for t in range(4)]
                        if n > 0:
                            chunks.append(("prev", n - 1))
                        chunks.append(("cen", n))
                        if n < NT - 1:
                            chunks.append(("next", n + 1))
                        return chunks

                    def drain(gen):
                        for _ in gen:
                            pass

                    if cfg.get("att_stop", 9) > 1:
                        drain(scores(0))
                        for n in range(NT):
                            gs_ = scores(n + 1) if n + 1 < NT else None
                            gp_ = pv_norm(n, chunk_list(n))
                            while gs_ is not None or gp_ is not None:
                                if gs_ is not None:
                                    try:
                                        next(gs_)
                                    except StopIteration:
                                        gs_ = None
                                if gp_ is not None:
                                    try:
                                        next(gp_)
                                    except StopIteration:
                                        gp_ = None
                    slo, ko = ring_next()
                    vo = slo[:, 0:4096].rearrange("p (k f) -> p k f", k=4)
                    DMA("pool", vo, w_outb[j][g * 512:(g + 1) * 512, :].rearrange("(k p) f -> p k f", p=128), (), [ko], scoped=False)
                    ko_i = 0
                    for tb in range(NB if cfg.get("att_stop", 9) > 2 else 0):
                        for dm in range(KC):
                            pd = ko_i % 2
                            ko_i += 1
                            for hh in range(4):
                                MM(ps[pd][:], vo[:, hh, dm * 128:(dm + 1) * 128], qoT[:, hh, tbs(tb)], hh == 0, hh == 3,
                                   [ko] + [("qo", 4 * tb + q) for q in range(4)], [("ps", pd)])
                            STT("dve", xT[:, dm, tbs(tb)], ps[pd][:], modall[:, l, 16 + dm:17 + dm], xT[:, dm, tbs(tb)],
                                ALU.mult, ALU.add, [("ps", pd), mk, ("x", dm, tb)], [("x", dm, tb)])
            S.barrier()

        def rev_ap(t, n):
            return bass.AP(t, n - 1, [[n, 128], [-1, n]])

        def hgrn(l):
            j = l // 2
            mk = ("mod", l)
            with contextlib.ExitStack() as ph:
                ogT = pl(ph, "ogT", [128, 2, T], BF16)
                qd = [pl(ph, "qd%d" % i, [128, T], BF16) for i in range(2)]
                kdT = [pl(ph, "kdT%d" % i, [128, T], BF16) for i in range(2)]
                kdtok = [pl(ph, "kdtok%d" % i, [128, NT, 128], BF16) for i in range(2)]
                vtok = pl(ph, "hvtok", [128, NT, 128], BF16)
                oacc = pl(ph, "oacc", [128, T], F32)
                sig = [pl(ph, "sig%d" % i, [128, TB], F32) for i in range(2)]
                lf = [pl(ph, "lf%d" % i, [128, TB], F32) for i in range(2)]
                ebt = pl(ph, "ebt", [128, TB], F32)
                enbt = pl(ph, "enbt", [128, TB], F32)
                qs = pl(ph, "qs", [128, TB], F32)
                osq = pl(ph, "osq", [128, TB], BF16)
                segm = [pl(ph, "segm%d" % i, [128, TB], F32) for i in range(2)]
                Dor = [pl(ph, "Dor%d" % i, [128, 64], F32) for i in range(2)]
                Dca = [pl(ph, "Dca%d" % i, [128, 64], F32) for i in range(2)]
                Tst = [[pl(ph, "Tst%d_%d" % (d, i), [128, 128], F32) for i in range(2)] for d in range(2)]
                Sbf = [[pl(ph, "Sbf%d_%d" % (d, i), [128, 128], BF16) for i in range(2)] for d in range(2)]
                sfin = [pl(ph, "sfin%d" % i, [128, 128], F32) for i in range(2)]
                amask = [[pl(ph, "am%d_%d" % (d, i), [128, 128], BF16) for i in range(2)] for d in range(2)]
                vblk = [pl(ph, "vblk%d" % i, [128, 2, 4, 128], BF16) for i in range(2)]

                MEMSET("dve", segm[0][:], 1.0, ["segm0"])
                MEMSET("dve", segm[1][:], 1.0, ["segm1"])
                MEMSET("dve", segm[0][:].rearrange("p (c i) -> p c i", i=32)[:, :, 0:1], 0.0, ["segm0"])
                MEMSET("dve", segm[1][:].rearrange("p (c i) -> p c i", i=32)[:, :, 31:32], 0.0, ["segm1"])
                sf_i = 0
                for h in range(8):
                    slA, kA = ring_next()
                    vA = slA[:, 0:4096].rearrange("p (k f) -> p k f", k=8)
                    wsrc = w_ina[j].rearrange("(k p) f -> p k f", p=128)
                    for gi, c0 in enumerate((0, 1024, 2048, 4096)):
                        DMA("pool", vA[:, :, gi * 128:(gi + 1) * 128], wsrc[:, :, c0 + h * 128:c0 + (h + 1) * 128], (), [kA], scoped=False)
                    slB, kB = ring_next()
                    vB = slB[:, 0:4096].rearrange("p (k f) -> p k f", k=8)
                    DMA("pool", vB[:, :, 0:128], wsrc[:, :, 3072 + h * 128:3072 + (h + 1) * 128], (), [kB], scoped=False)
                    for d in range(2):
                        DMA("sp", Tst[d][1][:], st0_in[j, d, h], (), [("T", d, 1)])
                    def kd_tr(tb):
                        for d in range(2):
                            pb = 6 + d
                            for ti in range(4):
                                t = tb * 4 + ti
                                TR(psb[pb][:, ti * 128:(ti + 1) * 128], kdT[d][:, t * 128:(t + 1) * 128], idb[:],
                                   [("kdT", d, tb), "idb"], [("ps", pb)])
                            CP("dve", kdtok[d][:, tb * 4:(tb + 1) * 4, :], psb[pb][:, 0:512].rearrange("p (t e) -> p t e", t=4),
                               [("ps", pb)], [("kdt", d, tb * 4 + q) for q in range(4)])

                    for tb in range(NB):
                        for gi in range(3):
                            for kc in range(KC):
                                MM(ps[gi][:], vA[:, kc, gi * 128:(gi + 1) * 128], hT[:, kc, tbs(tb)], kc == 0, kc == KC - 1,
                                   [kA, ("h", kc, tb)], [("ps", gi)])
                        if tb > 0:
                            kd_tr(tb - 1)
                        ACT(qs[:], ps[0][:], AF.Sigmoid, [("ps", 0)], ["qs"])
                        for d in range(2):
                            ACT(sig[d][:], ps[1 + d][:], AF.Sigmoid, [("ps", 1 + d)], [("sig", d)])
                        TT("dve", qs[:], qs[:], ps[0][:], ALU.mult, ["qs", ("ps", 0)], ["qs"])
                        for ti in range(4):
                            t = tb * 4 + ti
                            for kc in range(KC):
                                MM(ps[3][:, ti * 128:(ti + 1) * 128], hT[:, kc, t * 128:(t + 1) * 128], vB[:, kc, 0:128],
                                   kc == 0, kc == KC - 1, [kB, ("h", kc, tb)], [("ps", 3)])
                        CP("dve", vtok[:, tb * 4:(tb + 1) * 4, :], ps[3][:].rearrange("p (t e) -> p t e", t=4), [("ps", 3)],
                           [("hvt", tb * 4 + q) for q in range(4)])
                        for d in range(2):
                            col = d * 8 + h
                            ACT(lf[d][:], sig[d][:], AF.Ln, [("sig", d), "lbt", "omlt"], [("lf", d)],
                                scale=omlt[:, j, col:col + 1], bias=lbt[:, j, col:col + 1])
                        for d in range(2):
                            if d == 0:
                                S.op("dve", lambda: nc.vector.tensor_tensor_scan(
                                    out=lf[0][:], data0=segm[0][:], data1=lf[0][:], initial=0.0, op0=ALU.mult, op1=ALU.add),
                                    [("lf", 0), "segm0"], [("lf", 0)])
                            else:
                                S.op("dve", lambda: nc.vector.tensor_tensor_scan(
                                    out=rev_ap(lf[1], TB), data0=rev_ap(segm[1], TB), data1=rev_ap(lf[1], TB), initial=0.0,
                                    op0=ALU.mult, op1=ALU.add), [("lf", 1), "segm1"], [("lf", 1)])
                        for d in range(2):
                            col = d * 8 + h
                            ACT(ebt[:], lf[d][:], AF.Exp, [("lf", d)], ["ebt"])
                            ACT(enbt[:], lf[d][:], AF.Exp, [("lf", d)], ["enbt"], scale=-1.0)
                            TS("dve", sig[d][:], sig[d][:], nomlt[:, j, col:col + 1], omlt[:, j, col:col + 1], ALU.mult, ALU.add,
                               [("sig", d), "omlt", "nomlt"], [("sig", d)])
                            TT("dve", qd[d][:, tbs(tb)], qs[:], ebt[:], ALU.mult, ["qs", "ebt"], [("qd", d, tb)])
                            TT("dve", kdT[d][:, tbs(tb)], sig[d][:], enbt[:], ALU.mult, [("sig", d), "enbt"], [("kdT", d, tb)])
                            e3 = ebt[:].rearrange("p (c i) -> p c i", i=32)
                            pos = 31 if d == 0 else 0
                            CP("dve", Dor[d][:, tb * 16:(tb + 1) * 16].unsqueeze(2), e3[:, :, pos:pos + 1], ["ebt"], [("Dor", d)])
                    kd_tr(NB - 1)
                    for d in range(2):
                        CP("dve", Dca[d][:], Dor[d][:], [("Dor", d)], [("Dca", d)])
                        pos = 7 if d == 0 else 0
                        dv = Dca[d][:].rearrange("p (s c) -> p s c", c=8)[:, :, pos:pos + 1]
                        TS("dve", dv, dv, pcs[:, PC_CARRY:PC_CARRY + 1], None, ALU.mult, None, [("Dca", d), "pcs"], [("Dca", d)])
                    for i in range(NT):
                        dirs = ((0, i), (1, NT - 1 - i))
                        for ti_, tl_ in enumerate((i, NT - 1 - i)):
                            TT("pool", vblk[i % 2][:, ti_, :, :], vtok[:, tl_, :].unsqueeze(1).to_broadcast([128, 4, 128]),
                               bmb[:].unsqueeze(2).to_broadcast([128, 4, 128]), ALU.mult,
                               [("hvt", tl_), "bmb"], [("vblk", i % 2, ti_)])
                        for d, tile in dirs:
                            tsl = slice(tile * 128, (tile + 1) * 128)
                            pU = d * 2 + i % 2
                            MM(ps[pU][:], kdtok[d][:, tile, :], vblk[i % 2][:, d, :, :], True, True,
                               [("kdt", d, tile), ("vblk", i % 2, d)], [("ps", pU)])
                            xb = 4 + d * 2 + i % 2
                            MM(ps[xb][:, 0:128], kdT[d][:, tsl], qd[d][:, tsl], True, True,
                               [("kdT", d, tile // 4), ("qd", d, tile // 4)], [("ps", xb)])
                        for d, tile in dirs:
                            xb = 4 + d * 2 + i % 2
                            TT("dve", amask[d][i % 2][:], ps[xb][:, 0:128], hmk[:, d, :], ALU.mult, [("ps", xb), "hmk"],
                               [("am", d, i % 2)])
                        for d, tile in dirs:
                            xb = 4 + d * 2 + i % 2
                            MM(ps[xb][:, 128:256], vtok[:, tile, :], amask[d][i % 2][:], True, False,
                               [("hvt", tile), ("am", d, i % 2)], [("ps", xb)])
                        for cc in range(4):
                            for d, tile in dirs:
                                pU = d * 2 + i % 2
                                xb = 4 + d * 2 + i % 2
                                ok = ("ps", xb)
                                pO = ps[xb][:, 128:256]
                                c = cc if d == 0 else 3 - cc
                                cidx = tile * 4 + c
                                n = cidx if d == 0 else 63 - cidx
                                pc = cidx - 1 if d == 0 else cidx + 1
                                told = Tst[d][(n - 1) % 2]
                                tnew = Tst[d][n % 2]
                                sbf = Sbf[d][n % 2]
                                kold = ("T", d, (n - 1) % 2)
                                knew = ("T", d, n % 2)
                                ksb = ("Sbf", d, n % 2)
                                if n == 0:
                                    CP("act", sbf[:], told[:], [kold], [ksb])
                                else:
                                    ACT(sbf[:], told[:], AF.Copy, [kold, ("Dca", d)], [ksb], scale=Dca[d][:, pc:pc + 1])
                                MM(pO[:, c * 32:(c + 1) * 32], sbf[:], qd[d][:, tile * 128 + c * 32:tile * 128 + (c + 1) * 32],
                                   False, cc == 3, [ksb, ("qd", d, tile // 4)], [ok])
                                uap = ps[pU][:, c * 128:(c + 1) * 128]
                                if n == 0:
                                    TT("dve", tnew[:], told[:], uap, ALU.add, [kold, ("ps", pU)], [knew])
                                else:
                                    STT("dve", tnew[:], told[:], Dca[d][:, pc:pc + 1], uap, ALU.mult, ALU.add,
                                        [kold, ("Dca", d), ("ps", pU)], [knew])
                                seg_end = (cidx % 8 == 7) if d == 0 else (cidx % 8 == 0)
                                if seg_end:
                                    m = cidx // 8
                                    sfb = sfin[sf_i % 2]
                                    ksf = ("sfin", sf_i % 2)
                                    sf_i += 1
                                    ACT(sfb[:], tnew[:], AF.Copy, [knew, ("Dor", d)], [ksf], scale=Dor[d][:, cidx:cidx + 1])
                                    DMA("sp", ns_out[j, d, m, h], sfb[:], [ksf], [], is_output=True)
                        for d, tile in dirs:
                            tsl = slice(tile * 128, (tile + 1) * 128)
                            xb = 4 + d * 2 + i % 2
                            pO = ps[xb][:, 128:256]
                            if i <= 7:
                                CP("dve", oacc[:, tsl], pO, [("ps", xb)], [("oacc", tile)])
                            else:
                                TT("dve", oacc[:, tsl], oacc[:, tsl], pO, ALU.add, [("ps", xb), ("oacc", tile)], [("oacc", tile)])
                    gcol = R_GN + j * 8 + h

                    def gbufs(tb):
                        if tb % 2 == 0:
                            return (ebt, "ebt", enbt, "enbt", qs, "qs", osq[:], "osq")
                        return (sig[0], ("sig", 0), sig[1], ("sig", 1), lf[0], ("lf", 0),
                                lf[1][:].bitcast(BF16)[:, 0:TB], ("lf", 1))

                    for pr in range(2):
                        tbl = (2 * pr, 2 * pr + 1)
                        for tb in tbl:
                            b_rs, k_rs, b_t1, k_t1, b_gs, k_gs, b_sq, k_sq = gbufs(tb)
                            ok4 = [("oacc", 4 * tb + q) for q in range(4)]
                            ACT(b_sq, oacc[:, tbs(tb)], AF.Square, ok4, [k_sq])
                        for tb in tbl:
                            b_rs, k_rs, b_t1, k_t1, b_gs, k_gs, b_sq, k_sq = gbufs(tb)
                            pA_ = (tb % 2) * 2
                            pG_ = (tb % 2) * 2 + 1
                            MM(ps[pA_][:], onese[:], b_sq, True, True, [k_sq, "onese"], [("ps", pA_)])
                            for kc in range(KC):
                                MM(ps[pG_][:], vA[:, kc, 384:512], hT[:, kc, tbs(tb)], kc == 0, kc == KC - 1, [kA, ("h", kc, tb)],
                                   [("ps", pG_)])
                        for tb in tbl:
                            b_rs, k_rs, b_t1, k_t1, b_gs, k_gs, b_sq, k_sq = gbufs(tb)
                            pA_ = (tb % 2) * 2
                            ACT(b_rs[:], ps[pA_][:], AF.Ln, [("ps", pA_)], [k_rs], bias=EPS)
                        for tb in tbl:
                            b_rs, k_rs, b_t1, k_t1, b_gs, k_gs, b_sq, k_sq = gbufs(tb)
                            ACT(b_rs[:], b_rs[:], AF.Exp, [k_rs], [k_rs], scale=-0.5)
                        for tb in tbl:
                            b_rs, k_rs, b_t1, k_t1, b_gs, k_gs, b_sq, k_sq = gbufs(tb)
                            pG_ = (tb % 2) * 2 + 1
                            ACT(b_gs[:], ps[pG_][:], AF.Sigmoid, [("ps", pG_)], [k_gs])
                        for tb in tbl:
                            b_rs, k_rs, b_t1, k_t1, b_gs, k_gs, b_sq, k_sq = gbufs(tb)
                            pG_ = (tb % 2) * 2 + 1
                            ok4 = [("oacc", 4 * tb + q) for q in range(4)]
                            STT("dve", b_t1[:], oacc[:, tbs(tb)], vecT[:, gcol:gcol + 1], b_rs[:], ALU.mult, ALU.mult,
                                ok4 + ["vecT", k_rs], [k_t1])
                            TT("dve", b_t1[:], b_t1[:], b_gs[:], ALU.mult, [k_t1, k_gs], [k_t1])
                            TT("dve", ogT[:, h % 2, tbs(tb)], b_t1[:], ps[pG_][:], ALU.mult, [k_t1, ("ps", pG_)], [("og", h % 2, tb)])
                    if h % 2 == 1:
                        slo, ko = ring_next()
                        vo = slo[:, 0:2048].rearrange("p (k f) -> p k f", k=2)
                        DMA("pool", vo, w_outa[j][(h - 1) * 128:(h + 1) * 128, :].rearrange("(k p) f -> p k f", p=128), (), [ko],
                            scoped=False)
                        ko_i = 0
                        for tb in range(NB):
                            for dm in range(KC):
                                pd = 4 + ko_i % 4
                                ko_i += 1
                                for hh in range(2):
                                    MM(ps[pd][:], vo[:, hh, dm * 128:(dm + 1) * 128], ogT[:, hh, tbs(tb)], hh == 0, hh == 1,
                                       [ko, ("og", hh, tb)], [("ps", pd)])
                                STT("dve", xT[:, dm, tbs(tb)], ps[pd][:], modall[:, l, 16 + dm:17 + dm], xT[:, dm, tbs(tb)],
                                    ALU.mult, ALU.add, [("ps", pd), mk, ("x", dm, tb)], [("x", dm, tb)])
            S.barrier()


        l0 = cfg.get("l0", 0)
        for l in range(l0, depth):
            if l == l0 or not cfg["ffn"]:
                adaln(l)
            if cfg["mixers"]:
                norm_mod(l, 0)
                if l % 2 == 0:
                    hgrn(l)
                else:
                    attn(l)
            if cfg["ffn"]:
                norm_mod(l, 1)
                if l + 1 < depth:
                    ada_next[0] = adaln_gen(l + 1)
                ffn(l)
                while ada_next[0] is not None:
                    ada_step()
        final_out()
        stats = S.emit()
        if cfg.get("dbg"):
            print("sched stats", stats)
    return nc

def _consts():
    cm = np.zeros((5, 128, 128), np.float32)
    cm[0] = np.eye(128, dtype=np.float32)
    j = np.arange(128)[:, None]
    i = np.arange(128)[None, :]
    cm[1] = (i <= j).astype(np.float32)
    cm[2] = (j <= i).astype(np.float32)
    same = (j // 32) == (i // 32)
    cm[3] = (same & (j <= i)).astype(np.float32)
    cm[4] = (same & (j >= i)).astype(np.float32)
    return cm


def _rope_tables(sample):
    tab = np.zeros((NT, 128, 256), np.float32)
    if not sample:
        tab[:, :, 0:128] = 1.0
        return tab
    t = np.arange(T, dtype=np.float32)
    row = np.floor(t / 64.0).astype(np.float32)
    col = (t - row * 64.0).astype(np.float32)
    inv = (10000.0 ** (-np.arange(32, dtype=np.float32) / 32.0)).astype(np.float32)
    ar = row[:, None] * inv[None, :]
    ac = col[:, None] * inv[None, :]
    cr, sr, cc, sc_ = np.cos(ar), np.sin(ar), np.cos(ac), np.sin(ac)
    cos = np.concatenate([cr, cr, cc, cc], axis=1)
    sin = np.concatenate([-sr, sr, -sc_, sc_], axis=1)
    tab[:, :, 0:128] = cos.reshape(NT, 128, 128)
    tab[:, :, 128:256] = sin.reshape(NT, 128, 128)
    return tab.astype(np.float32)


def _pcs(sample):
    p = np.zeros((N_PCS,), np.float32)
    p[PC_CARRY] = 1.0 if sample else 0.0
    p[PC_CTXB] = 0.0 if sample else -30000.0
    for n in range(NT):
        for side in range(2):
            base = PC_MASK + (n * 2 + side) * 2
            if sample:
                p[base], p[base + 1] = 1.0, 0.0
            else:
                valid = (side == 1 and n % 2 == 0) or (side == 0 and n % 2 == 1)
                p[base], p[base + 1] = 0.0, (1.0 if valid else 0.0)
    return np.ascontiguousarray(np.broadcast_to(p[None, :], (128, N_PCS))).astype(np.float32)


def kernel(x_prompt, x_sample, cache_k, cache_v, state_hgrn, c, c_ctx, w_ada, b_ada, norm1, norm2, norm_final,
           w_gate_up, w_down, w_in_a, lower_bounds, gnorm_a, w_out_a, w_qkv_b, w_out_b, sink_b):
    f = lambda a: np.ascontiguousarray(np.asarray(a), dtype=np.float32)
    x_prompt, x_sample, cache_k, cache_v, state_hgrn = map(f, (x_prompt, x_sample, cache_k, cache_v, state_hgrn))
    c, c_ctx, b_ada, norm1, norm2, norm_final = map(f, (c, c_ctx, b_ada, norm1, norm2, norm_final))
    lower_bounds, gnorm_a, sink_b = map(f, (lower_bounds, gnorm_a, sink_b))
    weights = dict(w_ada=f(w_ada), w_gate_up=f(w_gate_up), w_down=f(w_down), w_in_a=f(w_in_a),
                   w_out_a=f(w_out_a), w_qkv_b=f(w_qkv_b), w_out_b=f(w_out_b))
    vecs = np.zeros((N_VROWS, 128), np.float32)
    vecs[R_BADA:R_BADA + 192] = b_ada.reshape(192, 128)
    vecs[R_N1:R_N1 + 32] = norm1.reshape(32, 128)
    vecs[R_N2:R_N2 + 32] = norm2.reshape(32, 128)
    vecs[R_NF:R_NF + 8] = norm_final.reshape(8, 128)
    vecs[R_GN:R_GN + 16] = gnorm_a.reshape(16, 128)
    vecs[R_LB:R_LB + 32] = lower_bounds.reshape(32, 128)
    cm = _consts()
    bm = (np.arange(128)[:, None] // 32 == np.arange(4)[None, :]).astype(np.float32)
    sinkb = np.ascontiguousarray(np.broadcast_to(sink_b.reshape(1, 16), (128, 16))).astype(np.float32)
    rope_s, rope_p = _rope_tables(True), _rope_tables(False)
    pcs_s, pcs_p = _pcs(True), _pcs(False)
    zck = np.zeros((2, 512, 256), np.float32)
    zst = np.zeros((2, 2, 8, 128, 128), np.float32)
    in_maps = []
    for core in range(8):
        sample = core < 4
        if sample:
            b = core
            m = dict(x=x_sample[b], cond=np.ascontiguousarray(c[b].reshape(8, 128).T),
                     ck=np.ascontiguousarray(cache_k[b].reshape(2, 512, 256)),
                     cv=np.ascontiguousarray(cache_v[b].reshape(2, 512, 256)),
                     st0=np.ascontiguousarray(state_hgrn[b]), rope=rope_s, pcs=pcs_s)
        else:
            s0 = (core - 4) * 8
            m = dict(x=np.ascontiguousarray(x_prompt[s0:s0 + 8].reshape(T, D)),
                     cond=np.ascontiguousarray(c_ctx.reshape(8, 128).T), ck=zck, cv=zck, st0=zst,
                     rope=rope_p, pcs=pcs_p)
        m.update(vecs=vecs, cmat=cm, sinkb=sinkb, bm=bm)
        m.update(weights)
        in_maps.append(m)
    nc = build_nc(CFG)
    res = run_bass_kernel_spmd(nc, in_maps, core_ids=list(range(8)))
    R = res.results
    y_prompt = np.zeros((32, 256, D), np.float32)
    y_sample = np.zeros((4, T, D), np.float32)
    nk = np.zeros((32, 2, 256, 2, 128), np.float32)
    nv = np.zeros((32, 2, 256, 2, 128), np.float32)
    nst = np.zeros((32, 2, 2, 8, 128, 128), np.float32)
    for core in range(8):
        r = R[core]
        if core < 4:
            y_sample[core] = r["y"]
        else:
            s0 = (core - 4) * 8
            y_prompt[s0:s0 + 8] = r["y"].reshape(8, 256, D)
            nk[s0:s0 + 8] = r["nk"].reshape(2, 8, 256, 2, 128).transpose(1, 0, 2, 3, 4)
            nv[s0:s0 + 8] = r["nv"].reshape(2, 8, 256, 2, 128).transpose(1, 0, 2, 3, 4)
            nst[s0:s0 + 8] = r["ns"].transpose(2, 0, 1, 3, 4, 5)
    return (y_prompt, y_sample, nk, nv, nst)
```

```python
import numpy as np
import contextlib
import concourse.bass as bass
import concourse.mybir as mybir
from concourse.bass_utils import run_bass_kernel_spmd
F32 = mybir.dt.float32
BF16 = mybir.dt.bfloat16
AF = mybir.ActivationFunctionType
ALU = mybir.AluOpType
AX = mybir.AxisListType


COMPUTE = ("pe", "act", "dve", "pool")


class Op:
    __slots__ = ("eng", "fn", "reads", "writes", "deps", "is_dma", "marked",
                 "token", "idx", "eidx", "pre_waits")

    def __init__(self, eng, fn, reads, writes, is_dma):
        self.eng = eng
        self.fn = fn
        self.reads = reads
        self.writes = writes
        self.is_dma = is_dma
        self.deps = []
        self.marked = False
        self.token = None
        self.pre_waits = []


class Sched:
    def __init__(self, nc, es, n_dma_sems=8):
        self.nc = nc
        self.engs = {"pe": nc.tensor, "act": nc.scalar, "dve": nc.vector,
                     "pool": nc.gpsimd, "sp": nc.sync}
        self.sems = {e: es.enter_context(nc.semaphore("s_" + e)) for e in COMPUTE}
        self.dma_sems = {}
        for q in ("sp", "pool", "act"):
            self.dma_sems[q] = [es.enter_context(nc.semaphore("d_%s%d" % (q, i)))
                                for i in range(n_dma_sems)]
        self.ops = []
        self.last_w = {}
        self.readers = {}
        self.out_dma_ops = []
        self.last_on_eng = {}
        self.dma_since = []
        self.bar = []

    def barrier(self):
        self.bar = sorted(set(list(self.last_on_eng.values()) + self.dma_since))
        self.dma_since = []

    def _keys(self, ks):
        out = []
        for k in ks:
            if isinstance(k, list):
                out.extend(k)
            else:
                out.append(k)
        return out

    def op(self, eng, fn, reads=(), writes=()):
        rk = self._keys(reads)
        wk = self._keys(writes)
        wk = wk + [k for k in rk if isinstance(k, tuple) and k[0] == "ps"]
        rk = [k for k in rk if not (isinstance(k, tuple) and k[0] == "ps")]
        o = Op(eng, fn, rk, wk, False)
        self._add(o, True)
        self.last_on_eng[eng] = o.idx
        return o

    def dma(self, queue, fn, reads=(), writes=(), is_output=False, scoped=True):
        o = Op(queue, fn, self._keys(reads), self._keys(writes), True)
        self._add(o, scoped)
        if scoped:
            self.dma_since.append(o.idx)
        if is_output:
            self.out_dma_ops.append(o)
        return o

    def _add(self, o, scoped=True):
        o.idx = len(self.ops)
        deps = set(self.bar) if scoped else set()
        for k in o.reads:
            w = self.last_w.get(k)
            if w is not None:
                deps.add(w)
        for k in o.writes:
            w = self.last_w.get(k)
            if w is not None:
                deps.add(w)
            for r in self.readers.get(k, ()):
                deps.add(r)
        deps.discard(o.idx)
        o.deps = sorted(deps)
        for k in o.reads:
            self.readers.setdefault(k, []).append(o.idx)
        for k in o.writes:
            self.last_w[k] = o.idx
            self.readers[k] = []
        self.ops.append(o)

    def emit(self):
        ops = self.ops
        eng_pos = {}
        per_eng_count = {}
        for o in ops:
            o.eidx = per_eng_count.get(o.eng, 0)
            per_eng_count[o.eng] = o.eidx + 1
        for o in ops:
            for d in o.deps:
                p = ops[d]
                if p.is_dma:
                    continue
                if p.eng == o.eng and not o.is_dma:
                    if p.eng == "pe":
                        continue
                p.marked = True
        cnt = {e: 0 for e in COMPUTE}
        dma_rr = {q: 0 for q in self.dma_sems}
        dma_val = {}
        for q, lst in self.dma_sems.items():
            for i in range(len(lst)):
                dma_val[(q, i)] = 0
        for o in ops:
            if o.is_dma:
                q = o.eng
                i = dma_rr[q]
                dma_rr[q] = (i + 1) % len(self.dma_sems[q])
                prev = dma_val[(q, i)]
                if prev > 0:
                    o.pre_waits.append((("dma", q, i), prev))
                dma_val[(q, i)] = prev + 16
                o.token = (("dma", q, i), prev + 16)
            elif o.marked:
                cnt[o.eng] += 1
                o.token = (("eng", o.eng), cnt[o.eng])
        clock = {e: {} for e in self.engs}
        vc = [None] * len(ops)

        def semh(s):
            if s[0] == "eng":
                return self.sems[s[1]]
            return self.dma_sems[s[1]][s[2]]

        def merge(a, b):
            for k, v in b.items():
                if a.get(k, 0) < v:
                    a[k] = v

        nwaits = 0
        for o in ops:
            e = o.eng
            ck = clock[e]
            need = {}
            for s, v in o.pre_waits:
                if ck.get(s, 0) < v:
                    need[s] = max(need.get(s, 0), v)
            for d in o.deps:
                p = ops[d]
                if p.token is None:
                    continue
                if (not p.is_dma) and p.eng == e and not o.is_dma:
                    if e == "pe":
                        continue
                s, v = p.token
                if ck.get(s, 0) >= v:
                    continue
                need[s] = max(need.get(s, 0), v)
                merge(ck, vc[d])
            engh = self.engs[e]
            for s, v in need.items():
                engh.wait_ge(semh(s), v)
                nwaits += 1
                if ck.get(s, 0) < v:
                    ck[s] = v
            inst = o.fn()
            if o.token is not None:
                s, v = o.token
                if o.is_dma:
                    inst.then_inc(semh(s), 16)
                else:
                    inst.then_inc(semh(s), 1)
                snap = dict(ck)
                snap[s] = v
                vc[o.idx] = snap
        sp = self.engs["sp"]
        ck = clock["sp"]
        need = {}
        for o in self.out_dma_ops:
            s, v = o.token
            need[s] = max(need.get(s, 0), v)
        for (q, i), v in dma_val.items():
            if v > 0:
                s = ("dma", q, i)
                need[s] = max(need.get(s, 0), v)
        for s, v in need.items():
            sp.wait_ge(semh(s), v)
        return dict(n_ops=len(ops), n_waits=nwaits, counts=per_eng_count, sem_counts=cnt)

T = 2048
D = 1024
KC = 8
NT = 16
NB = 4
TB = 512
DFF = 2816
EPS = 1e-6
CFG = dict(depth=4, mixers=True, ffn=True, dbg=False, l0=0)

R_BADA = 0
R_N1 = 192
R_N2 = 224
R_NF = 256
R_GN = 264
R_LB = 280
N_VROWS = 384
PC_CARRY = 0
PC_CTXB = 1
PC_MASK = 2
N_PCS = 2 + 64


def build_nc(cfg):
    nc = bass.Bass("TRN2", target_bir_lowering=False)
    depth = cfg["depth"]

    def din(name, shape, dt=F32):
        return nc.dram_tensor(name, list(shape), dt, kind="ExternalInput").ap()

    def dout(name, shape, dt=F32):
        return nc.dram_tensor(name, list(shape), dt, kind="ExternalOutput").ap()

    x_in = din("x", [T, D])
    cond_in = din("cond", [128, KC])
    vecs_in = din("vecs", [N_VROWS, 128])
    cmat_in = din("cmat", [5, 128, 128])
    pcs_in = din("pcs", [128, N_PCS])
    rope_in = din("rope", [NT, 128, 256])
    ck_in = din("ck", [2, 512, 256])
    cv_in = din("cv", [2, 512, 256])
    st0_in = din("st0", [2, 2, 8, 128, 128])
    sink_in = din("sinkb", [128, 16])
    bm_in = din("bm", [128, 4])
    w_ada = din("w_ada", [4, D, 6 * D])
    w_gu = din("w_gate_up", [4, D, 2 * DFF])
    w_dn = din("w_down", [4, DFF, D])
    w_ina = din("w_in_a", [2, D, 5 * D])
    w_outa = din("w_out_a", [2, D, D])
    w_qkv = din("w_qkv_b", [2, D, 1536])
    w_outb = din("w_out_b", [2, D, D])
    y_out = dout("y", [T, D])
    nk_out = dout("nk", [2, T, 256])
    nv_out = dout("nv", [2, T, 256])
    ns_out = dout("ns", [2, 2, 8, 8, 128, 128])

    with contextlib.ExitStack() as es:
        S = Sched(nc, es)

        def sb(name, shape, dt):
            return es.enter_context(nc.sbuf_tensor("sb_" + name, list(shape), dt))

        uniq = [0]

        def pl(ph, name, shape, dt):
            uniq[0] += 1
            return ph.enter_context(nc.sbuf_tensor("pl%d_%s" % (uniq[0], name), list(shape), dt))

        xT = sb("xT", [128, KC, T], F32)
        hT = sb("hT", [128, KC, T], BF16)
        ring = [sb("ring%d" % i, [128, 4096], BF16) for i in range(4)]
        vecT = sb("vecT", [128, N_VROWS], F32)
        idf = sb("idf", [128, 128], F32)
        idb = sb("idb", [128, 128], BF16)
        tri = sb("tri", [128, 2, 128], BF16)
        hmk = sb("hmk", [128, 2, 128], BF16)
        ones1 = sb("ones1", [128, 128], BF16)
        onesd = sb("onesd", [128, 128], BF16)
        onese = sb("onese", [128, 128], BF16)
        pcs = sb("pcs", [128, N_PCS], F32)
        bmb = sb("bmb", [128, 4], BF16)
        esink = sb("esink", [128, 16], F32)
        scb = sb("scb", [128, KC], BF16)
        condf = sb("condf", [128, KC], F32)
        modall = sb("modall", [128, 4, 64], F32)
        lbt = sb("lbt", [128, 2, 16], F32)
        omlt = sb("omlt", [128, 2, 16], F32)
        nomlt = sb("nomlt", [128, 2, 16], F32)
        ps = [es.enter_context(nc.psum_tensor("ps%d" % i, [128, 512], F32)) for i in range(8)]
        psb = [p[:].bitcast(BF16) for p in ps]

        ring_i = [0]

        def ring_next():
            i = ring_i[0]
            ring_i[0] = (i + 1) % 4
            return ring[i], ("ring", i)

        def MM(out, lhsT, rhs, start, stop, r, w, **kw):
            S.op("pe", lambda: nc.tensor.matmul(out, lhsT, rhs, start=start, stop=stop, **kw), r, w)

        def TR(out, in_, ident, r, w):
            S.op("pe", lambda: nc.tensor.transpose(out, in_, ident), r, w)

        def ACT(out, in_, func, r, w, bias=None, scale=None):
            kw = {}
            if bias is not None:
                kw["bias"] = bias
            if scale is not None:
                kw["scale"] = scale
            S.op("act", lambda: nc.scalar.activation(out=out, in_=in_, func=func, **kw), r, w)

        def ENG(e):
            return {"dve": nc.vector, "pool": nc.gpsimd}[e]

        def TT(e, out, in0, in1, op, r, w):
            S.op(e, lambda: ENG(e).tensor_tensor(out=out, in0=in0, in1=in1, op=op), r, w)

        def TS(e, out, in0, s1, s2, op0, op1, r, w):
            if s2 is None:
                S.op(e, lambda: ENG(e).tensor_scalar(out=out, in0=in0, scalar1=s1, scalar2=None, op0=op0), r, w)
            else:
                S.op(e, lambda: ENG(e).tensor_scalar(out=out, in0=in0, scalar1=s1, scalar2=s2, op0=op0, op1=op1), r, w)

        def STT(e, out, in0, scalar, in1, op0, op1, r, w):
            S.op(e, lambda: ENG(e).scalar_tensor_tensor(out=out, in0=in0, scalar=scalar, in1=in1, op0=op0, op1=op1), r, w)

        def CP(e, out, in_, r, w):
            if e == "act":
                S.op("act", lambda: nc.scalar.copy(out=out, in_=in_), r, w)
            else:
                S.op(e, lambda: ENG(e).tensor_copy(out=out, in_=in_), r, w)

        def MEMSET(e, out, val, w):
            S.op(e, lambda: ENG(e).memset(out, val), (), w)

        def DMA(q, out, in_, r, w, is_output=False, scoped=True):
            eng = {"sp": nc.sync, "pool": nc.gpsimd, "act": nc.scalar}[q]
            S.dma(q, lambda: eng.dma_start(out=out, in_=in_), r, w, is_output=is_output, scoped=scoped)

        def tbs(tb):
            return slice(tb * TB, (tb + 1) * TB)

        with contextlib.ExitStack() as ph:
            vtmp = pl(ph, "vtmp", [128, 3, 128], F32)
            DMA("sp", idf[:], cmat_in[0], (), ["idf"])
            DMA("pool", idb[:], cmat_in[0], (), ["idb"])
            DMA("pool", tri[:, 0, :], cmat_in[1], (), ["tri"])
            DMA("pool", tri[:, 1, :], cmat_in[2], (), ["tri"])
            DMA("pool", hmk[:, 0, :], cmat_in[3], (), ["hmk"])
            DMA("pool", hmk[:, 1, :], cmat_in[4], (), ["hmk"])
            DMA("sp", pcs[:], pcs_in, (), ["pcs"])
            DMA("pool", bmb[:], bm_in, (), ["bmb"])
            DMA("sp", esink[:], sink_in, (), ["esink"])
            DMA("sp", condf[:], cond_in, (), ["condf"])
            for i in range(3):
                DMA("sp", vtmp[:, i, :], vecs_in[i * 128:(i + 1) * 128, :], (), [("vtmp", i)])
            MEMSET("dve", ones1[:], 1.0, ["ones1"])
            MEMSET("dve", onesd[:], 1.0 / D, ["onesd"])
            MEMSET("dve", onese[:], 1.0 / 128, ["onese"])
            for i in range(3):
                TR(ps[0][:, i * 128:(i + 1) * 128], vtmp[:, i, :], idf[:], [("vtmp", i), "idf"], [("ps", 0)])
            CP("dve", vecT[:], ps[0][:, 0:N_VROWS], [("ps", 0)], ["vecT"])
            ACT(esink[:], esink[:], AF.Exp, ["esink"], ["esink"])
            ACT(scb[:], condf[:], AF.Silu, ["condf"], ["scb"])
            MEMSET("dve", lbt[:, 0, :], 0.0, ["lbt"])
            TT("dve", lbt[:, 1, :], vecT[:, R_LB + 16:R_LB + 32], vecT[:, R_LB:R_LB + 16], ALU.subtract, ["vecT", "lbt"], ["lbt"])
            ACT(lbt[:, 1, :], lbt[:, 1, :], AF.Sigmoid, ["lbt"], ["lbt"])
            TS("dve", omlt[:], lbt[:], -1.0, 1.0, ALU.mult, ALU.add, ["lbt"], ["omlt"])
            TS("dve", nomlt[:], omlt[:], -1.0, None, ALU.mult, None, ["omlt"], ["nomlt"])

            xtok = [pl(ph, "xtok%d" % i, [128, D], F32) for i in range(2)]
            for t in range(NT):
                xt = xtok[t % 2]
                DMA("sp", xt[:], x_in[t * 128:(t + 1) * 128, :], (), [("xtok", t % 2)])
                for hf in range(2):
                    pb = (2 * t + hf) % 4
                    for c4 in range(4):
                        c = hf * 4 + c4
                        TR(ps[pb][:, c4 * 128:(c4 + 1) * 128], xt[:, c * 128:(c + 1) * 128], idf[:],
                           [("xtok", t % 2), "idf"], [("ps", pb)])
                    dst = xT[:, hf * 4:(hf + 1) * 4, t * 128:(t + 1) * 128]
                    src = ps[pb][:].rearrange("p (c t) -> p c t", c=4)
                    wk = [("x", hf * 4 + c4, t // 4) for c4 in range(4)]
                    if hf == 0:
                        CP("dve", dst, src, [("ps", pb)], wk)
                    else:
                        CP("act", dst, src, [("ps", pb)], wk)
        S.barrier()

        def adaln_gen(l):
            for s in range(12):
                slot, sk = ring_next()
                sv = slot[:, 0:4096].rearrange("p (k f) -> p k f", k=8)
                DMA("pool", sv, w_ada[l].rearrange("(k p) f -> p k f", p=128)[:, :, s * 512:(s + 1) * 512],
                    (), [sk], scoped=False)
                for f in range(4):
                    col = s * 4 + f
                    for kc in range(KC):
                        MM(ps[7][:, col:col + 1], sv[:, kc, f * 128:(f + 1) * 128], scb[:, kc:kc + 1],
                           kc == 0, kc == KC - 1, [sk, "scb"], [("ps", 7)])
                yield s
            mk = ("mod", l)
            TT("dve", modall[:, l, 0:48], ps[7][:, 0:48], vecT[:, R_BADA + l * 48:R_BADA + (l + 1) * 48], ALU.add,
               [("ps", 7), "vecT"], [mk])
            STT("dve", modall[:, l, 48:56], modall[:, l, 8:16], 1.0, vecT[:, R_N1 + l * 8:R_N1 + (l + 1) * 8],
                ALU.add, ALU.mult, [mk, "vecT"], [mk])
            STT("dve", modall[:, l, 56:64], modall[:, l, 32:40], 1.0, vecT[:, R_N2 + l * 8:R_N2 + (l + 1) * 8],
                ALU.add, ALU.mult, [mk, "vecT"], [mk])
            yield 12

        def adaln(l):
            for _ in adaln_gen(l):
                pass

        ada_next = [None]

        def ada_step():
            g = ada_next[0]
            if g is not None:
                try:
                    next(g)
                except StopIteration:
                    ada_next[0] = None

        def norm_mod(l, which, ext=None):
            mk = ("mod", l)
            Acol = 48 if which == 0 else 56
            Bcol = 0 if which == 0 else 24
            with contextlib.ExitStack() as ph_own:
                ph = ext if ext is not None else ph_own
                sq = [pl(ph, "sq%d" % i, [128, KC, TB], BF16) for i in range(2)]
                rstd = [pl(ph, "rstd%d" % i, [128, TB], F32) for i in range(2)]
                tmpn = [pl(ph, "tmpn%d" % i, [128, TB], F32) for i in range(2)]
                for tb in range(NB):
                    q = sq[tb % 2]
                    rs = rstd[tb % 2]
                    for c in range(KC):
                        if c % 2 == 0:
                            ACT(q[:, c, :], xT[:, c, tbs(tb)], AF.Square, [("x", c, tb)], [("sq", tb % 2, c)])
                        else:
                            TT("dve", q[:, c, :], xT[:, c, tbs(tb)], xT[:, c, tbs(tb)], ALU.mult, [("x", c, tb)], [("sq", tb % 2, c)])
                    for c in range(KC):
                        MM(ps[6][:], onesd[:], q[:, c, :], c == 0, c == KC - 1, [("sq", tb % 2, c), "onesd"], [("ps", 6)])
                    ACT(rs[:], ps[6][:], AF.Ln, [("ps", 6)], [("rstd", tb % 2)], bias=EPS)
                    ACT(rs[:], rs[:], AF.Exp, [("rstd", tb % 2)], [("rstd", tb % 2)], scale=-0.5)
                    for c in range(KC):
                        tm = tmpn[c % 2]
                        STT("dve", tm[:], xT[:, c, tbs(tb)], modall[:, l, Acol + c:Acol + c + 1], rs[:], ALU.mult, ALU.mult,
                            [("x", c, tb), mk, ("rstd", tb % 2)], [("tmpn", c % 2)])
                        ACT(hT[:, c, tbs(tb)], tm[:], AF.Identity, [("tmpn", c % 2), mk], [("h", c, tb)],
                            bias=modall[:, l, Bcol + c:Bcol + c + 1])
            if ext is None:
                S.barrier()

        def ffn(l):
            mk = ("mod", l)
            with contextlib.ExitStack() as ph:
                norm_mod(l, 1, ext=ph)
                actT = pl(ph, "actT", [128, 11, T], BF16)
                sg = [pl(ph, "sg%d" % i, [128, TB], F32) for i in range(2)]
                k = 0
                kd = 0
                for half in range(2):
                    for (g0, gn) in ((0, 4), (4, 4), (8, 3)):
                        ada_step()
                        fc0 = half * 11 + g0
                        sl_g, kg = ring_next()
                        vg = sl_g[:, 0:4096].rearrange("p (k f) -> p k f", k=8)
                        src = w_gu[l].rearrange("(k p) f -> p k f", p=128)
                        DMA("pool", vg[:, :, 0:gn * 128], src[:, :, fc0 * 128:(fc0 + gn) * 128], (), [kg], scoped=False)
                        sl_u, ku = ring_next()
                        vu = sl_u[:, 0:4096].rearrange("p (k f) -> p k f", k=8)
                        DMA("pool", vu[:, :, 0:gn * 128], src[:, :, DFF + fc0 * 128:DFF + (fc0 + gn) * 128], (), [ku], scoped=False)
                        for tb in range(NB):
                            for j in range(gn):
                                pg = k % 2
                                pu = 2 + k % 2
                                k += 1
                                for kc in range(KC):
                                    MM(ps[pg][:], vg[:, kc, j * 128:(j + 1) * 128], hT[:, kc, tbs(tb)], kc == 0, kc == KC - 1,
                                       [kg, ("h", kc, tb)], [("ps", pg)])
                                for kc in range(KC):
                                    MM(ps[pu][:], vu[:, kc, j * 128:(j + 1) * 128], hT[:, kc, tbs(tb)], kc == 0, kc == KC - 1,
                                       [ku, ("h", kc, tb)], [("ps", pu)])
                                s_ = sg[k % 2]
                                ACT(s_[:], ps[pg][:], AF.Silu, [("ps", pg)], [("sg", k % 2)])
                                TT("dve", actT[:, g0 + j, tbs(tb)], s_[:], ps[pu][:], ALU.mult,
                                   [("sg", k % 2), ("ps", pu)], [("act", g0 + j, tb)])
                    for s in range(4):
                        ada_step()
                        sl_d, kdk = ring_next()
                        vd = sl_d[:, 0:11 * 256].rearrange("p (k f) -> p k f", k=11)
                        srcd = w_dn[l][half * 11 * 128:(half + 1) * 11 * 128, :].rearrange("(k p) f -> p k f", p=128)
                        DMA("pool", vd, srcd[:, :, s * 256:(s + 1) * 256], (), [kdk], scoped=False)
                        for tb in range(NB):
                            for dd in range(2):
                                dm = 2 * s + dd
                                pd = 4 + kd % 2
                                kd += 1
                                for fl in range(11):
                                    MM(ps[pd][:], vd[:, fl, dd * 128:(dd + 1) * 128], actT[:, fl, tbs(tb)], fl == 0, fl == 10,
                                       [kdk, ("act", fl, tb)], [("ps", pd)])
                                STT("dve", xT[:, dm, tbs(tb)], ps[pd][:], modall[:, l, 40 + dm:41 + dm], xT[:, dm, tbs(tb)],
                                    ALU.mult, ALU.add, [("ps", pd), mk, ("x", dm, tb)], [("x", dm, tb)])
            S.barrier()

        def final_out():
            with contextlib.ExitStack() as ph:
                sq = [pl(ph, "fsq%d" % i, [128, KC, TB], BF16) for i in range(2)]
                rstd = [pl(ph, "frstd%d" % i, [128, TB], F32) for i in range(2)]
                ynT = [pl(ph, "ynT%d" % i, [128, KC, TB], F32) for i in range(2)]
                ytok = [pl(ph, "ytok%d" % i, [128, D], F32) for i in range(2)]
                kk = 0
                for tb in range(NB):
                    q = sq[tb % 2]
                    rs = rstd[tb % 2]
                    yn = ynT[tb % 2]
                    for c in range(KC):
                        if c % 2 == 0:
                            ACT(q[:, c, :], xT[:, c, tbs(tb)], AF.Square, [("x", c, tb)], [("sq", tb % 2, c)])
                        else:
                            TT("dve", q[:, c, :], xT[:, c, tbs(tb)], xT[:, c, tbs(tb)], ALU.mult, [("x", c, tb)], [("sq", tb % 2, c)])
                    for c in range(KC):
                        MM(ps[6][:], onesd[:], q[:, c, :], c == 0, c == KC - 1, [("sq", tb % 2, c), "onesd"], [("ps", 6)])
                    ACT(rs[:], ps[6][:], AF.Ln, [("ps", 6)], [("rstd", tb % 2)], bias=EPS)
                    ACT(rs[:], rs[:], AF.Exp, [("rstd", tb % 2)], [("rstd", tb % 2)], scale=-0.5)
                    for c in range(KC):
                        STT("dve", yn[:, c, :], xT[:, c, tbs(tb)], vecT[:, R_NF + c:R_NF + c + 1], rs[:], ALU.mult, ALU.mult,
                            [("x", c, tb), "vecT", ("rstd", tb % 2)], [("yn", tb % 2, c)])
                    for t4 in range(4):
                        t = tb * 4 + t4
                        yt = ytok[kk % 2]
                        for hf in range(2):
                            pb = (2 * kk + hf) % 4
                            for c4 in range(4):
                                c = hf * 4 + c4
                                TR(ps[pb][:, c4 * 128:(c4 + 1) * 128], yn[:, c, t4 * 128:(t4 + 1) * 128], idf[:],
                                   [("yn", tb % 2, c), "idf"], [("ps", pb)])
                            if hf == 0:
                                CP("dve", yt[:, 0:512], ps[pb][:], [("ps", pb)], [("ytok", kk % 2, 0)])
                            else:
                                CP("act", yt[:, 512:1024], ps[pb][:], [("ps", pb)], [("ytok", kk % 2, 1)])
                        DMA("sp", y_out[t * 128:(t + 1) * 128, :], yt[:], [("ytok", kk % 2, 0), ("ytok", kk % 2, 1)], [],
                            is_output=True)
                        kk += 1
            S.barrier()


        def attn(l):
            j = l // 2
            mk = ("mod", l)
            SC = 1.0 / (128.0 ** 0.5)
            with contextlib.ExitStack() as ph:
                kctxT = pl(ph, "kctxT", [128, 2, 512], BF16)
                vctx = pl(ph, "vctx", [128, 4, 256], BF16)
                kctok = pl(ph, "kctok", [128, 4, 256], BF16)
                qoT = pl(ph, "qoT", [128, 4, T], BF16)
                kT = pl(ph, "kT", [128, T], BF16)
                vtok = pl(ph, "vtok", [128, NT, 128], BF16)
                ropet = [pl(ph, "ropet%d" % i, [128, 256], F32) for i in range(2)]
                qkf = [pl(ph, "qkf%d" % i, [128, 768], F32) for i in range(2)]
                r1 = [pl(ph, "r1%d" % i, [128, 640], F32) for i in range(2)]
                r2 = [pl(ph, "r2%d" % i, [128, 640], F32) for i in range(2)]
                rb = [pl(ph, "rb%d" % i, [128, 640], BF16) for i in range(2)]
                pT = [pl(ph, "pT%d" % i, [128, 7, 512], BF16) for i in range(2)]
                mskt = [pl(ph, "mskt%d" % i, [128, 128], BF16) for i in range(2)]
                den2 = [pl(ph, "den%d" % i, [128, 512], F32) for i in range(2)]

                DMA("pool", kctok[:], ck_in[j].rearrange("(t p) f -> p t f", p=128), (), ["kctok"])
                DMA("pool", vctx[:], cv_in[j].rearrange("(t p) f -> p t f", p=128), (), ["vctx"])
                for kvh in range(2):
                    for t in range(4):
                        TR(psb[kvh][:, t * 128:(t + 1) * 128], kctok[:, t, kvh * 128:(kvh + 1) * 128], idb[:],
                           ["kctok", "idb"], [("ps", kvh)])
                    CP("dve", kctxT[:, kvh, :], psb[kvh][:, 0:512], [("ps", kvh)], [("kctx", kvh)])

                kk = 0
                mi = 0
                for g in range(2 if cfg.get("att_stop", 9) > 0 else 0):
                    slq, kq = ring_next()
                    vq = slq[:, 0:4096].rearrange("p (k f) -> p k f", k=8)
                    wsrc = w_qkv[j].rearrange("(k p) f -> p k f", p=128)
                    DMA("pool", vq, wsrc[:, :, g * 512:(g + 1) * 512], (), [kq], scoped=False)
                    slkv, kkv = ring_next()
                    vkv = slkv[:, 0:4096].rearrange("p (k f) -> p k f", k=8)
                    DMA("pool", vkv[:, :, 0:128], wsrc[:, :, 1024 + g * 128:1024 + (g + 1) * 128], (), [kkv], scoped=False)
                    DMA("pool", vkv[:, :, 128:256], wsrc[:, :, 1280 + g * 128:1280 + (g + 1) * 128], (), [kkv], scoped=False)
                    def stage1_b(t):
                        tsl = slice(t * 128, (t + 1) * 128)
                        b2 = t % 2
                        pb = 4 + b2
                        for hh in range(5):
                            TR(psb[pb][:, hh * 128:(hh + 1) * 128], rb[b2][:, hh * 128:(hh + 1) * 128], idb[:],
                               [("rb", b2), "idb"], [("ps", pb)])
                        CP("act", qoT[:, :, tsl], psb[pb][:, 0:512].rearrange("p (h t) -> p h t", h=4), [("ps", pb)], [("qo", t)])
                        CP("act", kT[:, tsl], psb[pb][:, 512:640], [("ps", pb)], [("kT", t)])

                    for t in range(NT):
                        tsl = slice(t * 128, (t + 1) * 128)
                        b2 = t % 2
                        pq = ps[b2]
                        pkv = ps[2 + b2]
                        for kc in range(KC):
                            MM(pq[:], hT[:, kc, tsl], vq[:, kc, :], kc == 0, kc == KC - 1, [("h", kc, t // 4), kq], [("ps", b2)])
                        for kc in range(KC):
                            MM(pkv[:, 0:256], hT[:, kc, tsl], vkv[:, kc, 0:256], kc == 0, kc == KC - 1,
                               [("h", kc, t // 4), kkv], [("ps", 2 + b2)])
                        DMA("sp", ropet[b2][:], rope_in[t], (), [("ropet", b2)])
                        CP("act", qkf[b2][:, 0:512], pq[:], [("ps", b2)], [("qkf", b2, 0)])
                        CP("act", qkf[b2][:, 512:768], pkv[:, 0:256], [("ps", 2 + b2)], [("qkf", b2, 1)])
                        CP("dve", vtok[:, t, :], qkf[b2][:, 640:768], [("qkf", b2, 1)], [("vt", t)])
                        DMA("sp", nk_out[j, tsl, g * 128:(g + 1) * 128], qkf[b2][:, 512:640], [("qkf", b2, 1)], [], is_output=True)
                        DMA("sp", nv_out[j, tsl, g * 128:(g + 1) * 128], qkf[b2][:, 640:768], [("qkf", b2, 1)], [], is_output=True)
                        x3 = qkf[b2][:, 0:640].rearrange("p (h d) -> p h d", h=5)
                        cosb = ropet[b2][:, 0:128].unsqueeze(1).to_broadcast([128, 5, 128])
                        TT("dve", r1[b2][:].rearrange("p (h d) -> p h d", h=5), x3, cosb, ALU.mult,
                           [("qkf", b2, 0), ("qkf", b2, 1), ("ropet", b2)], [("r1", b2)])
                        x5 = qkf[b2][:, 0:640].rearrange("p (h a s i) -> p h a s i", h=5, a=2, s=2)
                        o5 = r2[b2][:].rearrange("p (h a s i) -> p h a s i", h=5, a=2, s=2)
                        s4 = ropet[b2][:, 128:256].rearrange("p (a s i) -> p a s i", a=2, s=2)
                        for sidx in range(2):
                            sinb = s4[:, :, sidx, :].unsqueeze(1).to_broadcast([128, 5, 2, 32])
                            TT("dve", o5[:, :, :, sidx, :], x5[:, :, :, 1 - sidx, :], sinb, ALU.mult,
                               [("qkf", b2, 0), ("qkf", b2, 1), ("ropet", b2)], [("r2", b2, sidx)])
                        TT("dve", rb[b2][:], r1[b2][:], r2[b2][:], ALU.add, [("r1", b2), ("r2", b2, 0), ("r2", b2, 1)], [("rb", b2)])
                        if t > 0:
                            stage1_b(t - 1)
                    stage1_b(NT - 1)
                    def scores(n):
                        nonlocal kk, mi
                        nsl = slice(n * 128, (n + 1) * 128)
                        chunks = [("ctx", t) for t in range(4)]
                        if n > 0:
                            chunks.append(("prev", n - 1))
                        chunks.append(("cen", n))
                        if n < NT - 1:
                            chunks.append(("next", n + 1))
                        pt = pT[n % 2]
                        for ci, (kind, idx) in enumerate(chunks):
                            pS = 4 + kk % 4
                            kk += 1
                            if kind == "ctx":
                                lhs = kctxT[:, g, idx * 128:(idx + 1) * 128]
                                rk = [("kctx", g)]
                            else:
                                lhs = kT[:, idx * 128:(idx + 1) * 128]
                                rk = [("kT", idx)]
                            MM(ps[pS][:], lhs, qoT[:, :, nsl], True, True, rk + [("qo", n)], [("ps", pS)])
                            pk = ("pT", n % 2, ci)
                            if kind == "ctx":
                                ACT(pt[:, ci, :], ps[pS][:], AF.Exp, [("ps", pS), "pcs"], [pk], scale=SC,
                                    bias=pcs[:, PC_CTXB:PC_CTXB + 1])
                            else:
                                ACT(pt[:, ci, :], ps[pS][:], AF.Exp, [("ps", pS)], [pk], scale=SC)
                            if kind in ("prev", "next"):
                                side = 0 if kind == "prev" else 1
                                base = PC_MASK + (n * 2 + side) * 2
                                mt = mskt[mi % 2]
                                mkk = ("mskt", mi % 2)
                                mi += 1
                                TS("dve", mt[:], tri[:, side, :], pcs[:, base:base + 1], pcs[:, base + 1:base + 2],
                                   ALU.mult, ALU.add, ["tri", "pcs"], [mkk])
                                pv = pt[:, ci, :].rearrange("p (h t) -> p h t", h=4)
                                TT("dve", pv, pv, mt[:].unsqueeze(1).to_broadcast([128, 4, 128]), ALU.mult, [pk, mkk], [pk])
                            yield chunks

                    def pv_norm(n, chunks):
                        nsl = slice(n * 128, (n + 1) * 128)
                        pt = pT[n % 2]
                        nch = len(chunks)
                        bO = n % 2
                        bD = 2 + n % 2
                        dn = den2[n % 2]
                        dk = ("den", n % 2)
                        for ci, (kind, idx) in enumerate(chunks):
                            pk = ("pT", n % 2, ci)
                            if kind == "ctx":
                                vl = vctx[:, idx, g * 128:(g + 1) * 128]
                                rv = ["vctx"]
                            else:
                                vl = vtok[:, idx, :]
                                rv = [("vt", idx)]
                            MM(ps[bO][:], vl, pt[:, ci, :], ci == 0, ci == nch - 1, rv + [pk], [("ps", bO)])
                            MM(ps[bD][:], ones1[:], pt[:, ci, :], ci == 0, ci == nch - 1, ["ones1", pk], [("ps", bD)])
                            yield ci
                        d3 = dn[:].rearrange("p (h t) -> p h t", h=4)
                        es4 = esink[:, j * 8 + g * 4:j * 8 + g * 4 + 4].unsqueeze(2).to_broadcast([128, 4, 128])
                        TT("dve", d3, ps[bD][:].rearrange("p (h t) -> p h t", h=4), es4, ALU.add, [("ps", bD), "esink"], [dk])
                        ACT(dn[:], dn[:], AF.Ln, [dk], [dk])
                        ACT(dn[:], dn[:], AF.Exp, [dk], [dk], scale=-1.0)
                        TT("dve", qoT[:, :, nsl], ps[bO][:].rearrange("p (h t) -> p h t", h=4), d3, ALU.mult,
                           [("ps", bO), dk], [("qo", n)])

                    def chunk_list(n):
                        chunks = [("ctx", t) for t in range(4)]
                        if n > 0:
                            chunks.append(("prev", n - 1))
                        chunks.append(("cen", n))
                        if n < NT - 1:
                            chunks.append(("next", n + 1))
                        return chunks

                    def drain(gen):
                        for _ in gen:
                            pass

                    if cfg.get("att_stop", 9) > 1:
                        drain(scores(0))
                        for n in range(NT):
                            gs_ = scores(n + 1) if n + 1 < NT else None
                            gp_ = pv_norm(n, chunk_list(n))
                            while gs_ is not None or gp_ is not None:
                                if gs_ is not None:
                                    try:
                                        next(gs_)
                                    except StopIteration:
                                        gs_ = None
                                if gp_ is not None:
                                    try:
                                        next(gp_)
                                    except StopIteration:
                                        gp_ = None
                    slo, ko = ring_next()
                    vo = slo[:, 0:4096].rearrange("p (k f) -> p k f", k=4)
                    DMA("pool", vo, w_outb[j][g * 512:(g + 1) * 512, :].rearrange("(k p) f -> p k f", p=128), (), [ko], scoped=False)
                    ko_i = 0
                    for tb in range(NB if cfg.get("att_stop", 9) > 2 else 0):
                        for dm in range(KC):
                            pd = ko_i % 2
                            ko_i += 1
                            for hh in range(4):
                                MM(ps[pd][:], vo[:, hh, dm * 128:(dm + 1) * 128], qoT[:, hh, tbs(tb)], hh == 0, hh == 3,
                                   [ko] + [("qo", 4 * tb + q) for q in range(4)], [("ps", pd)])
                            STT("dve", xT[:, dm, tbs(tb)], ps[pd][:], modall[:, l, 16 + dm:17 + dm], xT[:, dm, tbs(tb)],
                                ALU.mult, ALU.add, [("ps", pd), mk, ("x", dm, tb)], [("x", dm, tb)])
            S.barrier()

        def rev_ap(t, n):
            return bass.AP(t, n - 1, [[n, 128], [-1, n]])

        def hgrn(l):
            j = l // 2
            mk = ("mod", l)
            with contextlib.ExitStack() as ph:
                ogT = pl(ph, "ogT", [128, 2, T], BF16)
                qd = [pl(ph, "qd%d" % i, [128, T], BF16) for i in range(2)]
                kdT = [pl(ph, "kdT%d" % i, [128, T], BF16) for i in range(2)]
                kdtok = [pl(ph, "kdtok%d" % i, [128, NT, 128], BF16) for i in range(2)]
                vtok = pl(ph, "hvtok", [128, NT, 128], BF16)
                oacc = pl(ph, "oacc", [128, T], F32)
                sig = [pl(ph, "sig%d" % i, [128, TB], F32) for i in range(2)]
                lf = [pl(ph, "lf%d" % i, [128, TB], F32) for i in range(2)]
                ebt = pl(ph, "ebt", [128, TB], F32)
                enbt = pl(ph, "enbt", [128, TB], F32)
                qs = pl(ph, "qs", [128, TB], F32)
                osq = pl(ph, "osq", [128, TB], BF16)
                segm = [pl(ph, "segm%d" % i, [128, TB], F32) for i in range(2)]
                Dor = [pl(ph, "Dor%d" % i, [128, 64], F32) for i in range(2)]
                Dca = [pl(ph, "Dca%d" % i, [128, 64], F32) for i in range(2)]
                Tst = [[pl(ph, "Tst%d_%d" % (d, i), [128, 128], F32) for i in range(2)] for d in range(2)]
                Sbf = [[pl(ph, "Sbf%d_%d" % (d, i), [128, 128], BF16) for i in range(2)] for d in range(2)]
                sfin = [pl(ph, "sfin%d" % i, [128, 128], F32) for i in range(2)]
                amask = [[pl(ph, "am%d_%d" % (d, i), [128, 128], BF16) for i in range(2)] for d in range(2)]
                vblk = [pl(ph, "vblk%d" % i, [128, 2, 4, 128], BF16) for i in range(2)]

                MEMSET("dve", segm[0][:], 1.0, ["segm0"])
                MEMSET("dve", segm[1][:], 1.0, ["segm1"])
                MEMSET("dve", segm[0][:].rearrange("p (c i) -> p c i", i=32)[:, :, 0:1], 0.0, ["segm0"])
                MEMSET("dve", segm[1][:].rearrange("p (c i) -> p c i", i=32)[:, :, 31:32], 0.0, ["segm1"])
                sf_i = 0
                for h in range(8):
                    slA, kA = ring_next()
                    vA = slA[:, 0:4096].rearrange("p (k f) -> p k f", k=8)
                    wsrc = w_ina[j].rearrange("(k p) f -> p k f", p=128)
                    for gi, c0 in enumerate((0, 1024, 2048, 4096)):
                        DMA("pool", vA[:, :, gi * 128:(gi + 1) * 128], wsrc[:, :, c0 + h * 128:c0 + (h + 1) * 128], (), [kA], scoped=False)
                    slB, kB = ring_next()
                    vB = slB[:, 0:4096].rearrange("p (k f) -> p k f", k=8)
                    DMA("pool", vB[:, :, 0:128], wsrc[:, :, 3072 + h * 128:3072 + (h + 1) * 128], (), [kB], scoped=False)
                    for d in range(2):
                        DMA("sp", Tst[d][1][:], st0_in[j, d, h], (), [("T", d, 1)])
                    def kd_tr(tb):
                        for d in range(2):
                            pb = 6 + d
                            for ti in range(4):
                                t = tb * 4 + ti
                                TR(psb[pb][:, ti * 128:(ti + 1) * 128], kdT[d][:, t * 128:(t + 1) * 128], idb[:],
                                   [("kdT", d, tb), "idb"], [("ps", pb)])
                            CP("dve", kdtok[d][:, tb * 4:(tb + 1) * 4, :], psb[pb][:, 0:512].rearrange("p (t e) -> p t e", t=4),
                               [("ps", pb)], [("kdt", d, tb * 4 + q) for q in range(4)])

                    for tb in range(NB):
                        for gi in range(3):
                            for kc in range(KC):
                                MM(ps[gi][:], vA[:, kc, gi * 128:(gi + 1) * 128], hT[:, kc, tbs(tb)], kc == 0, kc == KC - 1,
                                   [kA, ("h", kc, tb)], [("ps", gi)])
                        if tb > 0:
                            kd_tr(tb - 1)
                        ACT(qs[:], ps[0][:], AF.Sigmoid, [("ps", 0)], ["qs"])
                        for d in range(2):
                            ACT(sig[d][:], ps[1 + d][:], AF.Sigmoid, [("ps", 1 + d)], [("sig", d)])
                        TT("dve", qs[:], qs[:], ps[0][:], ALU.mult, ["qs", ("ps", 0)], ["qs"])
                        for ti in range(4):
                            t = tb * 4 + ti
                            for kc in range(KC):
                                MM(ps[3][:, ti * 128:(ti + 1) * 128], hT[:, kc, t * 128:(t + 1) * 128], vB[:, kc, 0:128],
                                   kc == 0, kc == KC - 1, [kB, ("h", kc, tb)], [("ps", 3)])
                        CP("dve", vtok[:, tb * 4:(tb + 1) * 4, :], ps[3][:].rearrange("p (t e) -> p t e", t=4), [("ps", 3)],
                           [("hvt", tb * 4 + q) for q in range(4)])
                        for d in range(2):
                            col = d * 8 + h
                            ACT(lf[d][:], sig[d][:], AF.Ln, [("sig", d), "lbt", "omlt"], [("lf", d)],
                                scale=omlt[:, j, col:col + 1], bias=lbt[:, j, col:col + 1])
                        for d in range(2):
                            if d == 0:
                                S.op("dve", lambda: nc.vector.tensor_tensor_scan(
                                    out=lf[0][:], data0=segm[0][:], data1=lf[0][:], initial=0.0, op0=ALU.mult, op1=ALU.add),
                                    [("lf", 0), "segm0"], [("lf", 0)])
                            else:
                                S.op("dve", lambda: nc.vector.tensor_tensor_scan(
                                    out=rev_ap(lf[1], TB), data0=rev_ap(segm[1], TB), data1=rev_ap(lf[1], TB), initial=0.0,
                                    op0=ALU.mult, op1=ALU.add), [("lf", 1), "segm1"], [("lf", 1)])
                        for d in range(2):
                            col = d * 8 + h
                            ACT(ebt[:], lf[d][:], AF.Exp, [("lf", d)], ["ebt"])
                            ACT(enbt[:], lf[d][:], AF.Exp, [("lf", d)], ["enbt"], scale=-1.0)
                            TS("dve", sig[d][:], sig[d][:], nomlt[:, j, col:col + 1], omlt[:, j, col:col + 1], ALU.mult, ALU.add,
                               [("sig", d), "omlt", "nomlt"], [("sig", d)])
                            TT("dve", qd[d][:, tbs(tb)], qs[:], ebt[:], ALU.mult, ["qs", "ebt"], [("qd", d, tb)])
                            TT("dve", kdT[d][:, tbs(tb)], sig[d][:], enbt[:], ALU.mult, [("sig", d), "enbt"], [("kdT", d, tb)])
                            e3 = ebt[:].rearrange("p (c i) -> p c i", i=32)
                            pos = 31 if d == 0 else 0
                            CP("dve", Dor[d][:, tb * 16:(tb + 1) * 16].unsqueeze(2), e3[:, :, pos:pos + 1], ["ebt"], [("Dor", d)])
                    kd_tr(NB - 1)
                    for d in range(2):
                        CP("dve", Dca[d][:], Dor[d][:], [("Dor", d)], [("Dca", d)])
                        pos = 7 if d == 0 else 0
                        dv = Dca[d][:].rearrange("p (s c) -> p s c", c=8)[:, :, pos:pos + 1]
                        TS("dve", dv, dv, pcs[:, PC_CARRY:PC_CARRY + 1], None, ALU.mult, None, [("Dca", d), "pcs"], [("Dca", d)])
                    for i in range(NT):
                        dirs = ((0, i), (1, NT - 1 - i))
                        for ti_, tl_ in enumerate((i, NT - 1 - i)):
                            TT("pool", vblk[i % 2][:, ti_, :, :], vtok[:, tl_, :].unsqueeze(1).to_broadcast([128, 4, 128]),
                               bmb[:].unsqueeze(2).to_broadcast([128, 4, 128]), ALU.mult,
                               [("hvt", tl_), "bmb"], [("vblk", i % 2, ti_)])
                        for d, tile in dirs:
                            tsl = slice(tile * 128, (tile + 1) * 128)
                            pU = d * 2 + i % 2
                            MM(ps[pU][:], kdtok[d][:, tile, :], vblk[i % 2][:, d, :, :], True, True,
                               [("kdt", d, tile), ("vblk", i % 2, d)], [("ps", pU)])
                            xb = 4 + d * 2 + i % 2
                            MM(ps[xb][:, 0:128], kdT[d][:, tsl], qd[d][:, tsl], True, True,
                               [("kdT", d, tile // 4), ("qd", d, tile // 4)], [("ps", xb)])
                        for d, tile in dirs:
                            xb = 4 + d * 2 + i % 2
                            TT("dve", amask[d][i % 2][:], ps[xb][:, 0:128], hmk[:, d, :], ALU.mult, [("ps", xb), "hmk"],
                               [("am", d, i % 2)])
                        for d, tile in dirs:
                            xb = 4 + d * 2 + i % 2
                            MM(ps[xb][:, 128:256], vtok[:, tile, :], amask[d][i % 2][:], True, False,
                               [("hvt", tile), ("am", d, i % 2)], [("ps", xb)])
                        for cc in range(4):
                            for d, tile in dirs:
                                pU = d * 2 + i % 2
                                xb = 4 + d * 2 + i % 2
                                ok = ("ps", xb)
                                pO = ps[xb][:, 128:256]
                                c = cc if d == 0 else 3 - cc
                                cidx = tile * 4 + c
                                n = cidx if d == 0 else 63 - cidx
                                pc = cidx - 1 if d == 0 else cidx + 1
                                told = Tst[d][(n - 1) % 2]
                                tnew = Tst[d][n % 2]
                                sbf = Sbf[d][n % 2]
                                kold = ("T", d, (n - 1) % 2)
                                knew = ("T", d, n % 2)
                                ksb = ("Sbf", d, n % 2)
                                if n == 0:
                                    CP("act", sbf[:], told[:], [kold], [ksb])
                                else:
                                    ACT(sbf[:], told[:], AF.Copy, [kold, ("Dca", d)], [ksb], scale=Dca[d][:, pc:pc + 1])
                                MM(pO[:, c * 32:(c + 1) * 32], sbf[:], qd[d][:, tile * 128 + c * 32:tile * 128 + (c + 1) * 32],
                                   False, cc == 3, [ksb, ("qd", d, tile // 4)], [ok])
                                uap = ps[pU][:, c * 128:(c + 1) * 128]
                                if n == 0:
                                    TT("dve", tnew[:], told[:], uap, ALU.add, [kold, ("ps", pU)], [knew])
                                else:
                                    STT("dve", tnew[:], told[:], Dca[d][:, pc:pc + 1], uap, ALU.mult, ALU.add,
                                        [kold, ("Dca", d), ("ps", pU)], [knew])
                                seg_end = (cidx % 8 == 7) if d == 0 else (cidx % 8 == 0)
                                if seg_end:
                                    m = cidx // 8
                                    sfb = sfin[sf_i % 2]
                                    ksf = ("sfin", sf_i % 2)
                                    sf_i += 1
                                    ACT(sfb[:], tnew[:], AF.Copy, [knew, ("Dor", d)], [ksf], scale=Dor[d][:, cidx:cidx + 1])
                                    DMA("sp", ns_out[j, d, m, h], sfb[:], [ksf], [], is_output=True)
                        for d, tile in dirs:
                            tsl = slice(tile * 128, (tile + 1) * 128)
                            xb = 4 + d * 2 + i % 2
                            pO = ps[xb][:, 128:256]
                            if i <= 7:
                                CP("act", oacc[:, tsl], pO, [("ps", xb)], [("oacc", tile)])
                            else:
                                TT("dve", oacc[:, tsl], oacc[:, tsl], pO, ALU.add, [("ps", xb), ("oacc", tile)], [("oacc", tile)])
                    gcol = R_GN + j * 8 + h

                    def gbufs(tb):
                        if tb % 2 == 0:
                            return (ebt, "ebt", enbt, "enbt", qs, "qs", osq[:], "osq")
                        return (sig[0], ("sig", 0), sig[1], ("sig", 1), lf[0], ("lf", 0),
                                lf[1][:].bitcast(BF16)[:, 0:TB], ("lf", 1))

                    for pr in range(2):
                        tbl = (2 * pr, 2 * pr + 1)
                        for tb in tbl:
                            b_rs, k_rs, b_t1, k_t1, b_gs, k_gs, b_sq, k_sq = gbufs(tb)
                            ok4 = [("oacc", 4 * tb + q) for q in range(4)]
                            ACT(b_sq, oacc[:, tbs(tb)], AF.Square, ok4, [k_sq])
                        for tb in tbl:
                            b_rs, k_rs, b_t1, k_t1, b_gs, k_gs, b_sq, k_sq = gbufs(tb)
                            pA_ = (tb % 2) * 2
                            pG_ = (tb % 2) * 2 + 1
                            MM(ps[pA_][:], onese[:], b_sq, True, True, [k_sq, "onese"], [("ps", pA_)])
                            for kc in range(KC):
                                MM(ps[pG_][:], vA[:, kc, 384:512], hT[:, kc, tbs(tb)], kc == 0, kc == KC - 1, [kA, ("h", kc, tb)],
                                   [("ps", pG_)])
                        for tb in tbl:
                            b_rs, k_rs, b_t1, k_t1, b_gs, k_gs, b_sq, k_sq = gbufs(tb)
                            pA_ = (tb % 2) * 2
                            ACT(b_rs[:], ps[pA_][:], AF.Ln, [("ps", pA_)], [k_rs], bias=EPS)
                        for tb in tbl:
                            b_rs, k_rs, b_t1, k_t1, b_gs, k_gs, b_sq, k_sq = gbufs(tb)
                            ACT(b_rs[:], b_rs[:], AF.Exp, [k_rs], [k_rs], scale=-0.5)
                        for tb in tbl:
                            b_rs, k_rs, b_t1, k_t1, b_gs, k_gs, b_sq, k_sq = gbufs(tb)
                            pG_ = (tb % 2) * 2 + 1
                            ACT(b_gs[:], ps[pG_][:], AF.Sigmoid, [("ps", pG_)], [k_gs])
                        for tb in tbl:
                            b_rs, k_rs, b_t1, k_t1, b_gs, k_gs, b_sq, k_sq = gbufs(tb)
                            pG_ = (tb % 2) * 2 + 1
                            ok4 = [("oacc", 4 * tb + q) for q in range(4)]
                            STT("dve", b_t1[:], oacc[:, tbs(tb)], vecT[:, gcol:gcol + 1], b_rs[:], ALU.mult, ALU.mult,
                                ok4 + ["vecT", k_rs], [k_t1])
                            TT("dve", b_t1[:], b_t1[:], b_gs[:], ALU.mult, [k_t1, k_gs], [k_t1])
                            TT("dve", ogT[:, h % 2, tbs(tb)], b_t1[:], ps[pG_][:], ALU.mult, [k_t1, ("ps", pG_)], [("og", h % 2, tb)])
                    if h % 2 == 1:
                        slo, ko = ring_next()
                        vo = slo[:, 0:2048].rearrange("p (k f) -> p k f", k=2)
                        DMA("pool", vo, w_outa[j][(h - 1) * 128:(h + 1) * 128, :].rearrange("(k p) f -> p k f", p=128), (), [ko],
                            scoped=False)
                        ko_i = 0
                        for tb in range(NB):
                            for dm in range(KC):
                                pd = 4 + ko_i % 4
                                ko_i += 1
                                for hh in range(2):
                                    MM(ps[pd][:], vo[:, hh, dm * 128:(dm + 1) * 128], ogT[:, hh, tbs(tb)], hh == 0, hh == 1,
                                       [ko, ("og", hh, tb)], [("ps", pd)])
                                STT("dve", xT[:, dm, tbs(tb)], ps[pd][:], modall[:, l, 16 + dm:17 + dm], xT[:, dm, tbs(tb)],
                                    ALU.mult, ALU.add, [("ps", pd), mk, ("x", dm, tb)], [("x", dm, tb)])
            S.barrier()


        l0 = cfg.get("l0", 0)
        for l in range(l0, depth):
            if l == l0 or not cfg["ffn"]:
                adaln(l)
            if cfg["mixers"]:
                norm_mod(l, 0)
                if l % 2 == 0:
                    hgrn(l)
                else:
                    attn(l)
            if cfg["ffn"]:
                if l + 1 < depth:
                    ada_next[0] = adaln_gen(l + 1)
                ffn(l)
                while ada_next[0] is not None:
                    ada_step()
        final_out()
        stats = S.emit()
        if cfg.get("dbg"):
            print("sched stats", stats)
    return nc

def _consts():
    cm = np.zeros((5, 128, 128), np.float32)
    cm[0] = np.eye(128, dtype=np.float32)
    j = np.arange(128)[:, None]
    i = np.arange(128)[None, :]
    cm[1] = (i <= j).astype(np.float32)
    cm[2] = (j <= i).astype(np.float32)
    same = (j // 32) == (i // 32)
    cm[3] = (same & (j <= i)).astype(np.float32)
    cm[4] = (same & (j >= i)).astype(np.float32)
    return cm


def _rope_tables(sample):
    tab = np.zeros((NT, 128, 256), np.float32)
    if not sample:
        tab[:, :, 0:128] = 1.0
        return tab
    t = np.arange(T, dtype=np.float32)
    row = np.floor(t / 64.0).astype(np.float32)
    col = (t - row * 64.0).astype(np.float32)
    inv = (10000.0 ** (-np.arange(32, dtype=np.float32) / 32.0)).astype(np.float32)
    ar = row[:, None] * inv[None, :]
    ac = col[:, None] * inv[None, :]
    cr, sr, cc, sc_ = np.cos(ar), np.sin(ar), np.cos(ac), np.sin(ac)
    cos = np.concatenate([cr, cr, cc, cc], axis=1)
    sin = np.concatenate([-sr, sr, -sc_, sc_], axis=1)
    tab[:, :, 0:128] = cos.reshape(NT, 128, 128)
    tab[:, :, 128:256] = sin.reshape(NT, 128, 128)
    return tab.astype(np.float32)


def _pcs(sample):
    p = np.zeros((N_PCS,), np.float32)
    p[PC_CARRY] = 1.0 if sample else 0.0
    p[PC_CTXB] = 0.0 if sample else -30000.0
    for n in range(NT):
        for side in range(2):
            base = PC_MASK + (n * 2 + side) * 2
            if sample:
                p[base], p[base + 1] = 1.0, 0.0
            else:
                valid = (side == 1 and n % 2 == 0) or (side == 0 and n % 2 == 1)
                p[base], p[base + 1] = 0.0, (1.0 if valid else 0.0)
    return np.ascontiguousarray(np.broadcast_to(p[None, :], (128, N_PCS))).astype(np.float32)


def kernel(x_prompt, x_sample, cache_k, cache_v, state_hgrn, c, c_ctx, w_ada, b_ada, norm1, norm2, norm_final,
           w_gate_up, w_down, w_in_a, lower_bounds, gnorm_a, w_out_a, w_qkv_b, w_out_b, sink_b):
    f = lambda a: np.ascontiguousarray(np.asarray(a), dtype=np.float32)
    x_prompt, x_sample, cache_k, cache_v, state_hgrn = map(f, (x_prompt, x_sample, cache_k, cache_v, state_hgrn))
    c, c_ctx, b_ada, norm1, norm2, norm_final = map(f, (c, c_ctx, b_ada, norm1, norm2, norm_final))
    lower_bounds, gnorm_a, sink_b = map(f, (lower_bounds, gnorm_a, sink_b))
    weights = dict(w_ada=f(w_ada), w_gate_up=f(w_gate_up), w_down=f(w_down), w_in_a=f(w_in_a),
                   w_out_a=f(w_out_a), w_qkv_b=f(w_qkv_b), w_out_b=f(w_out_b))
    vecs = np.zeros((N_VROWS, 128), np.float32)
    vecs[R_BADA:R_BADA + 192] = b_ada.reshape(192, 128)
    vecs[R_N1:R_N1 + 32] = norm1.reshape(32, 128)
    vecs[R_N2:R_N2 + 32] = norm2.reshape(32, 128)
    vecs[R_NF:R_NF + 8] = norm_final.reshape(8, 128)
    vecs[R_GN:R_GN + 16] = gnorm_a.reshape(16, 128)
    vecs[R_LB:R_LB + 32] = lower_bounds.reshape(32, 128)
    cm = _consts()
    bm = (np.arange(128)[:, None] // 32 == np.arange(4)[None, :]).astype(np.float32)
    sinkb = np.ascontiguousarray(np.broadcast_to(sink_b.reshape(1, 16), (128, 16))).astype(np.float32)
    rope_s, rope_p = _rope_tables(True), _rope_tables(False)
    pcs_s, pcs_p = _pcs(True), _pcs(False)
    zck = np.zeros((2, 512, 256), np.float32)
    zst = np.zeros((2, 2, 8, 128, 128), np.float32)
    in_maps = []
    for core in range(8):
        sample = core < 4
        if sample:
            b = core
            m = dict(x=x_sample[b], cond=np.ascontiguousarray(c[b].reshape(8, 128).T),
                     ck=np.ascontiguousarray(cache_k[b].reshape(2, 512, 256)),
                     cv=np.ascontiguousarray(cache_v[b].reshape(2, 512, 256)),
                     st0=np.ascontiguousarray(state_hgrn[b]), rope=rope_s, pcs=pcs_s)
        else:
            s0 = (core - 4) * 8
            m = dict(x=np.ascontiguousarray(x_prompt[s0:s0 + 8].reshape(T, D)),
                     cond=np.ascontiguousarray(c_ctx.reshape(8, 128).T), ck=zck, cv=zck, st0=zst,
                     rope=rope_p, pcs=pcs_p)
        m.update(vecs=vecs, cmat=cm, sinkb=sinkb, bm=bm)
        m.update(weights)
        in_maps.append(m)
    nc = build_nc(CFG)
    res = run_bass_kernel_spmd(nc, in_maps, core_ids=list(range(8)))
    R = res.results
    y_prompt = np.zeros((32, 256, D), np.float32)
    y_sample = np.zeros((4, T, D), np.float32)
    nk = np.zeros((32, 2, 256, 2, 128), np.float32)
    nv = np.zeros((32, 2, 256, 2, 128), np.float32)
    nst = np.zeros((32, 2, 2, 8, 128, 128), np.float32)
    for core in range(8):
        r = R[core]
        if core < 4:
            y_sample[core] = r["y"]
        else:
            s0 = (core - 4) * 8
            y_prompt[s0:s0 + 8] = r["y"].reshape(8, 256, D)
            nk[s0:s0 + 8] = r["nk"].reshape(2, 8, 256, 2, 128).transpose(1, 0, 2, 3, 4)
            nv[s0:s0 + 8] = r["nv"].reshape(2, 8, 256, 2, 128).transpose(1, 0, 2, 3, 4)
            nst[s0:s0 + 8] = r["ns"].transpose(2, 0, 1, 3, 4, 5)
    return (y_prompt, y_sample, nk, nv, nst)
```

```python
import numpy as np
import contextlib
import concourse.bass as bass
import concourse.mybir as mybir
from concourse.bass_utils import run_bass_kernel_spmd
F32 = mybir.dt.float32
BF16 = mybir.dt.bfloat16
AF = mybir.ActivationFunctionType
ALU = mybir.AluOpType
AX = mybir.AxisListType


COMPUTE = ("pe", "act", "dve", "pool")


class Op:
    __slots__ = ("eng", "fn", "reads", "writes", "deps", "is_dma", "marked",
                 "token", "idx", "eidx", "pre_waits", "cpos")

    def __init__(self, eng, fn, reads, writes, is_dma):
        self.eng = eng
        self.fn = fn
        self.reads = reads
        self.writes = writes
        self.is_dma = is_dma
        self.deps = []
        self.marked = False
        self.token = None
        self.pre_waits = []


class Sched:
    def __init__(self, nc, es, n_dma_sems=8):
        self.nc = nc
        self.engs = {"pe": nc.tensor, "act": nc.scalar, "dve": nc.vector,
                     "pool": nc.gpsimd, "sp": nc.sync}
        self.sems = {e: es.enter_context(nc.semaphore("s_" + e)) for e in COMPUTE}
        self.dma_sems = {}
        for q in ("sp", "pool", "act"):
            self.dma_sems[q] = [es.enter_context(nc.semaphore("d_%s%d" % (q, i)))
                                for i in range(n_dma_sems)]
        self.ops = []
        self.last_w = {}
        self.readers = {}
        self.out_dma_ops = []
        self.last_on_eng = {}
        self.dma_since = []
        self.bar = []

    def barrier(self):
        self.bar = sorted(set(list(self.last_on_eng.values()) + self.dma_since))
        self.dma_since = []

    def _keys(self, ks):
        out = []
        for k in ks:
            if isinstance(k, list):
                out.extend(k)
            else:
                out.append(k)
        return out

    def op(self, eng, fn, reads=(), writes=()):
        rk = self._keys(reads)
        wk = self._keys(writes)
        wk = wk + [k for k in rk if isinstance(k, tuple) and k[0] == "ps"]
        rk = [k for k in rk if not (isinstance(k, tuple) and k[0] == "ps")]
        o = Op(eng, fn, rk, wk, False)
        self._add(o, True)
        self.last_on_eng[eng] = o.idx
        return o

    def dma(self, queue, fn, reads=(), writes=(), is_output=False, scoped=True):
        o = Op(queue, fn, self._keys(reads), self._keys(writes), True)
        self._add(o, scoped)
        if scoped:
            self.dma_since.append(o.idx)
        if is_output:
            self.out_dma_ops.append(o)
        return o

    def _add(self, o, scoped=True):
        o.idx = len(self.ops)
        deps = set(self.bar) if scoped else set()
        for k in o.reads:
            w = self.last_w.get(k)
            if w is not None:
                deps.add(w)
        for k in o.writes:
            w = self.last_w.get(k)
            if w is not None:
                deps.add(w)
            for r in self.readers.get(k, ()):
                deps.add(r)
        deps.discard(o.idx)
        o.deps = sorted(deps)
        for k in o.reads:
            self.readers.setdefault(k, []).append(o.idx)
        for k in o.writes:
            self.last_w[k] = o.idx
            self.readers[k] = []
        self.ops.append(o)

    def emit(self):
        ops = self.ops
        per_eng_count = {}
        by_eng = {}
        for o in ops:
            o.eidx = per_eng_count.get(o.eng, 0)
            per_eng_count[o.eng] = o.eidx + 1
            if not o.is_dma:
                lst = by_eng.setdefault(o.eng, [])
                o.cpos = len(lst)
                lst.append(o.idx)
        dma_rr = {q: 0 for q in self.dma_sems}
        dma_val = {}
        for q, lst in self.dma_sems.items():
            for i in range(len(lst)):
                dma_val[(q, i)] = 0
        for o in ops:
            if o.is_dma:
                q = o.eng
                i = dma_rr[q]
                dma_rr[q] = (i + 1) % len(self.dma_sems[q])
                prev = dma_val[(q, i)]
                if prev > 0:
                    o.pre_waits.append((("dma", q, i), prev))
                dma_val[(q, i)] = prev + 16
                o.token = (("dma", q, i), prev + 16)

        def merge(a, b):
            for k, v in b.items():
                if a.get(k, 0) < v:
                    a[k] = v

        clock = {e: {} for e in self.engs}
        vc = [None] * len(ops)
        waits = [None] * len(ops)
        for o in ops:
            e = o.eng
            ck = clock[e]
            need = {}
            for s, v in o.pre_waits:
                if ck.get(s, 0) < v:
                    need[s] = max(need.get(s, 0), v)
            for d in o.deps:
                p = ops[d]
                if p.is_dma:
                    s, v = p.token
                else:
                    if p.eng == e and not o.is_dma and e == "pe":
                        continue
                    s, v = ("eng", p.eng), p.cpos + 1
                if ck.get(s, 0) >= v:
                    continue
                need[s] = max(need.get(s, 0), v)
                merge(ck, vc[d])
            for s, v in need.items():
                if ck.get(s, 0) < v:
                    ck[s] = v
            waits[o.idx] = list(need.items())
            snap = dict(ck)
            if o.is_dma:
                snap[o.token[0]] = o.token[1]
            else:
                s_own = ("eng", e)
                snap[s_own] = max(snap.get(s_own, 0), o.cpos + 1)
            vc[o.idx] = snap
        for o in ops:
            for s, v in waits[o.idx]:
                if s[0] == "eng":
                    ops[by_eng[s[1]][v - 1]].marked = True
        cnt = {e: 0 for e in COMPUTE}
        for o in ops:
            if (not o.is_dma) and o.marked:
                cnt[o.eng] += 1
                o.token = (("eng", o.eng), cnt[o.eng])

        def semh(s):
            if s[0] == "eng":
                return self.sems[s[1]]
            return self.dma_sems[s[1]][s[2]]

        nwaits = 0
        for o in ops:
            engh = self.engs[o.eng]
            for s, v in waits[o.idx]:
                if s[0] == "eng":
                    v = ops[by_eng[s[1]][v - 1]].token[1]
                engh.wait_ge(semh(s), v)
                nwaits += 1
            inst = o.fn()
            if o.token is not None:
                if o.is_dma:
                    inst.then_inc(semh(o.token[0]), 16)
                else:
                    inst.then_inc(semh(o.token[0]), 1)
        sp = self.engs["sp"]
        for (q, i), v in dma_val.items():
            if v > 0:
                sp.wait_ge(semh(("dma", q, i)), v)
        return dict(n_ops=len(ops), n_waits=nwaits, counts=per_eng_count, sem_counts=cnt)

T = 2048
D = 1024
KC = 8
NT = 16
NB = 4
TB = 512
DFF = 2816
EPS = 1e-6
CFG = dict(depth=4, mixers=True, ffn=True, dbg=False, l0=0)

R_BADA = 0
R_N1 = 192
R_N2 = 224
R_NF = 256
R_GN = 264
R_LB = 280
N_VROWS = 384
PC_CARRY = 0
PC_CTXB = 1
PC_MASK = 2
N_PCS = 2 + 64


def build_nc(cfg):
    nc = bass.Bass("TRN2", target_bir_lowering=False)
    depth = cfg["depth"]

    def din(name, shape, dt=F32):
        return nc.dram_tensor(name, list(shape), dt, kind="ExternalInput").ap()

    def dout(name, shape, dt=F32):
        return nc.dram_tensor(name, list(shape), dt, kind="ExternalOutput").ap()

    x_in = din("x", [T, D])
    cond_in = din("cond", [128, KC])
    vecs_in = din("vecs", [N_VROWS, 128])
    cmat_in = din("cmat", [5, 128, 128])
    pcs_in = din("pcs", [128, N_PCS])
    rope_in = din("rope", [NT, 128, 256])
    ck_in = din("ck", [2, 512, 256])
    cv_in = din("cv", [2, 512, 256])
    st0_in = din("st0", [2, 2, 8, 128, 128])
    sink_in = din("sinkb", [128, 16])
    bm_in = din("bm", [128, 4])
    w_ada = din("w_ada", [4, D, 6 * D])
    w_gu = din("w_gate_up", [4, D, 2 * DFF])
    w_dn = din("w_down", [4, DFF, D])
    w_ina = din("w_in_a", [2, D, 5 * D])
    w_outa = din("w_out_a", [2, D, D])
    w_qkv = din("w_qkv_b", [2, D, 1536])
    w_outb = din("w_out_b", [2, D, D])
    y_out = dout("y", [T, D])
    nk_out = dout("nk", [2, T, 256])
    nv_out = dout("nv", [2, T, 256])
    ns_out = dout("ns", [2, 2, 8, 8, 128, 128])

    with contextlib.ExitStack() as es:
        S = Sched(nc, es)

        def sb(name, shape, dt):
            return es.enter_context(nc.sbuf_tensor("sb_" + name, list(shape), dt))

        uniq = [0]

        def pl(ph, name, shape, dt):
            uniq[0] += 1
            return ph.enter_context(nc.sbuf_tensor("pl%d_%s" % (uniq[0], name), list(shape), dt))

        xT = sb("xT", [128, KC, T], F32)
        hT = sb("hT", [128, KC, T], BF16)
        ring = [sb("ring%d" % i, [128, 4096], BF16) for i in range(4)]
        vecT = sb("vecT", [128, N_VROWS], F32)
        idf = sb("idf", [128, 128], F32)
        idb = sb("idb", [128, 128], BF16)
        tri = sb("tri", [128, 2, 128], BF16)
        hmk = sb("hmk", [128, 2, 128], BF16)
        ones1 = sb("ones1", [128, 128], BF16)
        onesd = sb("onesd", [128, 128], BF16)
        onese = sb("onese", [128, 128], BF16)
        pcs = sb("pcs", [128, N_PCS], F32)
        bmb = sb("bmb", [128, 4], BF16)
        esink = sb("esink", [128, 16], F32)
        scb = sb("scb", [128, KC], BF16)
        condf = sb("condf", [128, KC], F32)
        modall = sb("modall", [128, 4, 64], F32)
        lbt = sb("lbt", [128, 2, 16], F32)
        omlt = sb("omlt", [128, 2, 16], F32)
        nomlt = sb("nomlt", [128, 2, 16], F32)
        ps = [es.enter_context(nc.psum_tensor("ps%d" % i, [128, 512], F32)) for i in range(8)]
        psb = [p[:].bitcast(BF16) for p in ps]

        ring_i = [0]

        def ring_next():
            i = ring_i[0]
            ring_i[0] = (i + 1) % 4
            return ring[i], ("ring", i)

        def MM(out, lhsT, rhs, start, stop, r, w, **kw):
            S.op("pe", lambda: nc.tensor.matmul(out, lhsT, rhs, start=start, stop=stop, **kw), r, w)

        def TR(out, in_, ident, r, w):
            S.op("pe", lambda: nc.tensor.transpose(out, in_, ident), r, w)

        def ACT(out, in_, func, r, w, bias=None, scale=None):
            kw = {}
            if bias is not None:
                kw["bias"] = bias
            if scale is not None:
                kw["scale"] = scale
            S.op("act", lambda: nc.scalar.activation(out=out, in_=in_, func=func, **kw), r, w)

        def ENG(e):
            return {"dve": nc.vector, "pool": nc.gpsimd}[e]

        def TT(e, out, in0, in1, op, r, w):
            S.op(e, lambda: ENG(e).tensor_tensor(out=out, in0=in0, in1=in1, op=op), r, w)

        def TS(e, out, in0, s1, s2, op0, op1, r, w):
            if s2 is None:
                S.op(e, lambda: ENG(e).tensor_scalar(out=out, in0=in0, scalar1=s1, scalar2=None, op0=op0), r, w)
            else:
                S.op(e, lambda: ENG(e).tensor_scalar(out=out, in0=in0, scalar1=s1, scalar2=s2, op0=op0, op1=op1), r, w)

        def STT(e, out, in0, scalar, in1, op0, op1, r, w):
            S.op(e, lambda: ENG(e).scalar_tensor_tensor(out=out, in0=in0, scalar=scalar, in1=in1, op0=op0, op1=op1), r, w)

        def CP(e, out, in_, r, w):
            if e == "act":
                S.op("act", lambda: nc.scalar.copy(out=out, in_=in_), r, w)
            else:
                S.op(e, lambda: ENG(e).tensor_copy(out=out, in_=in_), r, w)

        def MEMSET(e, out, val, w):
            S.op(e, lambda: ENG(e).memset(out, val), (), w)

        def DMA(q, out, in_, r, w, is_output=False, scoped=True):
            eng = {"sp": nc.sync, "pool": nc.gpsimd, "act": nc.scalar}[q]
            S.dma(q, lambda: eng.dma_start(out=out, in_=in_), r, w, is_output=is_output, scoped=scoped)

        def tbs(tb):
            return slice(tb * TB, (tb + 1) * TB)

        with contextlib.ExitStack() as ph:
            vtmp = pl(ph, "vtmp", [128, 3, 128], F32)
            DMA("sp", idf[:], cmat_in[0], (), ["idf"])
            DMA("pool", idb[:], cmat_in[0], (), ["idb"])
            DMA("pool", tri[:, 0, :], cmat_in[1], (), ["tri"])
            DMA("pool", tri[:, 1, :], cmat_in[2], (), ["tri"])
            DMA("pool", hmk[:, 0, :], cmat_in[3], (), ["hmk"])
            DMA("pool", hmk[:, 1, :], cmat_in[4], (), ["hmk"])
            DMA("sp", pcs[:], pcs_in, (), ["pcs"])
            DMA("pool", bmb[:], bm_in, (), ["bmb"])
            DMA("sp", esink[:], sink_in, (), ["esink"])
            DMA("sp", condf[:], cond_in, (), ["condf"])
            for i in range(3):
                DMA("sp", vtmp[:, i, :], vecs_in[i * 128:(i + 1) * 128, :], (), [("vtmp", i)])
            MEMSET("dve", ones1[:], 1.0, ["ones1"])
            MEMSET("dve", onesd[:], 1.0 / D, ["onesd"])
            MEMSET("dve", onese[:], 1.0 / 128, ["onese"])
            for i in range(3):
                TR(ps[0][:, i * 128:(i + 1) * 128], vtmp[:, i, :], idf[:], [("vtmp", i), "idf"], [("ps", 0)])
            CP("dve", vecT[:], ps[0][:, 0:N_VROWS], [("ps", 0)], ["vecT"])
            ACT(esink[:], esink[:], AF.Exp, ["esink"], ["esink"])
            ACT(scb[:], condf[:], AF.Silu, ["condf"], ["scb"])
            MEMSET("dve", lbt[:, 0, :], 0.0, ["lbt"])
            TT("dve", lbt[:, 1, :], vecT[:, R_LB + 16:R_LB + 32], vecT[:, R_LB:R_LB + 16], ALU.subtract, ["vecT", "lbt"], ["lbt"])
            ACT(lbt[:, 1, :], lbt[:, 1, :], AF.Sigmoid, ["lbt"], ["lbt"])
            TS("dve", omlt[:], lbt[:], -1.0, 1.0, ALU.mult, ALU.add, ["lbt"], ["omlt"])
            TS("dve", nomlt[:], omlt[:], -1.0, None, ALU.mult, None, ["omlt"], ["nomlt"])

            xtok = [pl(ph, "xtok%d" % i, [128, D], F32) for i in range(2)]
            for t in range(NT):
                xt = xtok[t % 2]
                DMA("sp", xt[:], x_in[t * 128:(t + 1) * 128, :], (), [("xtok", t % 2)])
                for hf in range(2):
                    pb = (2 * t + hf) % 4
                    for c4 in range(4):
                        c = hf * 4 + c4
                        TR(ps[pb][:, c4 * 128:(c4 + 1) * 128], xt[:, c * 128:(c + 1) * 128], idf[:],
                           [("xtok", t % 2), "idf"], [("ps", pb)])
                    dst = xT[:, hf * 4:(hf + 1) * 4, t * 128:(t + 1) * 128]
                    src = ps[pb][:].rearrange("p (c t) -> p c t", c=4)
                    wk = [("x", hf * 4 + c4, t // 4) for c4 in range(4)]
                    if hf == 0:
                        CP("dve", dst, src, [("ps", pb)], wk)
                    else:
                        CP("act", dst, src, [("ps", pb)], wk)
        S.barrier()

        def adaln_gen(l):
            for s in range(12):
                slot, sk = ring_next()
                sv = slot[:, 0:4096].rearrange("p (k f) -> p k f", k=8)
                DMA("pool", sv, w_ada[l].rearrange("(k p) f -> p k f", p=128)[:, :, s * 512:(s + 1) * 512],
                    (), [sk], scoped=False)
                for f in range(4):
                    col = s * 4 + f
                    for kc in range(KC):
                        MM(ps[7][:, col:col + 1], sv[:, kc, f * 128:(f + 1) * 128], scb[:, kc:kc + 1],
                           kc == 0, kc == KC - 1, [sk, "scb"], [("ps", 7)])
                yield s
            mk = ("mod", l)
            TT("dve", modall[:, l, 0:48], ps[7][:, 0:48], vecT[:, R_BADA + l * 48:R_BADA + (l + 1) * 48], ALU.add,
               [("ps", 7), "vecT"], [mk])
            STT("dve", modall[:, l, 48:56], modall[:, l, 8:16], 1.0, vecT[:, R_N1 + l * 8:R_N1 + (l + 1) * 8],
                ALU.add, ALU.mult, [mk, "vecT"], [mk])
            STT("dve", modall[:, l, 56:64], modall[:, l, 32:40], 1.0, vecT[:, R_N2 + l * 8:R_N2 + (l + 1) * 8],
                ALU.add, ALU.mult, [mk, "vecT"], [mk])
            yield 12

        def adaln(l):
            for _ in adaln_gen(l):
                pass

        ada_next = [None]

        def ada_step():
            g = ada_next[0]
            if g is not None:
                try:
                    next(g)
                except StopIteration:
                    ada_next[0] = None

        def norm_mod(l, which, ext=None):
            mk = ("mod", l)
            Acol = 48 if which == 0 else 56
            Bcol = 0 if which == 0 else 24
            with contextlib.ExitStack() as ph_own:
                ph = ext if ext is not None else ph_own
                sq = [pl(ph, "sq%d" % i, [128, KC, TB], BF16) for i in range(2)]
                rstd = [pl(ph, "rstd%d" % i, [128, TB], F32) for i in range(2)]
                tmpn = [pl(ph, "tmpn%d" % i, [128, TB], F32) for i in range(2)]
                for tb in range(NB):
                    q = sq[tb % 2]
                    rs = rstd[tb % 2]
                    for c in range(KC):
                        if c % 2 == 0:
                            ACT(q[:, c, :], xT[:, c, tbs(tb)], AF.Square, [("x", c, tb)], [("sq", tb % 2, c)])
                        else:
                            TT("dve", q[:, c, :], xT[:, c, tbs(tb)], xT[:, c, tbs(tb)], ALU.mult, [("x", c, tb)], [("sq", tb % 2, c)])
                    for c in range(KC):
                        MM(ps[6][:], onesd[:], q[:, c, :], c == 0, c == KC - 1, [("sq", tb % 2, c), "onesd"], [("ps", 6)])
                    ACT(rs[:], ps[6][:], AF.Ln, [("ps", 6)], [("rstd", tb % 2)], bias=EPS)
                    ACT(rs[:], rs[:], AF.Exp, [("rstd", tb % 2)], [("rstd", tb % 2)], scale=-0.5)
                    for c in range(KC):
                        tm = tmpn[c % 2]
                        STT("dve", tm[:], xT[:, c, tbs(tb)], modall[:, l, Acol + c:Acol + c + 1], rs[:], ALU.mult, ALU.mult,
                            [("x", c, tb), mk, ("rstd", tb % 2)], [("tmpn", c % 2)])
                        ACT(hT[:, c, tbs(tb)], tm[:], AF.Identity, [("tmpn", c % 2), mk], [("h", c, tb)],
                            bias=modall[:, l, Bcol + c:Bcol + c + 1])
            if ext is None:
                S.barrier()

        def ffn(l):
            mk = ("mod", l)
            with contextlib.ExitStack() as ph:
                norm_mod(l, 1, ext=ph)
                actT = pl(ph, "actT", [128, 11, T], BF16)
                sg = [pl(ph, "sg%d" % i, [128, TB], F32) for i in range(2)]
                k = 0
                kd = 0
                for half in range(2):
                    for (g0, gn) in ((0, 4), (4, 4), (8, 3)):
                        ada_step()
                        fc0 = half * 11 + g0
                        sl_g, kg = ring_next()
                        vg = sl_g[:, 0:4096].rearrange("p (k f) -> p k f", k=8)
                        src = w_gu[l].rearrange("(k p) f -> p k f", p=128)
                        DMA("pool", vg[:, :, 0:gn * 128], src[:, :, fc0 * 128:(fc0 + gn) * 128], (), [kg], scoped=False)
                        sl_u, ku = ring_next()
                        vu = sl_u[:, 0:4096].rearrange("p (k f) -> p k f", k=8)
                        DMA("pool", vu[:, :, 0:gn * 128], src[:, :, DFF + fc0 * 128:DFF + (fc0 + gn) * 128], (), [ku], scoped=False)
                        for tb in range(NB):
                            for j in range(gn):
                                pg = k % 2
                                pu = 2 + k % 2
                                k += 1
                                for kc in range(KC):
                                    MM(ps[pg][:], vg[:, kc, j * 128:(j + 1) * 128], hT[:, kc, tbs(tb)], kc == 0, kc == KC - 1,
                                       [kg, ("h", kc, tb)], [("ps", pg)])
                                for kc in range(KC):
                                    MM(ps[pu][:], vu[:, kc, j * 128:(j + 1) * 128], hT[:, kc, tbs(tb)], kc == 0, kc == KC - 1,
                                       [ku, ("h", kc, tb)], [("ps", pu)])
                                s_ = sg[k % 2]
                                ACT(s_[:], ps[pg][:], AF.Silu, [("ps", pg)], [("sg", k % 2)])
                                TT("dve", actT[:, g0 + j, tbs(tb)], s_[:], ps[pu][:], ALU.mult,
                                   [("sg", k % 2), ("ps", pu)], [("act", g0 + j, tb)])
                    for s in range(4):
                        ada_step()
                        sl_d, kdk = ring_next()
                        vd = sl_d[:, 0:11 * 256].rearrange("p (k f) -> p k f", k=11)
                        srcd = w_dn[l][half * 11 * 128:(half + 1) * 11 * 128, :].rearrange("(k p) f -> p k f", p=128)
                        DMA("pool", vd, srcd[:, :, s * 256:(s + 1) * 256], (), [kdk], scoped=False)
                        for tb in range(NB):
                            for dd in range(2):
                                dm = 2 * s + dd
                                pd = 4 + kd % 2
                                kd += 1
                                for fl in range(11):
                                    MM(ps[pd][:], vd[:, fl, dd * 128:(dd + 1) * 128], actT[:, fl, tbs(tb)], fl == 0, fl == 10,
                                       [kdk, ("act", fl, tb)], [("ps", pd)])
                                STT("dve", xT[:, dm, tbs(tb)], ps[pd][:], modall[:, l, 40 + dm:41 + dm], xT[:, dm, tbs(tb)],
                                    ALU.mult, ALU.add, [("ps", pd), mk, ("x", dm, tb)], [("x", dm, tb)])
            S.barrier()

        def final_out():
            with contextlib.ExitStack() as ph:
                sq = [pl(ph, "fsq%d" % i, [128, KC, TB], BF16) for i in range(2)]
                rstd = [pl(ph, "frstd%d" % i, [128, TB], F32) for i in range(2)]
                ynT = [pl(ph, "ynT%d" % i, [128, KC, TB], F32) for i in range(2)]
                ytok = [pl(ph, "ytok%d" % i, [128, D], F32) for i in range(2)]
                kk = 0
                for tb in range(NB):
                    q = sq[tb % 2]
                    rs = rstd[tb % 2]
                    yn = ynT[tb % 2]
                    for c in range(KC):
                        if c % 2 == 0:
                            ACT(q[:, c, :], xT[:, c, tbs(tb)], AF.Square, [("x", c, tb)], [("sq", tb % 2, c)])
                        else:
                            TT("dve", q[:, c, :], xT[:, c, tbs(tb)], xT[:, c, tbs(tb)], ALU.mult, [("x", c, tb)], [("sq", tb % 2, c)])
                    for c in range(KC):
                        MM(ps[6][:], onesd[:], q[:, c, :], c == 0, c == KC - 1, [("sq", tb % 2, c), "onesd"], [("ps", 6)])
                    ACT(rs[:], ps[6][:], AF.Ln, [("ps", 6)], [("rstd", tb % 2)], bias=EPS)
                    ACT(rs[:], rs[:], AF.Exp, [("rstd", tb % 2)], [("rstd", tb % 2)], scale=-0.5)
                    for c in range(KC):
                        STT("dve", yn[:, c, :], xT[:, c, tbs(tb)], vecT[:, R_NF + c:R_NF + c + 1], rs[:], ALU.mult, ALU.mult,
                            [("x", c, tb), "vecT", ("rstd", tb % 2)], [("yn", tb % 2, c)])
                    for t4 in range(4):
                        t = tb * 4 + t4
                        yt = ytok[kk % 2]
                        for hf in range(2):
                            pb = (2 * kk + hf) % 4
                            for c4 in range(4):
                                c = hf * 4 + c4
                                TR(ps[pb][:, c4 * 128:(c4 + 1) * 128], yn[:, c, t4 * 128:(t4 + 1) * 128], idf[:],
                                   [("yn", tb % 2, c), "idf"], [("ps", pb)])
                            if hf == 0:
                                CP("dve", yt[:, 0:512], ps[pb][:], [("ps", pb)], [("ytok", kk % 2, 0)])
                            else:
                                CP("act", yt[:, 512:1024], ps[pb][:], [("ps", pb)], [("ytok", kk % 2, 1)])
                        DMA("sp", y_out[t * 128:(t + 1) * 128, :], yt[:], [("ytok", kk % 2, 0), ("ytok", kk % 2, 1)], [],
                            is_output=True)
                        kk += 1
            S.barrier()


        def attn(l):
            j = l // 2
            mk = ("mod", l)
            SC = 1.0 / (128.0 ** 0.5)
            with contextlib.ExitStack() as ph:
                kctxT = pl(ph, "kctxT", [128, 2, 512], BF16)
                vctx = pl(ph, "vctx", [128, 4, 256], BF16)
                kctok = pl(ph, "kctok", [128, 4, 256], BF16)
                qoT = pl(ph, "qoT", [128, 4, T], BF16)
                kT = pl(ph, "kT", [128, T], BF16)
                vtok = pl(ph, "vtok", [128, NT, 128], BF16)
                ropet = [pl(ph, "ropet%d" % i, [128, 256], F32) for i in range(2)]
                qkf = [pl(ph, "qkf%d" % i, [128, 768], F32) for i in range(2)]
                r1 = [pl(ph, "r1%d" % i, [128, 640], F32) for i in range(2)]
                r2 = [pl(ph, "r2%d" % i, [128, 640], F32) for i in range(2)]
                rb = [pl(ph, "rb%d" % i, [128, 640], BF16) for i in range(2)]
                pT = [pl(ph, "pT%d" % i, [128, 7, 512], BF16) for i in range(2)]
                mskt = [pl(ph, "mskt%d" % i, [128, 128], BF16) for i in range(2)]
                den2 = [pl(ph, "den%d" % i, [128, 512], F32) for i in range(2)]

                DMA("pool", kctok[:], ck_in[j].rearrange("(t p) f -> p t f", p=128), (), ["kctok"])
                DMA("pool", vctx[:], cv_in[j].rearrange("(t p) f -> p t f", p=128), (), ["vctx"])
                for kvh in range(2):
                    for t in range(4):
                        TR(psb[kvh][:, t * 128:(t + 1) * 128], kctok[:, t, kvh * 128:(kvh + 1) * 128], idb[:],
                           ["kctok", "idb"], [("ps", kvh)])
                    CP("dve", kctxT[:, kvh, :], psb[kvh][:, 0:512], [("ps", kvh)], [("kctx", kvh)])

                kk = 0
                mi = 0
                for g in range(2 if cfg.get("att_stop", 9) > 0 else 0):
                    slq, kq = ring_next()
                    vq = slq[:, 0:4096].rearrange("p (k f) -> p k f", k=8)
                    wsrc = w_qkv[j].rearrange("(k p) f -> p k f", p=128)
                    DMA("pool", vq, wsrc[:, :, g * 512:(g + 1) * 512], (), [kq], scoped=False)
                    slkv, kkv = ring_next()
                    vkv = slkv[:, 0:4096].rearrange("p (k f) -> p k f", k=8)
                    DMA("pool", vkv[:, :, 0:128], wsrc[:, :, 1024 + g * 128:1024 + (g + 1) * 128], (), [kkv], scoped=False)
                    DMA("pool", vkv[:, :, 128:256], wsrc[:, :, 1280 + g * 128:1280 + (g + 1) * 128], (), [kkv], scoped=False)
                    def stage1_b(t):
                        tsl = slice(t * 128, (t + 1) * 128)
                        b2 = t % 2
                        pb = 4 + b2
                        for hh in range(5):
                            TR(psb[pb][:, hh * 128:(hh + 1) * 128], rb[b2][:, hh * 128:(hh + 1) * 128], idb[:],
                               [("rb", b2), "idb"], [("ps", pb)])
                        CP("act", qoT[:, :, tsl], psb[pb][:, 0:512].rearrange("p (h t) -> p h t", h=4), [("ps", pb)], [("qo", t)])
                        CP("act", kT[:, tsl], psb[pb][:, 512:640], [("ps", pb)], [("kT", t)])

                    for t in range(NT):
                        tsl = slice(t * 128, (t + 1) * 128)
                        b2 = t % 2
                        pq = ps[b2]
                        pkv = ps[2 + b2]
                        for kc in range(KC):
                            MM(pq[:], hT[:, kc, tsl], vq[:, kc, :], kc == 0, kc == KC - 1, [("h", kc, t // 4), kq], [("ps", b2)])
                        for kc in range(KC):
                            MM(pkv[:, 0:256], hT[:, kc, tsl], vkv[:, kc, 0:256], kc == 0, kc == KC - 1,
                               [("h", kc, t // 4), kkv], [("ps", 2 + b2)])
                        DMA("sp", ropet[b2][:], rope_in[t], (), [("ropet", b2)])
                        CP("act", qkf[b2][:, 0:512], pq[:], [("ps", b2)], [("qkf", b2, 0)])
                        CP("act", qkf[b2][:, 512:768], pkv[:, 0:256], [("ps", 2 + b2)], [("qkf", b2, 1)])
                        CP("dve", vtok[:, t, :], qkf[b2][:, 640:768], [("qkf", b2, 1)], [("vt", t)])
                        DMA("sp", nk_out[j, tsl, g * 128:(g + 1) * 128], qkf[b2][:, 512:640], [("qkf", b2, 1)], [], is_output=True)
                        DMA("sp", nv_out[j, tsl, g * 128:(g + 1) * 128], qkf[b2][:, 640:768], [("qkf", b2, 1)], [], is_output=True)
                        x3 = qkf[b2][:, 0:640].rearrange("p (h d) -> p h d", h=5)
                        cosb = ropet[b2][:, 0:128].unsqueeze(1).to_broadcast([128, 5, 128])
                        TT("dve", r1[b2][:].rearrange("p (h d) -> p h d", h=5), x3, cosb, ALU.mult,
                           [("qkf", b2, 0), ("qkf", b2, 1), ("ropet", b2)], [("r1", b2)])
                        x5 = qkf[b2][:, 0:640].rearrange("p (h a s i) -> p h a s i", h=5, a=2, s=2)
                        o5 = r2[b2][:].rearrange("p (h a s i) -> p h a s i", h=5, a=2, s=2)
                        s4 = ropet[b2][:, 128:256].rearrange("p (a s i) -> p a s i", a=2, s=2)
                        for sidx in range(2):
                            sinb = s4[:, :, sidx, :].unsqueeze(1).to_broadcast([128, 5, 2, 32])
                            TT("dve", o5[:, :, :, sidx, :], x5[:, :, :, 1 - sidx, :], sinb, ALU.mult,
                               [("qkf", b2, 0), ("qkf", b2, 1), ("ropet", b2)], [("r2", b2, sidx)])
                        TT("dve", rb[b2][:], r1[b2][:], r2[b2][:], ALU.add, [("r1", b2), ("r2", b2, 0), ("r2", b2, 1)], [("rb", b2)])
                        if t > 0:
                            stage1_b(t - 1)
                    stage1_b(NT - 1)
                    def scores(n):
                        nonlocal kk, mi
                        nsl = slice(n * 128, (n + 1) * 128)
                        chunks = [("ctx", t) for t in range(4)]
                        if n > 0:
                            chunks.append(("prev", n - 1))
                        chunks.append(("cen", n))
                        if n < NT - 1:
                            chunks.append(("next", n + 1))
                        pt = pT[n % 2]
                        for ci, (kind, idx) in enumerate(chunks):
                            pS = 4 + kk % 4
                            kk += 1
                            if kind == "ctx":
                                lhs = kctxT[:, g, idx * 128:(idx + 1) * 128]
                                rk = [("kctx", g)]
                            else:
                                lhs = kT[:, idx * 128:(idx + 1) * 128]
                                rk = [("kT", idx)]
                            MM(ps[pS][:], lhs, qoT[:, :, nsl], True, True, rk + [("qo", n)], [("ps", pS)])
                            pk = ("pT", n % 2, ci)
                            if kind == "ctx":
                                ACT(pt[:, ci, :], ps[pS][:], AF.Exp, [("ps", pS), "pcs"], [pk], scale=SC,
                                    bias=pcs[:, PC_CTXB:PC_CTXB + 1])
                            else:
                                ACT(pt[:, ci, :], ps[pS][:], AF.Exp, [("ps", pS)], [pk], scale=SC)
                            if kind in ("prev", "next"):
                                side = 0 if kind == "prev" else 1
                                base = PC_MASK + (n * 2 + side) * 2
                                mt = mskt[mi % 2]
                                mkk = ("mskt", mi % 2)
                                mi += 1
                                TS("dve", mt[:], tri[:, side, :], pcs[:, base:base + 1], pcs[:, base + 1:base + 2],
                                   ALU.mult, ALU.add, ["tri", "pcs"], [mkk])
                                pv = pt[:, ci, :].rearrange("p (h t) -> p h t", h=4)
                                TT("dve", pv, pv, mt[:].unsqueeze(1).to_broadcast([128, 4, 128]), ALU.mult, [pk, mkk], [pk])
                            yield chunks

                    def pv_norm(n, chunks):
                        nsl = slice(n * 128, (n + 1) * 128)
                        pt = pT[n % 2]
                        nch = len(chunks)
                        bO = n % 2
                        bD = 2 + n % 2
                        dn = den2[n % 2]
                        dk = ("den", n % 2)
                        for ci, (kind, idx) in enumerate(chunks):
                            pk = ("pT", n % 2, ci)
                            if kind == "ctx":
                                vl = vctx[:, idx, g * 128:(g + 1) * 128]
                                rv = ["vctx"]
                            else:
                                vl = vtok[:, idx, :]
                                rv = [("vt", idx)]
                            MM(ps[bO][:], vl, pt[:, ci, :], ci == 0, ci == nch - 1, rv + [pk], [("ps", bO)])
                            MM(ps[bD][:], ones1[:], pt[:, ci, :], ci == 0, ci == nch - 1, ["ones1", pk], [("ps", bD)])
                            yield ci
                        d3 = dn[:].rearrange("p (h t) -> p h t", h=4)
                        es4 = esink[:, j * 8 + g * 4:j * 8 + g * 4 + 4].unsqueeze(2).to_broadcast([128, 4, 128])
                        TT("dve", d3, ps[bD][:].rearrange("p (h t) -> p h t", h=4), es4, ALU.add, [("ps", bD), "esink"], [dk])
                        ACT(dn[:], dn[:], AF.Ln, [dk], [dk])
                        ACT(dn[:], dn[:], AF.Exp, [dk], [dk], scale=-1.0)
                        TT("dve", qoT[:, :, nsl], ps[bO][:].rearrange("p (h t) -> p h t", h=4), d3, ALU.mult,
                           [("ps", bO), dk], [("qo", n)])

                    def chunk_list(n):
                        chunks = [("ctx", t) for t in range(4)]
                        if n > 0:
                            chunks.append(("prev", n - 1))
                        chunks.append(("cen", n))
                        if n < NT - 1:
                            chunks.append(("next", n + 1))
                        return chunks

                    def drain(gen):
                        for _ in gen:
                            pass

                    if cfg.get("att_stop", 9) > 1:
                        drain(scores(0))
                        for n in range(NT):
                            gs_ = scores(n + 1) if n + 1 < NT else None
                            gp_ = pv_norm(n, chunk_list(n))
                            while gs_ is not None or gp_ is not None:
                                if gs_ is not None:
                                    try:
                                        next(gs_)
                                    except StopIteration:
                                        gs_ = None
                                if gp_ is not None:
                                    try:
                                        next(gp_)
                                    except StopIteration:
                                        gp_ = None
                    slo, ko = ring_next()
                    vo = slo[:, 0:4096].rearrange("p (k f) -> p k f", k=4)
                    DMA("pool", vo, w_outb[j][g * 512:(g + 1) * 512, :].rearrange("(k p) f -> p k f", p=128), (), [ko], scoped=False)
                    ko_i = 0
                    for tb in range(NB if cfg.get("att_stop", 9) > 2 else 0):
                        for dm in range(KC):
                            pd = ko_i % 2
                            ko_i += 1
                            for hh in range(4):
                                MM(ps[pd][:], vo[:, hh, dm * 128:(dm + 1) * 128], qoT[:, hh, tbs(tb)], hh == 0, hh == 3,
                                   [ko] + [("qo", 4 * tb + q) for q in range(4)], [("ps", pd)])
                            STT("dve", xT[:, dm, tbs(tb)], ps[pd][:], modall[:, l, 16 + dm:17 + dm], xT[:, dm, tbs(tb)],
                                ALU.mult, ALU.add, [("ps", pd), mk, ("x", dm, tb)], [("x", dm, tb)])
            S.barrier()

        def rev_ap(t, n):
            return bass.AP(t, n - 1, [[n, 128], [-1, n]])

        def hgrn(l):
            j = l // 2
            mk = ("mod", l)
            with contextlib.ExitStack() as ph:
                ogT = pl(ph, "ogT", [128, 2, T], BF16)
                qd = [pl(ph, "qd%d" % i, [128, T], BF16) for i in range(2)]
                kdT = [pl(ph, "kdT%d" % i, [128, T], BF16) for i in range(2)]
                kdtok = [pl(ph, "kdtok%d" % i, [128, NT, 128], BF16) for i in range(2)]
                vtok = pl(ph, "hvtok", [128, NT, 128], BF16)
                oacc = pl(ph, "oacc", [128, T], F32)
                sig = [pl(ph, "sig%d" % i, [128, TB], F32) for i in range(2)]
                lf = [pl(ph, "lf%d" % i, [128, TB], F32) for i in range(2)]
                ebt = pl(ph, "ebt", [128, TB], F32)
                enbt = pl(ph, "enbt", [128, TB], F32)
                qs = pl(ph, "qs", [128, TB], F32)
                osq = pl(ph, "osq", [128, TB], BF16)
                segm = [pl(ph, "segm%d" % i, [128, TB], F32) for i in range(2)]
                Dor = [pl(ph, "Dor%d" % i, [128, 64], F32) for i in range(2)]
                Dca = [pl(ph, "Dca%d" % i, [128, 64], F32) for i in range(2)]
                Tst = [[pl(ph, "Tst%d_%d" % (d, i), [128, 128], F32) for i in range(2)] for d in range(2)]
                Sbf = [[pl(ph, "Sbf%d_%d" % (d, i), [128, 128], BF16) for i in range(2)] for d in range(2)]
                sfin = [pl(ph, "sfin%d" % i, [128, 128], F32) for i in range(2)]
                amask = [[pl(ph, "am%d_%d" % (d, i), [128, 128], BF16) for i in range(2)] for d in range(2)]
                vblk = [pl(ph, "vblk%d" % i, [128, 2, 4, 128], BF16) for i in range(2)]

                MEMSET("dve", segm[0][:], 1.0, ["segm0"])
                MEMSET("dve", segm[1][:], 1.0, ["segm1"])
                MEMSET("dve", segm[0][:].rearrange("p (c i) -> p c i", i=32)[:, :, 0:1], 0.0, ["segm0"])
                MEMSET("dve", segm[1][:].rearrange("p (c i) -> p c i", i=32)[:, :, 31:32], 0.0, ["segm1"])
                sf_i = 0
                for h in range(8):
                    slA, kA = ring_next()
                    vA = slA[:, 0:4096].rearrange("p (k f) -> p k f", k=8)
                    wsrc = w_ina[j].rearrange("(k p) f -> p k f", p=128)
                    for gi, c0 in enumerate((0, 1024, 2048, 4096)):
                        DMA("pool", vA[:, :, gi * 128:(gi + 1) * 128], wsrc[:, :, c0 + h * 128:c0 + (h + 1) * 128], (), [kA], scoped=False)
                    slB, kB = ring_next()
                    vB = slB[:, 0:4096].rearrange("p (k f) -> p k f", k=8)
                    DMA("pool", vB[:, :, 0:128], wsrc[:, :, 3072 + h * 128:3072 + (h + 1) * 128], (), [kB], scoped=False)
                    for d in range(2):
                        DMA("sp", Tst[d][1][:], st0_in[j, d, h], (), [("T", d, 1)])
                    def kd_tr(tb):
                        for d in range(2):
                            pb = 6 + d
                            for ti in range(4):
                                t = tb * 4 + ti
                                TR(psb[pb][:, ti * 128:(ti + 1) * 128], kdT[d][:, t * 128:(t + 1) * 128], idb[:],
                                   [("kdT", d, tb), "idb"], [("ps", pb)])
                            CP("dve", kdtok[d][:, tb * 4:(tb + 1) * 4, :], psb[pb][:, 0:512].rearrange("p (t e) -> p t e", t=4),
                               [("ps", pb)], [("kdt", d, tb * 4 + q) for q in range(4)])

                    for tb in range(NB):
                        for gi in range(3):
                            for kc in range(KC):
                                MM(ps[gi][:], vA[:, kc, gi * 128:(gi + 1) * 128], hT[:, kc, tbs(tb)], kc == 0, kc == KC - 1,
                                   [kA, ("h", kc, tb)], [("ps", gi)])
                        if tb > 0:
                            kd_tr(tb - 1)
                        ACT(qs[:], ps[0][:], AF.Sigmoid, [("ps", 0)], ["qs"])
                        for d in range(2):
                            ACT(sig[d][:], ps[1 + d][:], AF.Sigmoid, [("ps", 1 + d)], [("sig", d)])
                        TT("dve", qs[:], qs[:], ps[0][:], ALU.mult, ["qs", ("ps", 0)], ["qs"])
                        for ti in range(4):
                            t = tb * 4 + ti
                            for kc in range(KC):
                                MM(ps[3][:, ti * 128:(ti + 1) * 128], hT[:, kc, t * 128:(t + 1) * 128], vB[:, kc, 0:128],
                                   kc == 0, kc == KC - 1, [kB, ("h", kc, tb)], [("ps", 3)])
                        CP("dve", vtok[:, tb * 4:(tb + 1) * 4, :], ps[3][:].rearrange("p (t e) -> p t e", t=4), [("ps", 3)],
                           [("hvt", tb * 4 + q) for q in range(4)])
                        for d in range(2):
                            col = d * 8 + h
                            ACT(lf[d][:], sig[d][:], AF.Ln, [("sig", d), "lbt", "omlt"], [("lf", d)],
                                scale=omlt[:, j, col:col + 1], bias=lbt[:, j, col:col + 1])
                        for d in range(2):
                            if d == 0:
                                S.op("dve", lambda: nc.vector.tensor_tensor_scan(
                                    out=lf[0][:], data0=segm[0][:], data1=lf[0][:], initial=0.0, op0=ALU.mult, op1=ALU.add),
                                    [("lf", 0), "segm0"], [("lf", 0)])
                            else:
                                S.op("dve", lambda: nc.vector.tensor_tensor_scan(
                                    out=rev_ap(lf[1], TB), data0=rev_ap(segm[1], TB), data1=rev_ap(lf[1], TB), initial=0.0,
                                    op0=ALU.mult, op1=ALU.add), [("lf", 1), "segm1"], [("lf", 1)])
                        for d in range(2):
                            col = d * 8 + h
                            ACT(ebt[:], lf[d][:], AF.Exp, [("lf", d)], ["ebt"])
                            ACT(enbt[:], lf[d][:], AF.Exp, [("lf", d)], ["enbt"], scale=-1.0)
                            TS("dve", sig[d][:], sig[d][:], nomlt[:, j, col:col + 1], omlt[:, j, col:col + 1], ALU.mult, ALU.add,
                               [("sig", d), "omlt", "nomlt"], [("sig", d)])
                            TT("dve", qd[d][:, tbs(tb)], qs[:], ebt[:], ALU.mult, ["qs", "ebt"], [("qd", d, tb)])
                            TT("dve", kdT[d][:, tbs(tb)], sig[d][:], enbt[:], ALU.mult, [("sig", d), "enbt"], [("kdT", d, tb)])
                            e3 = ebt[:].rearrange("p (c i) -> p c i", i=32)
                            pos = 31 if d == 0 else 0
                            CP("dve", Dor[d][:, tb * 16:(tb + 1) * 16].unsqueeze(2), e3[:, :, pos:pos + 1], ["ebt"], [("Dor", d)])
                    kd_tr(NB - 1)
                    for d in range(2):
                        CP("dve", Dca[d][:], Dor[d][:], [("Dor", d)], [("Dca", d)])
                        pos = 7 if d == 0 else 0
                        dv = Dca[d][:].rearrange("p (s c) -> p s c", c=8)[:, :, pos:pos + 1]
                        TS("dve", dv, dv, pcs[:, PC_CARRY:PC_CARRY + 1], None, ALU.mult, None, [("Dca", d), "pcs"], [("Dca", d)])
                    for i in range(NT):
                        dirs = ((0, i), (1, NT - 1 - i))
                        for ti_, tl_ in enumerate((i, NT - 1 - i)):
                            TT("pool", vblk[i % 2][:, ti_, :, :], vtok[:, tl_, :].unsqueeze(1).to_broadcast([128, 4, 128]),
                               bmb[:].unsqueeze(2).to_broadcast([128, 4, 128]), ALU.mult,
                               [("hvt", tl_), "bmb"], [("vblk", i % 2, ti_)])
                        for d, tile in dirs:
                            tsl = slice(tile * 128, (tile + 1) * 128)
                            pU = d * 2 + i % 2
                            MM(ps[pU][:], kdtok[d][:, tile, :], vblk[i % 2][:, d, :, :], True, True,
                               [("kdt", d, tile), ("vblk", i % 2, d)], [("ps", pU)])
                            xb = 4 + d * 2 + i % 2
                            MM(ps[xb][:, 0:128], kdT[d][:, tsl], qd[d][:, tsl], True, True,
                               [("kdT", d, tile // 4), ("qd", d, tile // 4)], [("ps", xb)])
                        for d, tile in dirs:
                            xb = 4 + d * 2 + i % 2
                            TT("dve", amask[d][i % 2][:], ps[xb][:, 0:128], hmk[:, d, :], ALU.mult, [("ps", xb), "hmk"],
                               [("am", d, i % 2)])
                        for d, tile in dirs:
                            xb = 4 + d * 2 + i % 2
                            MM(ps[xb][:, 128:256], vtok[:, tile, :], amask[d][i % 2][:], True, False,
                               [("hvt", tile), ("am", d, i % 2)], [("ps", xb)])
                        for cc in range(4):
                            for d, tile in dirs:
                                pU = d * 2 + i % 2
                                xb = 4 + d * 2 + i % 2
                                ok = ("ps", xb)
                                pO = ps[xb][:, 128:256]
                                c = cc if d == 0 else 3 - cc
                                cidx = tile * 4 + c
                                n = cidx if d == 0 else 63 - cidx
                                pc = cidx - 1 if d == 0 else cidx + 1
                                told = Tst[d][(n - 1) % 2]
                                tnew = Tst[d][n % 2]
                                sbf = Sbf[d][n % 2]
                                kold = ("T", d, (n - 1) % 2)
                                knew = ("T", d, n % 2)
                                ksb = ("Sbf", d, n % 2)
                                if n == 0:
                                    CP("act", sbf[:], told[:], [kold], [ksb])
                                else:
                                    ACT(sbf[:], told[:], AF.Copy, [kold, ("Dca", d)], [ksb], scale=Dca[d][:, pc:pc + 1])
                                MM(pO[:, c * 32:(c + 1) * 32], sbf[:], qd[d][:, tile * 128 + c * 32:tile * 128 + (c + 1) * 32],
                                   False, cc == 3, [ksb, ("qd", d, tile // 4)], [ok])
                                uap = ps[pU][:, c * 128:(c + 1) * 128]
                                if n == 0:
                                    TT("dve", tnew[:], told[:], uap, ALU.add, [kold, ("ps", pU)], [knew])
                                else:
                                    STT("dve", tnew[:], told[:], Dca[d][:, pc:pc + 1], uap, ALU.mult, ALU.add,
                                        [kold, ("Dca", d), ("ps", pU)], [knew])
                                seg_end = (cidx % 8 == 7) if d == 0 else (cidx % 8 == 0)
                                if seg_end:
                                    m = cidx // 8
                                    sfb = sfin[sf_i % 2]
                                    ksf = ("sfin", sf_i % 2)
                                    sf_i += 1
                                    ACT(sfb[:], tnew[:], AF.Copy, [knew, ("Dor", d)], [ksf], scale=Dor[d][:, cidx:cidx + 1])
                                    DMA("sp", ns_out[j, d, m, h], sfb[:], [ksf], [], is_output=True)
                        for d, tile in dirs:
                            tsl = slice(tile * 128, (tile + 1) * 128)
                            xb = 4 + d * 2 + i % 2
                            pO = ps[xb][:, 128:256]
                            if i <= 7:
                                CP("act", oacc[:, tsl], pO, [("ps", xb)], [("oacc", tile)])
                            else:
                                TT("dve", oacc[:, tsl], oacc[:, tsl], pO, ALU.add, [("ps", xb), ("oacc", tile)], [("oacc", tile)])
                    gcol = R_GN + j * 8 + h

                    def gbufs(tb):
                        if tb % 2 == 0:
                            return (ebt, "ebt", enbt, "enbt", qs, "qs", osq[:], "osq")
                        return (sig[0], ("sig", 0), sig[1], ("sig", 1), lf[0], ("lf", 0),
                                lf[1][:].bitcast(BF16)[:, 0:TB], ("lf", 1))

                    for pr in range(2):
                        tbl = (2 * pr, 2 * pr + 1)
                        for tb in tbl:
                            b_rs, k_rs, b_t1, k_t1, b_gs, k_gs, b_sq, k_sq = gbufs(tb)
                            ok4 = [("oacc", 4 * tb + q) for q in range(4)]
                            ACT(b_sq, oacc[:, tbs(tb)], AF.Square, ok4, [k_sq])
                        for tb in tbl:
                            b_rs, k_rs, b_t1, k_t1, b_gs, k_gs, b_sq, k_sq = gbufs(tb)
                            pA_ = (tb % 2) * 2
                            pG_ = (tb % 2) * 2 + 1
                            MM(ps[pA_][:], onese[:], b_sq, True, True, [k_sq, "onese"], [("ps", pA_)])
                            for kc in range(KC):
                                MM(ps[pG_][:], vA[:, kc, 384:512], hT[:, kc, tbs(tb)], kc == 0, kc == KC - 1, [kA, ("h", kc, tb)],
                                   [("ps", pG_)])
                        for tb in tbl:
                            b_rs, k_rs, b_t1, k_t1, b_gs, k_gs, b_sq, k_sq = gbufs(tb)
                            pA_ = (tb % 2) * 2
                            ACT(b_rs[:], ps[pA_][:], AF.Ln, [("ps", pA_)], [k_rs], bias=EPS)
                        for tb in tbl:
                            b_rs, k_rs, b_t1, k_t1, b_gs, k_gs, b_sq, k_sq = gbufs(tb)
                            ACT(b_rs[:], b_rs[:], AF.Exp, [k_rs], [k_rs], scale=-0.5)
                        for tb in tbl:
                            b_rs, k_rs, b_t1, k_t1, b_gs, k_gs, b_sq, k_sq = gbufs(tb)
                            pG_ = (tb % 2) * 2 + 1
                            ACT(b_gs[:], ps[pG_][:], AF.Sigmoid, [("ps", pG_)], [k_gs])
                        for tb in tbl:
                            b_rs, k_rs, b_t1, k_t1, b_gs, k_gs, b_sq, k_sq = gbufs(tb)
                            pG_ = (tb % 2) * 2 + 1
                            ok4 = [("oacc", 4 * tb + q) for q in range(4)]
                            STT("dve", b_t1[:], oacc[:, tbs(tb)], vecT[:, gcol:gcol + 1], b_rs[:], ALU.mult, ALU.mult,
                                ok4 + ["vecT", k_rs], [k_t1])
                            TT("dve", b_t1[:], b_t1[:], b_gs[:], ALU.mult, [k_t1, k_gs], [k_t1])
                            TT("dve", ogT[:, h % 2, tbs(tb)], b_t1[:], ps[pG_][:], ALU.mult, [k_t1, ("ps", pG_)], [("og", h % 2, tb)])
                    if h % 2 == 1:
                        slo, ko = ring_next()
                        vo = slo[:, 0:2048].rearrange("p (k f) -> p k f", k=2)
                        DMA("pool", vo, w_outa[j][(h - 1) * 128:(h + 1) * 128, :].rearrange("(k p) f -> p k f", p=128), (), [ko],
                            scoped=False)
                        ko_i = 0
                        for tb in range(NB):
                            for dm in range(KC):
                                pd = 4 + ko_i % 4
                                ko_i += 1
                                for hh in range(2):
                                    MM(ps[pd][:], vo[:, hh, dm * 128:(dm + 1) * 128], ogT[:, hh, tbs(tb)], hh == 0, hh == 1,
                                       [ko, ("og", hh, tb)], [("ps", pd)])
                                STT("dve", xT[:, dm, tbs(tb)], ps[pd][:], modall[:, l, 16 + dm:17 + dm], xT[:, dm, tbs(tb)],
                                    ALU.mult, ALU.add, [("ps", pd), mk, ("x", dm, tb)], [("x", dm, tb)])
            S.barrier()


        l0 = cfg.get("l0", 0)
        for l in range(l0, depth):
            if l == l0 or not cfg["ffn"]:
                adaln(l)
            if cfg["mixers"]:
                norm_mod(l, 0)
                if l % 2 == 0:
                    hgrn(l)
                else:
                    attn(l)
            if cfg["ffn"]:
                if l + 1 < depth:
                    ada_next[0] = adaln_gen(l + 1)
                ffn(l)
                while ada_next[0] is not None:
                    ada_step()
        final_out()
        stats = S.emit()
        if cfg.get("dbg"):
            print("sched stats", stats)
    return nc

def _consts():
    cm = np.zeros((5, 128, 128), np.float32)
    cm[0] = np.eye(128, dtype=np.float32)
    j = np.arange(128)[:, None]
    i = np.arange(128)[None, :]
    cm[1] = (i <= j).astype(np.float32)
    cm[2] = (j <= i).astype(np.float32)
    same = (j // 32) == (i // 32)
    cm[3] = (same & (j <= i)).astype(np.float32)
    cm[4] = (same & (j >= i)).astype(np.float32)
    return cm


def _rope_tables(sample):
    tab = np.zeros((NT, 128, 256), np.float32)
    if not sample:
        tab[:, :, 0:128] = 1.0
        return tab
    t = np.arange(T, dtype=np.float32)
    row = np.floor(t / 64.0).astype(np.float32)
    col = (t - row * 64.0).astype(np.float32)
    inv = (10000.0 ** (-np.arange(32, dtype=np.float32) / 32.0)).astype(np.float32)
    ar = row[:, None] * inv[None, :]
    ac = col[:, None] * inv[None, :]
    cr, sr, cc, sc_ = np.cos(ar), np.sin(ar), np.cos(ac), np.sin(ac)
    cos = np.concatenate([cr, cr, cc, cc], axis=1)
    sin = np.concatenate([-sr, sr, -sc_, sc_], axis=1)
    tab[:, :, 0:128] = cos.reshape(NT, 128, 128)
    tab[:, :, 128:256] = sin.reshape(NT, 128, 128)
    return tab.astype(np.float32)


def _pcs(sample):
    p = np.zeros((N_PCS,), np.float32)
    p[PC_CARRY] = 1.0 if sample else 0.0
    p[PC_CTXB] = 0.0 if sample else -30000.0
    for n in range(NT):
        for side in range(2):
            base = PC_MASK + (n * 2 + side) * 2
            if sample:
                p[base], p[base + 1] = 1.0, 0.0
            else:
                valid = (side == 1 and n % 2 == 0) or (side == 0 and n % 2 == 1)
                p[base], p[base + 1] = 0.0, (1.0 if valid else 0.0)
    return np.ascontiguousarray(np.broadcast_to(p[None, :], (128, N_PCS))).astype(np.float32)


def kernel(x_prompt, x_sample, cache_k, cache_v, state_hgrn, c, c_ctx, w_ada, b_ada, norm1, norm2, norm_final,
           w_gate_up, w_down, w_in_a, lower_bounds, gnorm_a, w_out_a, w_qkv_b, w_out_b, sink_b):
    f = lambda a: np.ascontiguousarray(np.asarray(a), dtype=np.float32)
    x_prompt, x_sample, cache_k, cache_v, state_hgrn = map(f, (x_prompt, x_sample, cache_k, cache_v, state_hgrn))
    c, c_ctx, b_ada, norm1, norm2, norm_final = map(f, (c, c_ctx, b_ada, norm1, norm2, norm_final))
    lower_bounds, gnorm_a, sink_b = map(f, (lower_bounds, gnorm_a, sink_b))
    weights = dict(w_ada=f(w_ada), w_gate_up=f(w_gate_up), w_down=f(w_down), w_in_a=f(w_in_a),
                   w_out_a=f(w_out_a), w_qkv_b=f(w_qkv_b), w_out_b=f(w_out_b))
    vecs = np.zeros((N_VROWS, 128), np.float32)
    vecs[R_BADA:R_BADA + 192] = b_ada.reshape(192, 128)
    vecs[R_N1:R_N1 + 32] = norm1.reshape(32, 128)
    vecs[R_N2:R_N2 + 32] = norm2.reshape(32, 128)
    vecs[R_NF:R_NF + 8] = norm_final.reshape(8, 128)
    vecs[R_GN:R_GN + 16] = gnorm_a.reshape(16, 128)
    vecs[R_LB:R_LB + 32] = lower_bounds.reshape(32, 128)
    cm = _consts()
    bm = (np.arange(128)[:, None] // 32 == np.arange(4)[None, :]).astype(np.float32)
    sinkb = np.ascontiguousarray(np.broadcast_to(sink_b.reshape(1, 16), (128, 16))).astype(np.float32)
    rope_s, rope_p = _rope_tables(True), _rope_tables(False)
    pcs_s, pcs_p = _pcs(True), _pcs(False)
    zck = np.zeros((2, 512, 256), np.float32)
    zst = np.zeros((2, 2, 8, 128, 128), np.float32)
    in_maps = []
    for core in range(8):
        sample = core < 4
        if sample:
            b = core
            m = dict(x=x_sample[b], cond=np.ascontiguousarray(c[b].reshape(8, 128).T),
                     ck=np.ascontiguousarray(cache_k[b].reshape(2, 512, 256)),
                     cv=np.ascontiguousarray(cache_v[b].reshape(2, 512, 256)),
                     st0=np.ascontiguousarray(state_hgrn[b]), rope=rope_s, pcs=pcs_s)
        else:
            s0 = (core - 4) * 8
            m = dict(x=np.ascontiguousarray(x_prompt[s0:s0 + 8].reshape(T, D)),
                     cond=np.ascontiguousarray(c_ctx.reshape(8, 128).T), ck=zck, cv=zck, st0=zst,
                     rope=rope_p, pcs=pcs_p)
        m.update(vecs=vecs, cmat=cm, sinkb=sinkb, bm=bm)
        m.update(weights)
        in_maps.append(m)
    nc = build_nc(CFG)
    res = run_bass_kernel_spmd(nc, in_maps, core_ids=list(range(8)))
    R = res.results
    y_prompt = np.zeros((32, 256, D), np.float32)
    y_sample = np.zeros((4, T, D), np.float32)
    nk = np.zeros((32, 2, 256, 2, 128), np.float32)
    nv = np.zeros((32, 2, 256, 2, 128), np.float32)
    nst = np.zeros((32, 2, 2, 8, 128, 128), np.float32)
    for core in range(8):
        r = R[core]
        if core < 4:
            y_sample[core] = r["y"]
        else:
            s0 = (core - 4) * 8
            y_prompt[s0:s0 + 8] = r["y"].reshape(8, 256, D)
            nk[s0:s0 + 8] = r["nk"].reshape(2, 8, 256, 2, 128).transpose(1, 0, 2, 3, 4)
            nv[s0:s0 + 8] = r["nv"].reshape(2, 8, 256, 2, 128).transpose(1, 0, 2, 3, 4)
            nst[s0:s0 + 8] = r["ns"].transpose(2, 0, 1, 3, 4, 5)
    return (y_prompt, y_sample, nk, nv, nst)
```
